# Optimizing a Trainium2 kernel written in Bass

```python
import jax, jax.numpy as jnp
from jax import lax
import numpy as np

D_MODEL = 1024
BATCH = 16
SEQ = 2048
DEPTH = 1

CHUNK = 128
MIX_WIDTH = D_MODEL
GMLP_WIDTH = MIX_WIDTH // 2
GMLP_GROUP_DIM = 128
N_GMLP_GROUPS = GMLP_WIDTH // GMLP_GROUP_DIM
RET_WIDTH = MIX_WIDTH - GMLP_WIDTH
RET_HEAD_DIM = 128
N_RET_HEADS = RET_WIDTH // RET_HEAD_DIM
ROPE_THETA = 10000.0
EPS = 1e-6
PROJ_SIZES = [GMLP_WIDTH, GMLP_WIDTH, GMLP_WIDTH, RET_WIDTH, RET_WIDTH, RET_WIDTH, RET_WIDTH]
PROJ_WIDTH = sum(PROJ_SIZES)
PROJ_SPLITS = [int(i) for i in np.cumsum(PROJ_SIZES)[:-1]]

kernel_name = "hybrid_gmlp_retention_adaln_block"


def rmsnorm(x, g):
    xf = x.astype(jnp.float32)
    y = xf * lax.rsqrt(jnp.mean(xf * xf, axis=-1, keepdims=True) + EPS)
    return (y * g.astype(jnp.float32)).astype(x.dtype)


def group_layernorm(x, g):
    xf = x.astype(jnp.float32)
    mu = jnp.mean(xf, axis=-1, keepdims=True)
    var = jnp.mean(jnp.square(xf - mu), axis=-1, keepdims=True)
    return ((xf - mu) * lax.rsqrt(var + EPS) * g.astype(jnp.float32)).astype(x.dtype)


def rotary(x, positions):
    half = x.shape[-1] // 2
    inv_freq = 1.0 / (ROPE_THETA ** (jnp.arange(half, dtype=jnp.float32) / half))
    ang = positions.astype(jnp.float32)[..., None] * inv_freq
    cos = jnp.cos(ang)[:, :, None, :].astype(x.dtype)
    sin = jnp.sin(ang)[:, :, None, :].astype(x.dtype)
    x1, x2 = x[..., :half], x[..., half:]
    return jnp.concatenate([x1 * cos - x2 * sin, x2 * cos + x1 * sin], axis=-1)


def gmlp_branch(u, v, gate, ln_g, w_s, b_s):
    B, S, _ = u.shape
    nc = S // CHUNK
    v = v.reshape(B, nc, CHUNK, N_GMLP_GROUPS, GMLP_GROUP_DIM)
    v = group_layernorm(v, ln_g.reshape(N_GMLP_GROUPS, GMLP_GROUP_DIM))
    causal = jnp.tril(jnp.ones((CHUNK, CHUNK), dtype=w_s.dtype))
    ws = w_s * causal[None]
    mixed = jnp.einsum('gts,bnsgd->bntgd', ws, v) + b_s.T[None, None, :, :, None]
    out = u.reshape(B, nc, CHUNK, N_GMLP_GROUPS, GMLP_GROUP_DIM) * mixed
    return out.reshape(B, S, GMLP_WIDTH) * jax.nn.silu(gate)


def retention_branch(q, k, v, gate, positions, gn_g):
    B, S, _ = q.shape
    nc = S // CHUNK
    H, Dh = N_RET_HEADS, RET_HEAD_DIM
    q = rotary(q.reshape(B, S, H, Dh), positions)
    k = rotary(k.reshape(B, S, H, Dh), positions) * (Dh ** -0.5)
    v = v.reshape(B, S, H, Dh)
    q = q.reshape(B, nc, CHUNK, H, Dh)
    k = k.reshape(B, nc, CHUNK, H, Dh)
    v = v.reshape(B, nc, CHUNK, H, Dh)

    log_gamma = jnp.log(1.0 - 2.0 ** (-5.0 - jnp.arange(H, dtype=jnp.float32)))
    idx = jnp.arange(CHUNK, dtype=jnp.float32)
    diff = idx[:, None] - idx[None, :]
    decay_mask = jnp.where(diff[None] >= 0,
                           jnp.exp(jnp.maximum(diff, 0.0)[None] * log_gamma[:, None, None]),
                           0.0)

    scores = jnp.einsum('bnqhd,bnkhd->bnhqk', q, k) * decay_mask[None, None]
    intra = jnp.einsum('bnhqk,bnkhe->bnqhe', scores, v)

    k_decay = jnp.exp((CHUNK - 1.0 - idx)[:, None] * log_gamma[None, :])
    kv_chunk = jnp.einsum('bnkhd,bnkhe,kh->bnhde', k, v, k_decay)
    chunk_decay = jnp.exp(CHUNK * log_gamma)[None, :, None, None]

    def step(state, kv):
        return state * chunk_decay + kv, state

    init = jnp.zeros_like(kv_chunk[:, 0])
    _, states = lax.scan(step, init, jnp.moveaxis(kv_chunk, 1, 0))
    states = jnp.moveaxis(states, 0, 1)

    q_decay = jnp.exp((idx + 1.0)[:, None] * log_gamma[None, :])
    cross = jnp.einsum('bnqhd,bnhde,qh->bnqhe', q, states, q_decay)

    o = (intra + cross).reshape(B, S, H, Dh)
    o = group_layernorm(o, gn_g.reshape(H, Dh)).astype(gate.dtype)
    return o.reshape(B, S, RET_WIDTH) * jax.nn.silu(gate)


def hybrid_layer(x, c, positions, w_ada, b_ada, g_pre, w_in, gmlp_ln_g, gmlp_ws, gmlp_bs,
                 ret_gn_g, w_out, g_post):
    mod = jnp.einsum('bd,de->be', jax.nn.silu(c), w_ada) + b_ada
    shift, scale, gate = jnp.split(mod, 3, axis=-1)
    h = rmsnorm(x, g_pre) * (1.0 + scale[:, None, :]) + shift[:, None, :]
    proj = jnp.einsum('bsd,de->bse', h, w_in)
    gu, gv, gg, rq, rk, rv, rg = jnp.split(proj, PROJ_SPLITS, axis=-1)
    y_gmlp = gmlp_branch(gu, gv, gg, gmlp_ln_g, gmlp_ws, gmlp_bs)
    y_ret = retention_branch(rq, rk, rv, rg, positions, ret_gn_g)
    y = jnp.concatenate([y_gmlp, y_ret], axis=-1)
    y = rmsnorm(jnp.einsum('bsm,md->bsd', y, w_out), g_post)
    return x + gate[:, None, :] * y


def setup_inputs(seed: int = 0) -> dict:
    key = jax.random.key(seed)
    ks = jax.random.split(key, 14)
    f32 = jnp.float32
    x = jax.random.normal(ks[0], (BATCH, SEQ, D_MODEL), f32)
    c = jax.random.normal(ks[1], (BATCH, D_MODEL), f32)
    positions = jnp.broadcast_to(jnp.arange(SEQ, dtype=jnp.int32)[None, :], (BATCH, SEQ))
    w_ada = jax.random.normal(ks[2], (DEPTH, D_MODEL, 3 * D_MODEL), f32) * (0.5 * D_MODEL ** -0.5)
    b_ada = jax.random.normal(ks[3], (DEPTH, 3 * D_MODEL), f32) * 0.02
    g_pre = 1.0 + 0.02 * jax.random.normal(ks[4], (DEPTH, D_MODEL), f32)
    w_in = jax.random.normal(ks[5], (DEPTH, D_MODEL, PROJ_WIDTH), f32) * (D_MODEL ** -0.5)
    gmlp_ln_g = 1.0 + 0.02 * jax.random.normal(ks[6], (DEPTH, GMLP_WIDTH), f32)
    gmlp_ws = jax.random.normal(ks[7], (DEPTH, N_GMLP_GROUPS, CHUNK, CHUNK), f32) * (CHUNK ** -0.5)
    gmlp_bs = 1.0 + 0.02 * jax.random.normal(ks[8], (DEPTH, N_GMLP_GROUPS, CHUNK), f32)
    ret_gn_g = 1.0 + 0.02 * jax.random.normal(ks[9], (DEPTH, RET_WIDTH), f32)
    w_out = jax.random.normal(ks[10], (DEPTH, MIX_WIDTH, D_MODEL), f32) * (MIX_WIDTH ** -0.5)
    g_post = 1.0 + 0.02 * jax.random.normal(ks[11], (DEPTH, D_MODEL), f32)
    return {"x": x, "c": c, "positions": positions, "w_ada": w_ada, "b_ada": b_ada,
            "g_pre": g_pre, "w_in": w_in, "gmlp_ln_g": gmlp_ln_g, "gmlp_ws": gmlp_ws,
            "gmlp_bs": gmlp_bs, "ret_gn_g": ret_gn_g, "w_out": w_out, "g_post": g_post}


def reference(x, c, positions, w_ada, b_ada, g_pre, w_in, gmlp_ln_g, gmlp_ws, gmlp_bs,
              ret_gn_g, w_out, g_post):
    for layer in range(DEPTH):
        x = hybrid_layer(x, c, positions, w_ada[layer], b_ada[layer], g_pre[layer], w_in[layer],
                         gmlp_ln_g[layer], gmlp_ws[layer], gmlp_bs[layer], ret_gn_g[layer],
                         w_out[layer], g_post[layer])
    return x
```

```python
import numpy as np
from contextlib import ExitStack
from collections import defaultdict
import concourse.bass as bass
import concourse.mybir as mybir
from concourse.bass_utils import run_bass_kernel_spmd

F32 = mybir.dt.float32
BF16 = mybir.dt.bfloat16
I32 = mybir.dt.int32
AF = mybir.ActivationFunctionType
ALU = mybir.AluOpType

NCORES = 8
BATCH, SEQ, D = 16, 2048, 1024
NB = BATCH // NCORES
NCH = SEQ // 128
G = NB * NCH
PW = 3584
O_GU, O_GV, O_GG, O_RQ, O_RK, O_RV, O_RG = 0, 512, 1024, 1536, 2048, 2560, 3072
EPS = 1e-6
MTC = 2
TM = MTC * 128
NMT = G // MTC
NX = 6
TWO_PI_SAFE = 6.283185
NEAR = 3


class Op:
    __slots__ = ("eng", "fn", "reads", "writes", "dma", "idx", "eidx", "deps", "signal", "tick", "dma_ord")


class Sched:
    def __init__(self):
        self.ops = []
        self.dma_count = defaultdict(int)

    def add(self, eng, fn, r=(), w=(), dma=None):
        op = Op()
        op.eng, op.fn, op.reads, op.writes, op.dma = eng, fn, tuple(r), tuple(w), dma
        op.idx = len(self.ops)
        op.deps = []
        op.signal = False
        op.tick = 0
        op.dma_ord = 0
        self.ops.append(op)
        return op

    def analyze(self):
        last_w = {}
        readers = {}
        ecount = defaultdict(int)
        dma_seen = defaultdict(int)
        for op in self.ops:
            op.eidx = ecount[op.eng]
            ecount[op.eng] += 1
            deps = {}
            for k in op.reads:
                if k in last_w:
                    deps[last_w[k]] = "raw"
            for k in op.writes:
                if k in last_w:
                    deps.setdefault(last_w[k], "waw")
                for rr in readers.get(k, ()):
                    deps.setdefault(rr, "war")
            for k in op.reads:
                readers.setdefault(k, []).append(op.idx)
            for k in op.writes:
                last_w[k] = op.idx
                readers[k] = []
            need = {}
            for pi, kind in deps.items():
                if pi == op.idx:
                    continue
                p = self.ops[pi]
                if p.dma is None and op.dma is None and p.eng == op.eng:
                    if op.eng == "pe":
                        continue
                    if (op.eidx - p.eidx) > NEAR:
                        continue
                if p.dma is not None:
                    key = ("dma", p.dma)
                    need[key] = max(need.get(key, 0), 16 * dma_seen[p.dma])
                else:
                    p.signal = True
                    key = ("eng", p.eng)
                    prev = need.get(key)
                    if prev is None or self.ops[prev].idx < p.idx:
                        need[key] = p.idx
            op.deps = need
            if op.dma is not None:
                dma_seen[op.dma] += 1
                op.dma_ord = dma_seen[op.dma]
        tick = defaultdict(int)
        for op in self.ops:
            if op.dma is None and op.signal:
                tick[op.eng] += 1
                op.tick = tick[op.eng]
        self.dma_total = dict(dma_seen)

    def emit_engine(self, eng_name, eng, eng_sems, dma_sems):
        waited = {}
        for op in self.ops:
            if op.eng != eng_name:
                continue
            for key, val in op.deps.items():
                if key[0] == "dma":
                    sem, v = dma_sems[key[1]], val
                else:
                    sem, v = eng_sems[key[1]], self.ops[val].tick
                if waited.get(key, 0) >= v:
                    continue
                eng.wait_ge(sem, v)
                waited[key] = v
            ins = op.fn(eng)
            if op.dma is not None:
                ins.then_inc(dma_sems[op.dma], 16)
            elif op.signal:
                ins.then_inc(eng_sems[op.eng], 1)


class PsPool:
    def __init__(self, banks):
        self.banks = banks
        self.i = 0

    def next(self):
        b = self.banks[self.i % len(self.banks)]
        self.i += 1
        return b


def build_program():
    nc = bass.Bass("TRN2", target_bir_lowering=False)
    dt_in = lambda name, shape, dt=F32: nc.dram_tensor(name, list(shape), dt, kind="ExternalInput").ap()
    x_d = dt_in("x", [G * 128, D])
    wada_d = dt_in("w_ada", [D, 3 * D])
    bada_d = dt_in("b_ada", [1, 3 * D])
    win_d = dt_in("w_in", [D, PW])
    wsT_d = dt_in("wsT", [128, 512])
    bs_d = dt_in("bs", [1, 512])
    lngr_d = dt_in("lng_r2", [2, 512])
    wout_d = dt_in("w_out", [D, D])
    gpost_d = dt_in("g_post_r", [128, D])
    ident_d = dt_in("ident", [128, 128])
    cmask_d = dt_in("cmask", [128, 128])
    gkinv_d = dt_in("gkinv", [128, 512])
    sel_d = dt_in("sel", [2, 256])
    cpk_d = dt_in("cpk", [128, 160])
    out_d = nc.dram_tensor("out", [G * 128, D], F32, kind="ExternalOutput").ap()

    gC = [float((1.0 - 2.0 ** (-5.0 - h)) ** 128) for h in range(4)]

    es = ExitStack()
    with es:
        sb = lambda name, shape, dt=F32: es.enter_context(nc.sbuf_tensor(name, list(shape), dt))
        w_in = sb("w_in_sb", [128, 8, PW], BF16)
        w_out = sb("w_out_sb", [128, 8, D], BF16)
        xbuf = sb("xbuf", [128, NX * D], F32)
        xbuf_bf = xbuf[:].bitcast(BF16)
        xsl = [xbuf[:, i * D:(i + 1) * D] for i in range(NX)]
        wa_stage = [xbuf_bf[:, j * 3072:(j + 1) * 3072] for j in range(4)]
        wa_xkeys = [[("x", 0), ("x", 1)], [("x", 1), ("x", 2)], [("x", 3), ("x", 4)], [("x", 4), ("x", 5)]]
        xsb = [sb(f"xsb{i}", [128, D], BF16) for i in range(2)]
        hT = [sb(f"hT{i}", [128, 8, TM], BF16) for i in range(2)]
        ug = [sb(f"ug{i}", [128, 4, TM], F32) for i in range(2)]
        srg = [sb(f"srg{i}", [128, 4, TM], F32) for i in range(2)]
        sgg = [sb(f"sgg{i}", [128, TM], F32) for i in range(2)]
        srgt = [sb(f"srgt{i}", [128, TM], F32) for i in range(2)]
        vln = [sb(f"vln{i}", [128, 512], BF16) for i in range(2)]
        qrot = [sb(f"qrot{i}", [128, 512], BF16) for i in range(2)]
        krot = [sb(f"krot{i}", [128, 512], BF16) for i in range(2)]
        At = [sb(f"At{i}", [128, 512], F32) for i in range(2)]
        Bt = [sb(f"Bt{i}", [128, 512], F32) for i in range(2)]
        ktok = [sb(f"ktok{i}", [128, 512], BF16) for i in range(2)]
        vbf = [sb(f"vbf{i}", [128, 512], BF16) for i in range(3)]
        qT = [sb(f"qT{i}", [128, 512], BF16) for i in range(2)]
        kT = [sb(f"kT{i}", [128, 512], BF16) for i in range(2)]
        msT = [sb(f"msT{i}", [128, 512], BF16) for i in range(2)]
        Sst = [sb(f"Sst{i}", [128, 512], F32) for i in range(NB)]
        Sbf = [sb(f"Sbf{i}", [128, 512], BF16) for i in range(2)]
        onb = [sb(f"onb{i}", [128, 512], BF16) for i in range(2)]
        yT = [sb(f"yT{i}", [128, 8, 128], BF16) for i in range(3)]
        scr12 = sb("scr12", [128, 3072], F32)
        tmpz = [scr12[:, 0:1024], scr12[:, 1024:2048]]
        junkA = scr12[:, 2048:3072]
        mod_sb = scr12[0:2, :]
        SCRK = [("tmp", 0), ("tmp", 1), "junkA"]
        cosT = sb("cosT", [128, NCH, 64], F32)
        sinT = sb("sinT", [128, NCH, 64], F32)
        gp = [sb(f"gp{i}", [128, D], F32) for i in range(NB)]
        gkinv = sb("gkinv_sb", [128, 512], F32)
        wsTb = sb("wsTb", [128, 512], BF16)
        identf = sb("identf", [128, 128], F32)
        identb = sb("identb", [128, 128], BF16)
        cmask = sb("cmask_sb", [128, 128], F32)
        cpk = sb("cpk_sb", [128, 160], F32)
        gpre, lng, gng, kdec, epsp = cpk[:, 0:8], cpk[:, 8:12], cpk[:, 12:16], cpk[:, 16:20], cpk[:, 20:24]
        invf, cTf = cpk[:, 24:88], cpk[:, 88:104]
        x0T = cpk[:, 104:120].rearrange("p (k b) -> p k b", b=NB)
        pos_i = cpk[:, 120:152].bitcast(I32)
        CPK_KEYS = ["gpre", "lng", "gng", "kdec", "epsp", "invf", "cTf", "x0T", "pos_i"]
        sel = sb("sel_sb", [2, 256], F32)
        cTb = sb("cTb", [128, 16], BF16)
        posf = sb("posf", [128, NCH], F32)
        gsT = sb("gsT", [128, 8, NB], F32)
        shT = sb("shT", [128, 8, NB], F32)
        neghalf = sb("neghalf", [128, 8], F32)
        onesb = sb("onesb", [2, 128], BF16)
        b2 = sb("b2", [2, 512], BF16)
        ginv2 = sb("ginv2", [2, 512], BF16)
        ssq = sb("ssq", [128, NX], F32)
        rstd = sb("rstd", [128, NX], F32)
        rtmp = sb("rtmp", [128, NX], F32)
        lnst = [sb(f"lnst{i}", [128, 4, 6], F32) for i in range(2)]
        lnmv = [sb(f"lnmv{i}", [128, 4, 2], F32) for i in range(2)]
        lnr = [sb(f"lnr{i}", [128, 4], F32) for i in range(2)]
        lnb = [sb(f"lnb{i}", [128, 4], F32) for i in range(2)]
        gnst = [sb(f"gnst{i}", [128, 4, 6], F32) for i in range(2)]
        gnmv = [sb(f"gnmv{i}", [128, 4, 2], F32) for i in range(2)]
        gnr = [sb(f"gnr{i}", [128, 4], F32) for i in range(2)]
        gnb = [sb(f"gnb{i}", [128, 4], F32) for i in range(2)]
        zss = [sb(f"zss{i}", [128, 2], F32) for i in range(2)]
        zr = [sb(f"zr{i}", [128, 2], F32) for i in range(2)]
        sqt = sb("sqt", [128, 8, NB], F32)
        sqs = sb("sqs", [128, NB], F32)
        ones_f = sb("ones_f", [128, 1], F32)
        Lq = sb("Lq", [128, 8, NB, 2], F32)
        r0t = sb("r0t", [1, NB], F32)
        rstd0 = sb("rstd0", [1, NB], F32)
        pos0f = sb("pos0f", [1, NB], F32)
        f0 = sb("f0", [1, NB * 64], F32)
        f0i = sb("f0i", [1, NB * 64], I32)
        f0b = sb("f0b", [1, NB * 64], F32)
        cos0 = sb("cos0", [1, NB * 64], F32)
        sin0 = sb("sin0", [1, NB * 64], F32)
        s4 = sb("s4", [1, 4], F32)
        s0p = [sb(f"s0p{i}", [1, 4], F32) for i in range(NB)]
        gpost = hT[0][:].rearrange("p k t -> p (k t)").bitcast(F32)
        GPOSTK = [("hT", 0, c) for c in range(MTC)]
        wsTf = hT[1][:].rearrange("p k t -> p (k t)").bitcast(F32)[:, 0:512]
        WSTFK = [("hT", 1, c) for c in range(MTC)]
        bs_f = At[0][0:1, :]
        hi_f = Bt[0][0:1, :]
        hi_b = At[1][:].bitcast(BF16)[0:1, 0:512]
        lo_b = Bt[1][:].bitcast(BF16)[0:1, 0:512]
        gi_f = Sst[1][0:2, :]
        ps = [es.enter_context(nc.psum_tensor(f"ps{i}", [128, 512], F32)) for i in range(8)]
        psb = [p[:].bitcast(BF16) for p in ps]
        big = PsPool([0, 1, 2, 3, 4, 5, 6, 7])
        trp = big
        mix = big
        PK = lambda b: ("ps", b)

        S = Sched()
        add = S.add

        def dma(q, out, in_, sem, r=(), w=()):
            add(q, lambda e, out=out, in_=in_: e.dma_start(out=out, in_=in_), r=r, w=w, dma=sem)

        dma("sp", cpk[:], cpk_d[:, :], "c0", w=CPK_KEYS)
        dma("sp", identf[:], ident_d[:, :], "c0", w=["identf"])
        dma("sp", sel[:], sel_d[:, :], "c0", w=["sel"])
        dma("sp", gpost, gpost_d[:, :], "c1", w=GPOSTK)
        dma("sp", wsTf, wsT_d[:, :], "c1", w=WSTFK)
        dma("sp", cmask[:], cmask_d[:, :], "c1", w=["cmask"])
        dma("sp", bs_f, bs_d[:, :], "c1", w=[("A", 0)])
        dma("sp", gkinv[:], gkinv_d[:, :], "c1", w=["gkinv"])

        def ada_load(k):
            j = k % 4
            if k < 8:
                dma("pool", wa_stage[j], wada_d[k * 128:(k + 1) * 128, :], f"wa{j}", w=wa_xkeys[j])
            else:
                dma("pool", wa_stage[j][0:1, :], bada_d[:, :], f"wa{j}", w=wa_xkeys[j])

        for k in range(4):
            ada_load(k)

        add("dve", lambda e: e.memset(neghalf[:], -0.5), w=["neghalf"])
        add("dve", lambda e: e.memset(onesb[:], 1.0), w=["onesb"])
        add("dve", lambda e: e.tensor_copy(out=identb[:], in_=identf[:]), r=["identf"], w=["identb"])
        add("act", lambda e: e.activation(out=cTb[:], in_=cTf[:], func=AF.Silu), r=["cTf"], w=["cTb"])

        cTb3 = cTb[:].rearrange("p (k b) -> p k b", b=NB)
        for k in range(9):
            j = k % 4

            def f(e, k=k, j=j):
                ins = None
                for n in range(6):
                    if k < 8:
                        lhsT = cTb3[:, k, :]
                        rhs = wa_stage[j][:, n * 512:(n + 1) * 512]
                    else:
                        lhsT = onesb[0:1, 0:2]
                        rhs = wa_stage[j][0:1, n * 512:(n + 1) * 512]
                    ins = e.matmul(ps[n][0:2, :], lhsT=lhsT, rhs=rhs, start=(k == 0), stop=(k == 8))
                return ins

            add("pe", f, r=wa_xkeys[j] + ["cTb", "onesb"], w=[PK(n) for n in range(6)])
            if k + 4 < 9:
                ada_load(k + 4)

        def wload(dst, src, c0, width, sem, key):
            for kk in range(2):
                dma("pool", dst[:, kk * 4:(kk + 1) * 4, c0:c0 + width],
                    src[kk * 512:(kk + 1) * 512, c0:c0 + width].rearrange("(k p) c -> p k c", p=128), sem, w=[key])

        early_buf = {0: (ug[0][:].rearrange("p j t -> p (j t)"), ("ug", 0)),
                     1: (ug[1][:].rearrange("p j t -> p (j t)"), ("ug", 1)),
                     2: (srg[0][:].rearrange("p j t -> p (j t)"), ("srg", 0))}

        def early_front(g, head=True, tail=True):
            xs_, xkey = early_buf[g]
            sl = g % NX
            if head:
                dma("sp", xs_, x_d[g * 128:(g + 1) * 128, :], f"xe{g}", w=[xkey])
                add("act", lambda e: e.activation(out=junkA, in_=xs_, func=AF.Square, accum_out=ssq[:, sl:sl + 1]),
                    r=[xkey], w=["junkA", ("ssq", sl)])
                add("dve", lambda e: e.tensor_scalar(out=rtmp[:, sl:sl + 1], in0=ssq[:, sl:sl + 1], scalar1=1.0 / D, scalar2=EPS,
                                                     op0=ALU.mult, op1=ALU.add), r=[("ssq", sl)], w=[("rtmp", sl)])
                add("pool", lambda e: e.tensor_tensor(out=rstd[:, sl:sl + 1], in0=rtmp[:, sl:sl + 1], in1=neghalf[:, 0:1], op=ALU.pow),
                    r=[("rtmp", sl), "neghalf"], w=[("rstd", sl)])
            if tail:
                xb = xsb[g % 2]
                add("act", lambda e: e.activation(out=xb[:], in_=xs_, func=AF.Copy, scale=rstd[:, sl:sl + 1]),
                    r=[xkey, ("rstd", sl)], w=[("xsb", g % 2)])

        early_front(0)
        early_front(1)
        early_front(2, tail=False)

        for (c0, nm) in ((O_GV, "gv"), (O_RQ, "rq"), (O_RK, "rk"), (O_RV, "rv"), (O_GG, "gg"), (O_GU, "gu"), (O_RG, "rg")):
            wload(w_in, win_d, c0, 512, "w_" + nm, ("win", c0))

        for n in range(6):
            add("dve" if n % 2 == 0 else "act",
                (lambda e, n=n: e.tensor_copy(out=mod_sb[:, n * 512:(n + 1) * 512], in_=ps[n][0:2, :])) if n % 2 == 0 else
                (lambda e, n=n: e.activation(out=mod_sb[:, n * 512:(n + 1) * 512], in_=ps[n][0:2, :], func=AF.Copy)),
                w=[PK(n)] + SCRK)
        def f(e):
            ins = None
            for j in range(16):
                ins = e.transpose(ps[6][:, j * 2:(j + 1) * 2], mod_sb[:, j * 128:(j + 1) * 128], identf[0:2, 0:2])
            return ins
        add("pe", f, r=SCRK + ["identf"], w=[PK(6)])
        ps6v = ps[6][:, 0:32].rearrange("p (j b) -> p j b", b=NB)
        add("dve", lambda e: e.tensor_copy(out=shT[:], in_=ps6v[:, 0:8, :]), w=[PK(6), "shT"])
        add("dve", lambda e: e.scalar_tensor_tensor(out=gsT[:], in0=ps6v[:, 8:16, :], scalar=1.0,
                                                    in1=gpre[:].unsqueeze(2).to_broadcast([128, 8, NB]),
                                                    op0=ALU.add, op1=ALU.mult), r=["gpre"], w=[PK(6), "gsT"])
        for b in range(NB):
            for half in range(2):
                bk = 7 if (b * 2 + half) % 2 == 0 else 6
                add("pe", lambda e, b=b, half=half, bk=bk: e.matmul(
                    ps[bk][:, :], lhsT=sel[0:2, b * 128:(b + 1) * 128],
                    rhs=mod_sb[:, 2048 + half * 512:2048 + (half + 1) * 512], start=True, stop=True),
                    r=SCRK + ["sel"], w=[PK(bk)])
                add("dve", lambda e, b=b, half=half, bk=bk: e.tensor_tensor(
                    out=gp[b][:, half * 512:(half + 1) * 512], in0=ps[bk][:, :],
                    in1=gpost[:, half * 512:(half + 1) * 512], op=ALU.mult),
                    r=GPOSTK, w=[PK(bk), ("gp", b)])
        add("dve", lambda e: e.tensor_tensor(out=wsTb[:].rearrange("p (g t) -> p g t", g=4),
                                             in0=wsTf.rearrange("p (g t) -> p g t", g=4),
                                             in1=cmask[:].unsqueeze(1).to_broadcast([128, 4, 128]), op=ALU.mult),
            r=WSTFK + ["cmask"], w=["wsTb"])
        add("dve", lambda e: e.tensor_copy(out=hi_b, in_=bs_f), r=[("A", 0)], w=[("A", 1)])
        add("dve", lambda e: e.tensor_copy(out=hi_f, in_=hi_b), r=[("A", 1)], w=[("B", 0)])
        add("dve", lambda e: e.tensor_tensor(out=lo_b, in0=bs_f, in1=hi_f, op=ALU.subtract), r=[("A", 0), ("B", 0)], w=[("B", 1)])
        dma("sp", b2[0:1, :], hi_b, "c2", r=[("A", 1)], w=["b2"])
        dma("sp", b2[1:2, :], lo_b, "c2", r=[("B", 1)], w=["b2"])

        dma("sp", gi_f, lngr_d[:, :], "c2", w=[("S", 1)])
        add("dve", lambda e: e.reciprocal(out=gi_f, in_=gi_f), r=[("S", 1)], w=[("S", 1)])
        add("dve", lambda e: e.tensor_copy(out=ginv2[:], in_=gi_f), r=[("S", 1)], w=["ginv2"])
        def precise_mm():
            ug0v = tmpz[0]
            ug1v = tmpz[1]
            sr0v = xsl[3]
            sr1v = tmpz[1]
            add("dve", lambda e: e.memset(ones_f[:], 1.0), w=["ones_f"])
            add("dve", lambda e: e.tensor_tensor(out=sqt[:], in0=x0T[:], in1=x0T[:], op=ALU.mult), r=["x0T"], w=["sqt"])
            add("dve", lambda e: e.reduce_sum(out=sqs[:], in_=sqt[:].rearrange("p k b -> p b k"), axis=mybir.AxisListType.X),
                r=["sqt"], w=["sqs"])
            add("pe", lambda e: e.matmul(ps[7][0:1, 0:NB], lhsT=ones_f[:, 0:1], rhs=sqs[:, :], start=True, stop=True),
                r=["ones_f", "sqs"], w=[PK(7)])
            add("dve", lambda e: e.tensor_scalar(out=r0t[:], in0=ps[7][0:1, 0:NB], scalar1=1.0 / D, scalar2=EPS, op0=ALU.mult, op1=ALU.add),
                w=[PK(7), "r0t"])
            add("pool", lambda e: e.tensor_tensor(out=rstd0[:], in0=r0t[:], in1=neghalf[0:1, 0:NB], op=ALU.pow), r=["r0t", "neghalf"], w=["rstd0"])
            add("dve", lambda e: e.tensor_tensor(out=Lq[:, :, :, 0], in0=x0T[:], in1=gsT[:], op=ALU.mult), r=["x0T", "gsT"], w=["Lq0"])
            add("dve", lambda e: e.tensor_copy(out=Lq[:, :, :, 1], in_=shT[:]), r=["shT"], w=["Lq1"])
            Lq2 = Lq[:].rearrange("p k b t -> p k (b t)")
            for k in range(8):
                slot = 3 + k % 3
                dma("sp", xsl[slot], win_d[k * 128:(k + 1) * 128, O_RQ:O_RQ + 1024], f"wq{k % 3}", w=[("x", slot)])

                def f(e, k=k, slot=slot):
                    ins = None
                    for half in range(2):
                        ins = e.matmul(ps[5 + half][0:2 * NB, :], lhsT=Lq2[:, k, :], rhs=xsl[slot][:, half * 512:(half + 1) * 512],
                                       start=(k == 0), stop=(k == 7))
                    return ins
                add("pe", f, r=[("x", slot), "Lq0", "Lq1"], w=[PK(5), PK(6)])
            add("dve", lambda e: e.tensor_copy(out=ug0v[0:2 * NB, 0:512], in_=ps[5][0:2 * NB, :]), w=[PK(5), ("tmp", 0)])
            add("act", lambda e: e.activation(out=ug0v[0:2 * NB, 512:1024], in_=ps[6][0:2 * NB, :], func=AF.Copy), w=[PK(6), ("tmp", 0)])

        def precise_el():
            ug0v = tmpz[0]
            ug1v = tmpz[1]
            sr0v = xsl[3]
            sr1v = tmpz[1]
            dma("sp", ug1v[0:1, :], ug0v[1:2, :], "c3", r=[("tmp", 0)], w=[("tmp", 1)])
            dma("sp", sr0v[0:1, :], ug0v[2:3, :], "c3", r=[("tmp", 0)], w=[("x", 3)])
            add("dve", lambda e: e.tensor_copy(out=pos0f[:], in_=pos_i[0:1, ::NCH]), r=["pos_i"], w=["pos0f"])
            f0_3 = f0[:].rearrange("p (b d) -> p b d", b=NB)
            add("dve", lambda e: e.tensor_tensor(out=f0_3, in0=pos0f[:].unsqueeze(2).to_broadcast([1, NB, 64]),
                                                 in1=invf[0:1, :].unsqueeze(1).to_broadcast([1, NB, 64]), op=ALU.mult),
                r=["pos0f", "invf"], w=["f0"])
            add("dve", lambda e: e.tensor_copy(out=f0i[:], in_=f0[:]), r=["f0"], w=["f0i"])
            add("dve", lambda e: e.tensor_copy(out=f0b[:], in_=f0i[:]), r=["f0i"], w=["f0b"])
            add("dve", lambda e: e.tensor_tensor(out=f0[:], in0=f0[:], in1=f0b[:], op=ALU.subtract), r=["f0", "f0b"], w=["f0"])
            add("act", lambda e: e.activation(out=sin0[:], in_=f0[:], func=AF.Sin, scale=TWO_PI_SAFE), r=["f0"], w=["sin0"])
            add("dve", lambda e: e.tensor_scalar(out=f0b[:], in0=f0[:], scalar1=0.25, scalar2=None, op0=ALU.add), r=["f0"], w=["f0b"])
            add("dve", lambda e: e.tensor_copy(out=f0i[:], in_=f0b[:]), r=["f0b"], w=["f0i"])
            add("dve", lambda e: e.tensor_copy(out=f0[:], in_=f0i[:]), r=["f0i", "sin0"], w=["f0"])
            add("dve", lambda e: e.tensor_tensor(out=f0b[:], in0=f0b[:], in1=f0[:], op=ALU.subtract), r=["f0", "f0b"], w=["f0b"])
            add("act", lambda e: e.activation(out=cos0[:], in_=f0b[:], func=AF.Sin, scale=TWO_PI_SAFE), r=["f0b"], w=["cos0"])
            for b in range(NB):
                if b == 1:
                    dma("sp", sr1v[0:1, :], ug0v[3:4, :], "c3", r=[("tmp", 0)], w=[("tmp", 1)])
                U = (ug0v if b == 0 else sr0v)[0:1, :]
                Wb = (ug1v if b == 0 else sr1v)[0:1, :]
                ku = ("tmp", 0) if b == 0 else ("x", 3)
                kw = ("tmp", 1) if b == 0 else ("tmp", 1)
                Aq = xsl[4][0:1, :]
                Bq = xsl[5][0:1, :]
                U4 = U.rearrange("p (h two d) -> p h two d", h=8, two=2)
                A4 = Aq.rearrange("p (h two d) -> p h two d", h=8, two=2)
                B4 = Bq.rearrange("p (h two d) -> p h two d", h=8, two=2)
                cb = cos0[0:1, b * 64:(b + 1) * 64].unsqueeze(1).unsqueeze(1).to_broadcast([1, 8, 2, 64])
                sbc = sin0[0:1, b * 64:(b + 1) * 64].unsqueeze(1).unsqueeze(1).to_broadcast([1, 8, 2, 64])
                add("dve", lambda e, U=U, Wb=Wb, b=b: e.scalar_tensor_tensor(out=U, in0=U, scalar=rstd0[0:1, b:b + 1], in1=Wb,
                                                                             op0=ALU.mult, op1=ALU.add),
                    r=[ku, kw, "rstd0"], w=[ku])
                add("dve", lambda e, U4=U4, A4=A4, cb=cb: e.tensor_tensor(out=A4, in0=U4, in1=cb, op=ALU.mult), r=[ku, "cos0"], w=[("x", 4)])
                add("dve", lambda e, U4=U4, B4=B4, sbc=sbc: e.tensor_tensor(out=B4, in0=U4[:, :, ::-1, :], in1=sbc, op=ALU.mult),
                    r=[ku, "sin0"], w=[("x", 5)])
                add("pool", lambda e, U4=U4, A4=A4, B4=B4: e.tensor_tensor(out=U4[:, :, 0, :], in0=A4[:, :, 0, :], in1=B4[:, :, 0, :], op=ALU.subtract),
                    r=[("x", 4), ("x", 5)], w=[ku])
                add("pool", lambda e, U4=U4, A4=A4, B4=B4: e.tensor_tensor(out=U4[:, :, 1, :], in0=A4[:, :, 1, :], in1=B4[:, :, 1, :], op=ALU.add),
                    r=[("x", 4), ("x", 5)], w=[ku])
                add("dve", lambda e, U=U, Aq=Aq: e.tensor_tensor(out=Aq[:, 0:512], in0=U[:, 0:512], in1=U[:, 512:1024], op=ALU.mult),
                    r=[ku], w=[("x", 4)])
                add("dve", lambda e, Aq=Aq: e.reduce_sum(out=s4[:], in_=Aq[:, 0:512].rearrange("p (h d) -> p h d", h=4), axis=mybir.AxisListType.X),
                    r=[("x", 4)], w=["s4"])
                add("dve", lambda e, b=b: e.tensor_tensor(out=s0p[b][:], in0=s4[:], in1=gkinv[0:1, ::128], op=ALU.mult),
                    r=["s4", "gkinv"], w=[("s0p", b)])

        T1, T2, T3 = tmpz[0], tmpz[1], junkA
        T2i = T2.bitcast(I32)
        cos2 = cosT[:].rearrange("p n d -> p (n d)")
        sin2 = sinT[:].rearrange("p n d -> p (n d)")
        T1_3 = T1.rearrange("p (n d) -> p n d", d=64)

        def tables(b):
            add("dve", lambda e: e.tensor_copy(out=posf[:], in_=pos_i[:, b * NCH:(b + 1) * NCH]), r=["pos_i"], w=["posf"])
            add("dve", lambda e: e.tensor_tensor(out=T1_3, in0=posf[:].unsqueeze(2).to_broadcast([128, NCH, 64]),
                                                 in1=invf[:].unsqueeze(1).to_broadcast([128, NCH, 64]), op=ALU.mult),
                r=["posf", "invf"], w=[("tmp", 0)])
            add("dve", lambda e: e.tensor_copy(out=T2i, in_=T1), r=[("tmp", 0)], w=[("tmp", 1)])
            add("dve", lambda e: e.tensor_copy(out=T3, in_=T2i), r=[("tmp", 1)], w=["junkA"])
            add("dve", lambda e: e.tensor_tensor(out=T1, in0=T1, in1=T3, op=ALU.subtract), r=[("tmp", 0), "junkA"], w=[("tmp", 0)])
            add("act", lambda e: e.activation(out=sin2, in_=T1, func=AF.Sin, scale=TWO_PI_SAFE), r=[("tmp", 0)], w=["sinT"])
            add("dve", lambda e: e.tensor_scalar(out=T3, in0=T1, scalar1=0.25, scalar2=None, op0=ALU.add), r=[("tmp", 0)], w=["junkA"])
            add("dve", lambda e: e.tensor_copy(out=T2i, in_=T3), r=["junkA"], w=[("tmp", 1)])
            add("dve", lambda e: e.tensor_copy(out=T1, in_=T2i), r=[("tmp", 1)], w=[("tmp", 0)])
            add("dve", lambda e: e.tensor_tensor(out=T3, in0=T3, in1=T1, op=ALU.subtract), r=[("tmp", 0), "junkA"], w=["junkA"])
            add("act", lambda e: e.activation(out=cos2, in_=T3, func=AF.Sin, scale=TWO_PI_SAFE), r=["junkA"], w=["cosT"])

        tables(0)

        def front0(g):
            xl = g % 3
            dma("sp", xsl[xl], x_d[g * 128:(g + 1) * 128, :], f"x{xl}", w=[("x", xl)])

        def reload(g):
            xl = 3 + g % 3
            dma("sp", xsl[xl], x_d[g * 128:(g + 1) * 128, :], f"x{xl}", w=[("x", xl)])

        def front1(g):
            b, m, c, sl = g // NCH, g // MTC, g % MTC, g % NX
            xl = g % 3
            xs_ = xsl[xl]
            add("act", lambda e: e.activation(out=junkA, in_=xs_, func=AF.Square, accum_out=ssq[:, sl:sl + 1]),
                r=[("x", xl)], w=["junkA", ("ssq", sl)])
            add("dve", lambda e: e.tensor_scalar(out=rtmp[:, sl:sl + 1], in0=ssq[:, sl:sl + 1], scalar1=1.0 / D, scalar2=EPS,
                                                 op0=ALU.mult, op1=ALU.add), r=[("ssq", sl)], w=[("rtmp", sl)])
            add("pool", lambda e: e.tensor_tensor(out=rstd[:, sl:sl + 1], in0=rtmp[:, sl:sl + 1], in1=neghalf[:, 0:1], op=ALU.pow),
                r=[("rtmp", sl), "neghalf"], w=[("rstd", sl)])
            xb = xsb[g % 2]
            add("act", lambda e: e.activation(out=xb[:], in_=xs_, func=AF.Copy, scale=rstd[:, sl:sl + 1]),
                r=[("x", xl), ("rstd", sl)], w=[("xsb", g % 2)])

        def front2(g):
            b, m, c, sl = g // NCH, g // MTC, g % MTC, g % NX
            xb = xsb[g % 2]
            bk = trp.next()

            def f(e):
                ins = None
                for k in range(8):
                    ins = e.transpose(psb[bk][:, k * 128:(k + 1) * 128], xb[:, k * 128:(k + 1) * 128], identb[:])
                return ins
            add("pe", f, r=[("xsb", g % 2), "identb"], w=[PK(bk)])
            hdst = hT[m % 2]

            def fa(e):
                ins = None
                for k in range(0, 4):
                    ins = e.activation(out=hdst[:, k, c * 128:(c + 1) * 128], in_=psb[bk][:, k * 128:(k + 1) * 128],
                                       func=AF.Identity, scale=gsT[:, k, b:b + 1], bias=shT[:, k, b:b + 1])
                return ins

            def fd(e):
                ins = None
                for k in range(4, 8):
                    ins = e.tensor_scalar(out=hdst[:, k, c * 128:(c + 1) * 128], in0=psb[bk][:, k * 128:(k + 1) * 128],
                                          scalar1=gsT[:, k, b:b + 1], scalar2=shT[:, k, b:b + 1], op0=ALU.mult, op1=ALU.add)
                return ins
            add("act", fa, r=["gsT", "shT"], w=[PK(bk), ("hT", m % 2, c)])
            add("dve", fd, r=["gsT", "shT"], w=[PK(bk), ("hT", m % 2, c)])

        def projB(m):
            hsrc = hT[m % 2]
            hkeys = [("hT", m % 2, c) for c in range(MTC)]

            def proj(col0, wkey):
                bk = big.next()

                def f(e):
                    ins = None
                    for k in range(8):
                        ins = e.matmul(ps[bk][:, 0:TM], lhsT=w_in[:, k, col0:col0 + 128], rhs=hsrc[:, k, :],
                                       start=(k == 0), stop=(k == 7))
                    return ins
                add("pe", f, r=hkeys + [wkey], w=[PK(bk)])
                return bk

            for j in range(4):
                sg = sgg[j % 2]
                bk = proj(O_GG + j * 128, ("win", O_GG))
                add("act", lambda e, bk=bk, sg=sg: e.activation(out=sg[:], in_=ps[bk][:, 0:TM], func=AF.Silu),
                    w=[PK(bk), ("sgg", j % 2)])
                bk = proj(O_GU + j * 128, ("win", O_GU))
                add("dve", lambda e, bk=bk, sg=sg, j=j: e.scalar_tensor_tensor(
                    out=ug[m % 2][:, j, :], in0=ps[bk][:, 0:TM], scalar=lng[:, j:j + 1], in1=sg[:],
                    op0=ALU.mult, op1=ALU.mult), r=[("sgg", j % 2), "lng"], w=[PK(bk), ("ug", m % 2)])
            for j in range(4):
                st = srgt[j % 2]
                bk = proj(O_RG + j * 128, ("win", O_RG))
                add("act", lambda e, bk=bk, st=st: e.activation(out=st[:], in_=ps[bk][:, 0:TM], func=AF.Silu),
                    w=[PK(bk), ("srgt", j % 2)])
                add("act", lambda e, st=st, j=j: e.activation(out=srg[m % 2][:, j, :], in_=st[:], func=AF.Copy, scale=gng[:, j:j + 1]),
                    r=[("srgt", j % 2), "gng"], w=[("srg", m % 2)])

        def rotary(bk, n, dst, par, kname):
            A, Bm = At[par], Bt[par]
            p4 = ps[bk][:, :].rearrange("p (h two d) -> p h two d", h=4, two=2)
            A4 = A[:].rearrange("p (h two d) -> p h two d", h=4, two=2)
            B4 = Bm[:].rearrange("p (h two d) -> p h two d", h=4, two=2)
            d4 = dst[:].rearrange("p (h two d) -> p h two d", h=4, two=2)
            cb = cosT[:, n, :].unsqueeze(1).unsqueeze(1).to_broadcast([128, 4, 2, 64])
            sbc = sinT[:, n, :].unsqueeze(1).unsqueeze(1).to_broadcast([128, 4, 2, 64])
            add("dve", lambda e: e.tensor_tensor(out=A4, in0=p4, in1=cb, op=ALU.mult), r=["cosT"], w=[PK(bk), ("A", par)])
            add("dve", lambda e: e.tensor_tensor(out=B4, in0=p4[:, :, ::-1, :], in1=sbc, op=ALU.mult), r=["sinT"], w=[PK(bk), ("B", par)])
            add("pool", lambda e: e.tensor_tensor(out=d4[:, :, 0, :], in0=A4[:, :, 0, :], in1=B4[:, :, 0, :], op=ALU.subtract),
                r=[("A", par), ("B", par)], w=[kname])
            add("pool", lambda e: e.tensor_tensor(out=d4[:, :, 1, :], in0=A4[:, :, 1, :], in1=B4[:, :, 1, :], op=ALU.add),
                r=[("A", par), ("B", par)], w=[kname])

        def projA(g):
            b, n, m, c = g // NCH, g % NCH, g // MTC, g % MTC
            hsrc = hT[m % 2]
            hkey = ("hT", m % 2, c)

            def proj(col0):
                bk = big.next()

                def f(e):
                    ins = None
                    for k in range(8):
                        ins = e.matmul(ps[bk][:, :], lhsT=hsrc[:, k, c * 128:(c + 1) * 128], rhs=w_in[:, k, col0:col0 + 512],
                                       start=(k == 0), stop=(k == 7))
                    return ins
                add("pe", f, r=[hkey, ("win", col0)], w=[PK(bk)])
                return bk

            par = g % 2
            bk = proj(O_GV)
            st, mv, lr, lb = lnst[par], lnmv[par], lnr[par], lnb[par]

            def fs(e, bk=bk):
                ins = None
                for q in range(4):
                    ins = e.bn_stats(out=st[:, q, :], in_=ps[bk][:, q * 128:(q + 1) * 128])
                return ins
            add("dve", fs, w=[PK(bk), ("lnst", par)])

            def fg(e):
                ins = None
                for q in range(4):
                    ins = e.bn_aggr(out=mv[:, q, :], in_=st[:, q, :])
                return ins
            add("dve", fg, r=[("lnst", par)], w=[("lnmv", par)])
            add("dve", lambda e: e.tensor_scalar(out=lr[:], in0=mv[:, :, 1], scalar1=EPS, scalar2=None, op0=ALU.add),
                r=[("lnmv", par)], w=[("lnr", par)])
            add("pool", lambda e: e.tensor_tensor(out=lr[:], in0=lr[:], in1=neghalf[:, 0:4], op=ALU.pow), r=[("lnr", par)], w=[("lnr", par)])
            def ln_tail(bk=bk):
                add("dve", lambda e: e.scalar_tensor_tensor(out=lb[:], in0=mv[:, :, 0], scalar=-1.0, in1=lr[:], op0=ALU.mult, op1=ALU.mult),
                    r=[("lnmv", par), ("lnr", par)], w=[("lnb", par)])

                def fn_(e, bk=bk):
                    ins = None
                    for q in range(4):
                        ins = e.activation(out=vln[par][:, q * 128:(q + 1) * 128], in_=ps[bk][:, q * 128:(q + 1) * 128],
                                           func=AF.Identity, scale=lr[:, q:q + 1], bias=lb[:, q:q + 1])
                    return ins
                add("act", fn_, r=[("lnr", par), ("lnb", par)], w=[PK(bk), ("vln", par)])
            bk = proj(O_RQ)
            rotary(bk, n, qrot[par], 0, ("qrot", par))
            ln_tail()
            bk = proj(O_RK)
            rotary(bk, n, krot[par], 1, ("krot", par))
            add("pool", lambda e: e.tensor_tensor(out=ktok[par][:].rearrange("p (h d) -> p h d", h=4),
                                                  in0=krot[par][:].rearrange("p (h d) -> p h d", h=4),
                                                  in1=kdec[:].unsqueeze(2).to_broadcast([128, 4, 128]), op=ALU.mult),
                r=[("krot", par), "kdec"], w=[("ktok", par)])
            bk = proj(O_RV)
            add("act", lambda e, bk=bk: e.activation(out=vbf[g % 3][:], in_=ps[bk][:, :], func=AF.Copy), w=[PK(bk), ("vbf", g % 3)])

        def mixA1(g):
            b, n, m, c = g // NCH, g % NCH, g // MTC, g % MTC
            par = g % 2
            bk = trp.next()

            def f(e, bk=bk):
                ins = None
                for h in range(4):
                    ins = e.transpose(psb[bk][:, h * 128:(h + 1) * 128], qrot[par][:, h * 128:(h + 1) * 128], identb[:])
                for h in range(4):
                    ins = e.transpose(psb[bk][:, 512 + h * 128:512 + (h + 1) * 128], krot[par][:, h * 128:(h + 1) * 128], identb[:])
                return ins
            add("pe", f, r=[("qrot", par), ("krot", par), "identb"], w=[PK(bk)])
            add("act", lambda e, bk=bk: e.activation(out=qT[par][:], in_=psb[bk][:, 0:512], func=AF.Copy), w=[PK(bk), ("qT", par)])
            add("dve", lambda e, bk=bk: e.tensor_tensor(out=kT[par][:], in0=psb[bk][:, 512:1024], in1=gkinv[:], op=ALU.mult),
                r=["gkinv"], w=[PK(bk), ("kT", par)])

        def mixA2(g):
            b, n, m, c = g // NCH, g % NCH, g // MTC, g % MTC
            par = g % 2
            bk = mix.next()

            def f(e, bk=bk):
                ins = None
                for q in range(4):
                    ins = e.matmul(ps[bk][:, q * 128:(q + 1) * 128], lhsT=ginv2[:, q * 128:(q + 1) * 128],
                                   rhs=b2[:, q * 128:(q + 1) * 128], start=(q == 0), stop=False)
                for q in range(4):
                    ins = e.matmul(ps[bk][:, q * 128:(q + 1) * 128], lhsT=vln[par][:, q * 128:(q + 1) * 128],
                                   rhs=wsTb[:, q * 128:(q + 1) * 128], start=False, stop=(q == 3))
                return ins
            add("pe", f, r=[("vln", par), "wsTb", "b2", "ginv2"], w=[PK(bk)])
            add("dve", lambda e, bk=bk: e.tensor_tensor(out=yT[g % 3][:, 0:4, :], in0=ps[bk][:, :].rearrange("p (q t) -> p q t", q=4),
                                                 in1=ug[m % 2][:, :, c * 128:(c + 1) * 128], op=ALU.mult),
                r=[("ug", m % 2)], w=[PK(bk), ("yTg", g % 3)])
            bk = mix.next()

            def f(e, bk=bk):
                ins = None
                for h in range(4):
                    ins = e.matmul(ps[bk][:, h * 128:(h + 1) * 128], lhsT=kT[par][:, h * 128:(h + 1) * 128],
                                   rhs=qT[par][:, h * 128:(h + 1) * 128], start=True, stop=True)
                return ins
            add("pe", f, r=[("qT", par), ("kT", par)], w=[PK(bk)])
            add("dve", lambda e, bk=bk: e.tensor_tensor(out=msT[par][:].rearrange("p (h q) -> p h q", h=4),
                                                 in0=ps[bk][:, :].rearrange("p (h q) -> p h q", h=4),
                                                 in1=cmask[:].unsqueeze(1).to_broadcast([128, 4, 128]), op=ALU.mult),
                r=["cmask"], w=[PK(bk), ("msT", par)])
            if n == 0:
                add("dve", lambda e: e.tensor_copy(out=msT[par][0:1, ::128], in_=s0p[b][:]), r=[("s0p", b)], w=[("msT", par)])
            bk = mix.next()

            def f(e, bk=bk):
                ins = None
                for h in range(4):
                    ins = e.matmul(ps[bk][:, h * 128:(h + 1) * 128], lhsT=ktok[par][:, h * 128:(h + 1) * 128],
                                   rhs=vbf[g % 3][:, h * 128:(h + 1) * 128], start=True, stop=True)
                return ins
            add("pe", f, r=[("ktok", par), ("vbf", g % 3)], w=[PK(bk)])
            if n == 0:
                add("dve", lambda e, bk=bk: e.tensor_copy(out=Sst[b][:], in_=ps[bk][:, :]), w=[PK(bk), ("S", b)])
            else:
                def f(e, bk=bk):
                    ins = None
                    for h in range(4):
                        ins = e.scalar_tensor_tensor(out=Sst[b][:, h * 128:(h + 1) * 128], in0=Sst[b][:, h * 128:(h + 1) * 128],
                                                     scalar=gC[h], in1=ps[bk][:, h * 128:(h + 1) * 128], op0=ALU.mult, op1=ALU.add)
                    return ins
                add("dve", f, r=[("S", b)], w=[PK(bk), ("S", b)])
            add("act", lambda e: e.activation(out=Sbf[(g + 1) % 2][:], in_=Sst[b][:], func=AF.Copy),
                r=[("S", b)], w=[("Sbf", (g + 1) % 2)])

        def mixB1(g):
            b, n, m, c = g // NCH, g % NCH, g // MTC, g % MTC
            par = g % 2
            bk = mix.next()

            def f(e):
                ins = None
                for h in range(4):
                    hs = slice(h * 128, (h + 1) * 128)
                    ins = e.matmul(ps[bk][:, hs], lhsT=msT[par][:, hs], rhs=vbf[g % 3][:, hs], start=True, stop=(n == 0))
                    if n > 0:
                        ins = e.matmul(ps[bk][:, hs], lhsT=qT[par][:, hs], rhs=Sbf[g % 2][:, hs], start=False, stop=True)
                return ins
            rk = [("msT", par), ("vbf", g % 3), ("qT", par)] + ([("Sbf", g % 2)] if n > 0 else [])
            add("pe", f, r=rk, w=[PK(bk)])
            st, mv, gr, gb = gnst[par], gnmv[par], gnr[par], gnb[par]

            def fs(e):
                ins = None
                for q in range(4):
                    ins = e.bn_stats(out=st[:, q, :], in_=ps[bk][:, q * 128:(q + 1) * 128])
                return ins
            add("dve", fs, w=[PK(bk), ("gnst", par)])

            def fg(e):
                ins = None
                for q in range(4):
                    ins = e.bn_aggr(out=mv[:, q, :], in_=st[:, q, :])
                return ins
            add("dve", fg, r=[("gnst", par)], w=[("gnmv", par)])
            add("dve", lambda e: e.tensor_tensor(out=gr[:], in0=mv[:, :, 1], in1=epsp[:], op=ALU.add), r=[("gnmv", par), "epsp"], w=[("gnr", par)])
            add("pool", lambda e: e.tensor_tensor(out=gr[:], in0=gr[:], in1=neghalf[:, 0:4], op=ALU.pow), r=[("gnr", par)], w=[("gnr", par)])
            def gn_tail():
                add("dve", lambda e: e.scalar_tensor_tensor(out=gb[:], in0=mv[:, :, 0], scalar=-1.0, in1=gr[:], op0=ALU.mult, op1=ALU.mult),
                    r=[("gnmv", par), ("gnr", par)], w=[("gnb", par)])

                def fn_(e):
                    ins = None
                    for q in range(4):
                        ins = e.activation(out=onb[par][:, q * 128:(q + 1) * 128], in_=ps[bk][:, q * 128:(q + 1) * 128],
                                           func=AF.Identity, scale=gr[:, q:q + 1], bias=gb[:, q:q + 1])
                    return ins
                add("act", fn_, r=[("gnr", par), ("gnb", par)], w=[PK(bk), ("onb", par)])
            return gn_tail

        def mixB2(g):
            b, n, m, c = g // NCH, g % NCH, g // MTC, g % MTC
            par = g % 2
            bk2 = trp.next()

            def f(e):
                ins = None
                for h in range(4):
                    ins = e.transpose(psb[bk2][:, h * 128:(h + 1) * 128], onb[par][:, h * 128:(h + 1) * 128], identb[:])
                return ins
            add("pe", f, r=[("onb", par), "identb"], w=[PK(bk2)])
            add("dve", lambda e: e.tensor_tensor(out=yT[g % 3][:, 4:8, :], in0=psb[bk2][:, 0:512].rearrange("p (h t) -> p h t", h=4),
                                                 in1=srg[m % 2][:, :, c * 128:(c + 1) * 128], op=ALU.mult),
                r=[("srg", m % 2)], w=[PK(bk2), ("yTr", g % 3)])

        def outp(g):
            b, sl = g // NCH, g % NX
            par = g % 2
            bks = []
            for half in range(2):
                bk = big.next()
                bks.append(bk)

                def f(e, half=half, bk=bk):
                    ins = None
                    for k in range(8):
                        ins = e.matmul(ps[bk][:, :], lhsT=yT[g % 3][:, k, :], rhs=w_out[:, k, half * 512:(half + 1) * 512],
                                       start=(k == 0), stop=(k == 7))
                    return ins
                add("pe", f, r=[("yTg", g % 3), ("yTr", g % 3), ("wout", half)], w=[PK(bk)])
                add("act", lambda e, half=half, bk=bk: e.activation(out=junkA[:, 0:512], in_=ps[bk][:, :], func=AF.Square,
                                                                   accum_out=zss[par][:, half:half + 1]),
                    w=[PK(bk), "junkA", ("zss", par, half)])
            add("dve", lambda e: e.tensor_tensor(out=zr[par][:, 0:1], in0=zss[par][:, 0:1], in1=zss[par][:, 1:2], op=ALU.add),
                r=[("zss", par, 0), ("zss", par, 1)], w=[("zr0", par)])
            add("dve", lambda e: e.tensor_scalar(out=zr[par][:, 0:1], in0=zr[par][:, 0:1], scalar1=1.0 / D, scalar2=EPS,
                                                 op0=ALU.mult, op1=ALU.add), r=[("zr0", par)], w=[("zr0", par)])
            add("pool", lambda e: e.tensor_tensor(out=zr[par][:, 1:2], in0=zr[par][:, 0:1], in1=neghalf[:, 0:1], op=ALU.pow),
                r=[("zr0", par), "neghalf"], w=[("zr1", par)])
            def out_tail():
                for half in range(2):
                    bk = bks[half]
                    add("dve", lambda e, half=half, bk=bk: e.scalar_tensor_tensor(
                        out=tmpz[par][:, half * 512:(half + 1) * 512], in0=ps[bk][:, :], scalar=zr[par][:, 1:2],
                        in1=gp[b][:, half * 512:(half + 1) * 512], op0=ALU.mult, op1=ALU.mult),
                        r=[("zr1", par), ("gp", b)], w=[PK(bk), ("tmp", par)])
                xl = 3 + g % 3
                add("dve", lambda e: e.tensor_tensor(out=xsl[xl], in0=xsl[xl], in1=tmpz[par], op=ALU.add),
                    r=[("tmp", par)], w=[("x", xl)])
                dma("sp", out_d[g * 128:(g + 1) * 128, :], xsl[xl], f"o{xl}", r=[("x", xl)], w=[("out", g)])
            return out_tail

        assert MTC == 2
        front0(3)
        front2(0)
        early_front(2, head=False)
        front2(1)
        for s in range(G + 4):
            if s == 1:
                precise_el()
            if s == 2:
                wload(w_out, wout_d, 0, 512, "w_o0", ("wout", 0))
                wload(w_out, wout_d, 512, 512, "w_o1", ("wout", 1))
            if 0 <= s - 3 < G:
                reload(s - 3)
            if s + MTC + 2 < G:
                front0(s + MTC + 2)
            if s + MTC < G:
                front2(s + MTC)
            if 0 <= s - 1 < G:
                mixA1(s - 1)
            if s + MTC + 1 < G:
                front1(s + MTC + 1)
            if s == NCH:
                tables(1)
            if s < G:
                projA(s)
            if s == 0:
                precise_mm()
            if 0 <= s - 3 < G:
                mixB2(s - 3)
            if s < G and s % MTC == 0:
                projB(s // MTC)
            gn_tail = mixB1(s - 2) if 0 <= s - 2 < G else None
            out_tail = outp(s - 4) if 0 <= s - 4 < G else None
            if gn_tail is not None:
                gn_tail()
            if 0 <= s - 1 < G:
                mixA2(s - 1)
            if out_tail is not None:
                out_tail()
        add("sp", lambda e: e.nop(), r=[("out", g) for g in range(G)])

        print("sbuf bytes remaining:", nc.sbuf_bytes_remaining, "ops:", len(S.ops))
        S.analyze()
        eng_sems = {n: es.enter_context(nc.semaphore("s_" + n)) for n in ("pe", "act", "dve", "pool", "sp")}
        dma_sems = {n: es.enter_context(nc.semaphore("d_" + n)) for n in S.dma_total}
        block = es.enter_context(nc.Block())

        @block.sync
        def _(e):
            S.emit_engine("sp", e, eng_sems, dma_sems)

        @block.tensor
        def _(e):
            S.emit_engine("pe", e, eng_sems, dma_sems)

        @block.scalar
        def _(e):
            S.emit_engine("act", e, eng_sems, dma_sems)

        @block.vector
        def _(e):
            S.emit_engine("dve", e, eng_sems, dma_sems)

        @block.gpsimd
        def _(e):
            S.emit_engine("pool", e, eng_sems, dma_sems)
    return nc


def _consts():
    h = np.arange(4, dtype=np.float64)
    gam = 1.0 - 2.0 ** (-5.0 - h)
    t = np.arange(128, dtype=np.float64)
    sc = 128.0 ** -0.5
    gkinv = (sc * gam[:, None] ** (-(t[None, :] + 1.0))).reshape(1, 512)
    gkinv = np.broadcast_to(gkinv, (128, 512)).astype(np.float32)
    kdec = (sc * gam[None, :] ** (127.0 - t[:, None])).astype(np.float32)
    epsp = (EPS * gam[None, :] ** (-2.0 * (t[:, None] + 1.0))).astype(np.float32)
    half = 64
    inv_freq = (1.0 / (10000.0 ** (np.arange(half, dtype=np.float32) / half))).astype(np.float32)
    invf = np.broadcast_to((inv_freq.astype(np.float64) / (2.0 * np.pi)).astype(np.float32)[None, :], (128, 64))
    cmask = (t[None, :] >= t[:, None]).astype(np.float32)
    sel = np.zeros((2, 256), np.float32)
    sel[0, 0:128] = 1.0
    sel[1, 128:256] = 1.0
    return dict(gkinv=np.ascontiguousarray(gkinv), kdec=kdec, epsp=epsp, invf=np.ascontiguousarray(invf),
                cmask=cmask, sel=sel, ident=np.eye(128, dtype=np.float32))


def _pack(g_pre_t, lng_t, gng_t, kdec, epsp, invf, cT, x0T, pos):
    cpk = np.zeros((128, 160), np.float32)
    cpk[:, 0:8], cpk[:, 8:12], cpk[:, 12:16], cpk[:, 16:20], cpk[:, 20:24] = g_pre_t, lng_t, gng_t, kdec, epsp
    cpk[:, 24:88], cpk[:, 88:104], cpk[:, 104:120] = invf, cT, x0T
    cpk[:, 120:152] = np.ascontiguousarray(pos.astype(np.int32)).view(np.float32)
    return cpk


_NC_CACHE = {}


def kernel(x, c, positions, w_ada, b_ada, g_pre, w_in, gmlp_ln_g, gmlp_ws, gmlp_bs, ret_gn_g, w_out, g_post):
    f = lambda a: np.ascontiguousarray(np.asarray(a))
    x, c, positions = f(x), f(c), f(positions)
    w_ada, b_ada, g_pre, w_in = f(w_ada)[0], f(b_ada)[0], f(g_pre)[0], f(w_in)[0]
    lng, ws, bs, gng, w_out, g_post = f(gmlp_ln_g)[0], f(gmlp_ws)[0], f(gmlp_bs)[0], f(ret_gn_g)[0], f(w_out)[0], f(g_post)[0]
    if "nc" not in _NC_CACHE:
        _NC_CACHE["nc"] = build_program()
    nc = _NC_CACHE["nc"]
    cst = _consts()
    g_pre_t, lng_t, gng_t = f(g_pre.reshape(8, 128).T), f(lng.reshape(4, 128).T), f(gng.reshape(4, 128).T)
    kdec_c, epsp_c, invf_c = cst.pop("kdec"), cst.pop("epsp"), cst.pop("invf")
    shared = dict(
        w_ada=w_ada, b_ada=f(b_ada.reshape(1, -1)), w_in=w_in,
        lng_r2=f(np.broadcast_to(lng.reshape(1, 512), (2, 512))), wsT=f(ws.transpose(2, 0, 1).reshape(128, 512)), bs=f(bs.reshape(1, 512)),
        w_out=w_out, g_post_r=f(np.broadcast_to(g_post[None, :], (128, D))), **cst)
    in_maps = []
    for i in range(NCORES):
        bsl = slice(i * NB, (i + 1) * NB)
        xm = f(x[bsl].reshape(G * 128, D))
        cT = f(c[bsl].reshape(NB, 8, 128).transpose(2, 1, 0).reshape(128, 16))
        pos = f(positions[bsl].reshape(NB, NCH, 128).transpose(2, 0, 1).reshape(128, NB * NCH).astype(np.int32))
        x0T = f(x[bsl, 0, :].reshape(NB, 8, 128).transpose(2, 1, 0).reshape(128, 8 * NB))
        cpk = _pack(g_pre_t, lng_t, gng_t, kdec_c, epsp_c, invf_c, cT, x0T, pos)
        in_maps.append(dict(x=xm, cpk=cpk, **shared))
    res = run_bass_kernel_spmd(nc, in_maps, core_ids=list(range(NCORES)))
    out = np.concatenate([np.asarray(r["out"]).reshape(NB, SEQ, D) for r in res.results], axis=0)
    return out.astype(np.float32, copy=False)
```

```python
import numpy as np
from contextlib import ExitStack
from collections import defaultdict
import concourse.bass as bass
import concourse.mybir as mybir
from concourse.bass_utils import run_bass_kernel_spmd

F32 = mybir.dt.float32
BF16 = mybir.dt.bfloat16
I32 = mybir.dt.int32
AF = mybir.ActivationFunctionType
ALU = mybir.AluOpType

NCORES = 8
BATCH, SEQ, D = 16, 2048, 1024
NB = BATCH // NCORES
NCH = SEQ // 128
G = NB * NCH
PW = 3584
O_GU, O_GV, O_GG, O_RQ, O_RK, O_RV, O_RG = 0, 512, 1024, 1536, 2048, 2560, 3072
EPS = 1e-6
MTC = 2
TM = MTC * 128
NMT = G // MTC
NX = 6
TWO_PI_SAFE = 6.283185
NEAR = 3


class Op:
    __slots__ = ("eng", "fn", "reads", "writes", "dma", "idx", "eidx", "deps", "signal", "tick", "dma_ord")


class Sched:
    def __init__(self):
        self.ops = []
        self.dma_count = defaultdict(int)

    def add(self, eng, fn, r=(), w=(), dma=None):
        op = Op()
        op.eng, op.fn, op.reads, op.writes, op.dma = eng, fn, tuple(r), tuple(w), dma
        op.idx = len(self.ops)
        op.deps = []
        op.signal = False
        op.tick = 0
        op.dma_ord = 0
        self.ops.append(op)
        return op

    def analyze(self):
        last_w = {}
        readers = {}
        ecount = defaultdict(int)
        dma_seen = defaultdict(int)
        for op in self.ops:
            op.eidx = ecount[op.eng]
            ecount[op.eng] += 1
            deps = {}
            for k in op.reads:
                if k in last_w:
                    deps[last_w[k]] = "raw"
            for k in op.writes:
                if k in last_w:
                    deps.setdefault(last_w[k], "waw")
                for rr in readers.get(k, ()):
                    deps.setdefault(rr, "war")
            for k in op.reads:
                readers.setdefault(k, []).append(op.idx)
            for k in op.writes:
                last_w[k] = op.idx
                readers[k] = []
            need = {}
            for pi, kind in deps.items():
                if pi == op.idx:
                    continue
                p = self.ops[pi]
                if p.dma is None and op.dma is None and p.eng == op.eng:
                    if op.eng == "pe":
                        continue
                    if (op.eidx - p.eidx) > NEAR:
                        continue
                if p.dma is not None:
                    key = ("dma", p.dma)
                    need[key] = max(need.get(key, 0), 16 * dma_seen[p.dma])
                else:
                    p.signal = True
                    key = ("eng", p.eng)
                    prev = need.get(key)
                    if prev is None or self.ops[prev].idx < p.idx:
                        need[key] = p.idx
            op.deps = need
            if op.dma is not None:
                dma_seen[op.dma] += 1
                op.dma_ord = dma_seen[op.dma]
        tick = defaultdict(int)
        for op in self.ops:
            if op.dma is None and op.signal:
                tick[op.eng] += 1
                op.tick = tick[op.eng]
        self.dma_total = dict(dma_seen)

    def emit_engine(self, eng_name, eng, eng_sems, dma_sems):
        waited = {}
        for op in self.ops:
            if op.eng != eng_name:
                continue
            for key, val in op.deps.items():
                if key[0] == "dma":
                    sem, v = dma_sems[key[1]], val
                else:
                    sem, v = eng_sems[key[1]], self.ops[val].tick
                if waited.get(key, 0) >= v:
                    continue
                eng.wait_ge(sem, v)
                waited[key] = v
            ins = op.fn(eng)
            if op.dma is not None:
                ins.then_inc(dma_sems[op.dma], 16)
            elif op.signal:
                ins.then_inc(eng_sems[op.eng], 1)


class PsPool:
    def __init__(self, banks):
        self.banks = banks
        self.i = 0

    def next(self):
        b = self.banks[self.i % len(self.banks)]
        self.i += 1
        return b


def build_program():
    nc = bass.Bass("TRN2", target_bir_lowering=False)
    dt_in = lambda name, shape, dt=F32: nc.dram_tensor(name, list(shape), dt, kind="ExternalInput").ap()
    x_d = dt_in("x", [G * 128, D])
    wada_d = dt_in("w_ada", [D, 3 * D])
    bada_d = dt_in("b_ada", [1, 3 * D])
    win_d = dt_in("w_in", [D, PW])
    wsT_d = dt_in("wsT", [128, 512])
    bs_d = dt_in("bs", [1, 512])
    lngr_d = dt_in("lng_r2", [2, 512])
    wout_d = dt_in("w_out", [D, D])
    gpost_d = dt_in("g_post_r", [128, D])
    ident_d = dt_in("ident", [128, 128])
    cmask_d = dt_in("cmask", [128, 128])
    gkinv_d = dt_in("gkinv", [128, 512])
    sel_d = dt_in("sel", [2, 256])
    cpk_d = dt_in("cpk", [128, 160])
    out_d = nc.dram_tensor("out", [G * 128, D], F32, kind="ExternalOutput").ap()

    gC = [float((1.0 - 2.0 ** (-5.0 - h)) ** 128) for h in range(4)]

    es = ExitStack()
    with es:
        sb = lambda name, shape, dt=F32: es.enter_context(nc.sbuf_tensor(name, list(shape), dt))
        w_in = sb("w_in_sb", [128, 8, PW], BF16)
        w_out = sb("w_out_sb", [128, 8, D], BF16)
        xbuf = sb("xbuf", [128, NX * D], F32)
        xbuf_bf = xbuf[:].bitcast(BF16)
        xsl = [xbuf[:, i * D:(i + 1) * D] for i in range(NX)]
        wa_stage = [xbuf_bf[:, j * 3072:(j + 1) * 3072] for j in range(4)]
        wa_xkeys = [[("x", 0), ("x", 1)], [("x", 1), ("x", 2)], [("x", 3), ("x", 4)], [("x", 4), ("x", 5)]]
        xsb = [sb(f"xsb{i}", [128, D], BF16) for i in range(2)]
        hT = [sb(f"hT{i}", [128, 8, TM], BF16) for i in range(2)]
        ug = [sb(f"ug{i}", [128, 4, TM], F32) for i in range(2)]
        srg = [sb(f"srg{i}", [128, 4, TM], F32) for i in range(2)]
        sgg = [sb(f"sgg{i}", [128, TM], F32) for i in range(2)]
        srgt = [sb(f"srgt{i}", [128, TM], F32) for i in range(2)]
        vln = [sb(f"vln{i}", [128, 512], BF16) for i in range(2)]
        qrot = [sb(f"qrot{i}", [128, 512], BF16) for i in range(2)]
        krot = [sb(f"krot{i}", [128, 512], BF16) for i in range(2)]
        At = [sb(f"At{i}", [128, 512], F32) for i in range(2)]
        Bt = [sb(f"Bt{i}", [128, 512], F32) for i in range(2)]
        ktok = [sb(f"ktok{i}", [128, 512], BF16) for i in range(2)]
        vbf = [sb(f"vbf{i}", [128, 512], BF16) for i in range(3)]
        qT = [sb(f"qT{i}", [128, 512], BF16) for i in range(2)]
        kT = [sb(f"kT{i}", [128, 512], BF16) for i in range(2)]
        msT = [sb(f"msT{i}", [128, 512], BF16) for i in range(2)]
        Sst = [sb(f"Sst{i}", [128, 512], F32) for i in range(NB)]
        Sbf = [sb(f"Sbf{i}", [128, 512], BF16) for i in range(2)]
        onb = [sb(f"onb{i}", [128, 512], BF16) for i in range(2)]
        yT = [sb(f"yT{i}", [128, 8, 128], BF16) for i in range(3)]
        scr12 = sb("scr12", [128, 3072], F32)
        tmpz = [scr12[:, 0:1024], scr12[:, 1024:2048]]
        junkA = scr12[:, 2048:3072]
        mod_sb = scr12[0:2, :]
        SCRK = [("tmp", 0), ("tmp", 1), "junkA"]
        cosT = sb("cosT", [128, NCH, 64], F32)
        sinT = sb("sinT", [128, NCH, 64], F32)
        gp = [sb(f"gp{i}", [128, D], F32) for i in range(NB)]
        gkinv = sb("gkinv_sb", [128, 512], F32)
        wsTb = sb("wsTb", [128, 512], BF16)
        identf = sb("identf", [128, 128], F32)
        identb = sb("identb", [128, 128], BF16)
        cmask = sb("cmask_sb", [128, 128], F32)
        cpk = sb("cpk_sb", [128, 160], F32)
        gpre, lng, gng, kdec, epsp = cpk[:, 0:8], cpk[:, 8:12], cpk[:, 12:16], cpk[:, 16:20], cpk[:, 20:24]
        invf, cTf = cpk[:, 24:88], cpk[:, 88:104]
        x0T = cpk[:, 104:120].rearrange("p (k b) -> p k b", b=NB)
        pos_i = cpk[:, 120:152].bitcast(I32)
        CPK_KEYS = ["gpre", "lng", "gng", "kdec", "epsp", "invf", "cTf", "x0T", "pos_i"]
        sel = sb("sel_sb", [2, 256], F32)
        cTb = sb("cTb", [128, 16], BF16)
        posf = sb("posf", [128, NCH], F32)
        gsT = sb("gsT", [128, 8, NB], F32)
        shT = sb("shT", [128, 8, NB], F32)
        neghalf = sb("neghalf", [128, 8], F32)
        onesb = sb("onesb", [2, 128], BF16)
        b2 = sb("b2", [2, 512], BF16)
        ginv2 = sb("ginv2", [2, 512], BF16)
        ssq = sb("ssq", [128, NX], F32)
        rstd = sb("rstd", [128, NX], F32)
        rtmp = sb("rtmp", [128, NX], F32)
        lnst = [sb(f"lnst{i}", [128, 4, 6], F32) for i in range(2)]
        lnmv = [sb(f"lnmv{i}", [128, 4, 2], F32) for i in range(2)]
        lnr = [sb(f"lnr{i}", [128, 4], F32) for i in range(2)]
        lnb = [sb(f"lnb{i}", [128, 4], F32) for i in range(2)]
        gnst = [sb(f"gnst{i}", [128, 4, 6], F32) for i in range(2)]
        gnmv = [sb(f"gnmv{i}", [128, 4, 2], F32) for i in range(2)]
        gnr = [sb(f"gnr{i}", [128, 4], F32) for i in range(2)]
        gnb = [sb(f"gnb{i}", [128, 4], F32) for i in range(2)]
        zss = [sb(f"zss{i}", [128, 2], F32) for i in range(2)]
        zr = [sb(f"zr{i}", [128, 2], F32) for i in range(2)]
        sqt = sb("sqt", [128, 8, NB], F32)
        sqs = sb("sqs", [128, NB], F32)
        ones_f = sb("ones_f", [128, 1], F32)
        Lq = sb("Lq", [128, 8, NB, 2], F32)
        r0t = sb("r0t", [1, NB], F32)
        rstd0 = sb("rstd0", [1, NB], F32)
        pos0f = sb("pos0f", [1, NB], F32)
        f0 = sb("f0", [1, NB * 64], F32)
        f0i = sb("f0i", [1, NB * 64], I32)
        f0b = sb("f0b", [1, NB * 64], F32)
        cos0 = sb("cos0", [1, NB * 64], F32)
        sin0 = sb("sin0", [1, NB * 64], F32)
        s4 = sb("s4", [1, 4], F32)
        s0p = [sb(f"s0p{i}", [1, 4], F32) for i in range(NB)]
        gpost = hT[0][:].rearrange("p k t -> p (k t)").bitcast(F32)
        GPOSTK = [("hT", 0, c) for c in range(MTC)]
        wsTf = hT[1][:].rearrange("p k t -> p (k t)").bitcast(F32)[:, 0:512]
        WSTFK = [("hT", 1, c) for c in range(MTC)]
        bs_f = At[0][0:1, :]
        hi_f = Bt[0][0:1, :]
        hi_b = At[1][:].bitcast(BF16)[0:1, 0:512]
        lo_b = Bt[1][:].bitcast(BF16)[0:1, 0:512]
        gi_f = Sst[1][0:2, :]
        ps = [es.enter_context(nc.psum_tensor(f"ps{i}", [128, 512], F32)) for i in range(8)]
        psb = [p[:].bitcast(BF16) for p in ps]
        big = PsPool([0, 1, 2, 3, 4, 5, 6, 7])
        trp = big
        mix = big
        PK = lambda b: ("ps", b)

        S = Sched()
        add = S.add

        def dma(q, out, in_, sem, r=(), w=()):
            add(q, lambda e, out=out, in_=in_: e.dma_start(out=out, in_=in_), r=r, w=w, dma=sem)

        dma("sp", cpk[:], cpk_d[:, :], "c0", w=CPK_KEYS)
        dma("sp", identf[:], ident_d[:, :], "c0", w=["identf"])
        dma("sp", sel[:], sel_d[:, :], "c0", w=["sel"])
        dma("sp", gpost, gpost_d[:, :], "c1", w=GPOSTK)
        dma("sp", wsTf, wsT_d[:, :], "c1", w=WSTFK)
        dma("sp", cmask[:], cmask_d[:, :], "c1", w=["cmask"])
        dma("sp", bs_f, bs_d[:, :], "c1", w=[("A", 0)])
        dma("sp", gkinv[:], gkinv_d[:, :], "c1", w=["gkinv"])

        def ada_load(k):
            j = k % 4
            if k < 8:
                dma("pool", wa_stage[j], wada_d[k * 128:(k + 1) * 128, :], f"wa{j}", w=wa_xkeys[j])
            else:
                dma("pool", wa_stage[j][0:1, :], bada_d[:, :], f"wa{j}", w=wa_xkeys[j])

        for k in range(4):
            ada_load(k)

        add("dve", lambda e: e.memset(neghalf[:], -0.5), w=["neghalf"])
        add("dve", lambda e: e.memset(onesb[:], 1.0), w=["onesb"])
        add("dve", lambda e: e.tensor_copy(out=identb[:], in_=identf[:]), r=["identf"], w=["identb"])
        add("act", lambda e: e.activation(out=cTb[:], in_=cTf[:], func=AF.Silu), r=["cTf"], w=["cTb"])

        cTb3 = cTb[:].rearrange("p (k b) -> p k b", b=NB)
        for k in range(9):
            j = k % 4

            def f(e, k=k, j=j):
                ins = None
                for n in range(6):
                    if k < 8:
                        lhsT = cTb3[:, k, :]
                        rhs = wa_stage[j][:, n * 512:(n + 1) * 512]
                    else:
                        lhsT = onesb[0:1, 0:2]
                        rhs = wa_stage[j][0:1, n * 512:(n + 1) * 512]
                    ins = e.matmul(ps[n][0:2, :], lhsT=lhsT, rhs=rhs, start=(k == 0), stop=(k == 8))
                return ins

            add("pe", f, r=wa_xkeys[j] + ["cTb", "onesb"], w=[PK(n) for n in range(6)])
            if k + 4 < 9:
                ada_load(k + 4)

        def wload(dst, src, c0, width, sem, key):
            for kk in range(2):
                dma("pool", dst[:, kk * 4:(kk + 1) * 4, c0:c0 + width],
                    src[kk * 512:(kk + 1) * 512, c0:c0 + width].rearrange("(k p) c -> p k c", p=128), sem, w=[key])

        early_buf = {0: (ug[0][:].rearrange("p j t -> p (j t)"), ("ug", 0)),
                     1: (ug[1][:].rearrange("p j t -> p (j t)"), ("ug", 1)),
                     2: (srg[0][:].rearrange("p j t -> p (j t)"), ("srg", 0))}

        def early_front(g, head=True, tail=True):
            xs_, xkey = early_buf[g]
            sl = g % NX
            if head:
                dma("sp", xs_, x_d[g * 128:(g + 1) * 128, :], f"xe{g}", w=[xkey])
                add("act", lambda e: e.activation(out=junkA, in_=xs_, func=AF.Square, accum_out=ssq[:, sl:sl + 1]),
                    r=[xkey], w=["junkA", ("ssq", sl)])
                add("dve", lambda e: e.tensor_scalar(out=rtmp[:, sl:sl + 1], in0=ssq[:, sl:sl + 1], scalar1=1.0 / D, scalar2=EPS,
                                                     op0=ALU.mult, op1=ALU.add), r=[("ssq", sl)], w=[("rtmp", sl)])
                add("pool", lambda e: e.tensor_tensor(out=rstd[:, sl:sl + 1], in0=rtmp[:, sl:sl + 1], in1=neghalf[:, 0:1], op=ALU.pow),
                    r=[("rtmp", sl), "neghalf"], w=[("rstd", sl)])
            if tail:
                xb = xsb[g % 2]
                add("act", lambda e: e.activation(out=xb[:], in_=xs_, func=AF.Copy, scale=rstd[:, sl:sl + 1]),
                    r=[xkey, ("rstd", sl)], w=[("xsb", g % 2)])

        early_front(0)
        early_front(1)
        early_front(2, tail=False)

        for (c0, nm) in ((O_GV, "gv"), (O_RQ, "rq"), (O_RK, "rk"), (O_RV, "rv"), (O_GG, "gg"), (O_GU, "gu"), (O_RG, "rg")):
            wload(w_in, win_d, c0, 512, "w_" + nm, ("win", c0))

        for n in range(6):
            add("dve" if n % 2 == 0 else "act",
                (lambda e, n=n: e.tensor_copy(out=mod_sb[:, n * 512:(n + 1) * 512], in_=ps[n][0:2, :])) if n % 2 == 0 else
                (lambda e, n=n: e.activation(out=mod_sb[:, n * 512:(n + 1) * 512], in_=ps[n][0:2, :], func=AF.Copy)),
                w=[PK(n)] + SCRK)
        def f(e):
            ins = None
            for j in range(16):
                ins = e.transpose(ps[6][:, j * 2:(j + 1) * 2], mod_sb[:, j * 128:(j + 1) * 128], identf[0:2, 0:2])
            return ins
        add("pe", f, r=SCRK + ["identf"], w=[PK(6)])
        ps6v = ps[6][:, 0:32].rearrange("p (j b) -> p j b", b=NB)
        add("dve", lambda e: e.tensor_copy(out=shT[:], in_=ps6v[:, 0:8, :]), w=[PK(6), "shT"])
        add("dve", lambda e: e.scalar_tensor_tensor(out=gsT[:], in0=ps6v[:, 8:16, :], scalar=1.0,
                                                    in1=gpre[:].unsqueeze(2).to_broadcast([128, 8, NB]),
                                                    op0=ALU.add, op1=ALU.mult), r=["gpre"], w=[PK(6), "gsT"])
        for b in range(NB):
            for half in range(2):
                bk = 7 if (b * 2 + half) % 2 == 0 else 6
                add("pe", lambda e, b=b, half=half, bk=bk: e.matmul(
                    ps[bk][:, :], lhsT=sel[0:2, b * 128:(b + 1) * 128],
                    rhs=mod_sb[:, 2048 + half * 512:2048 + (half + 1) * 512], start=True, stop=True),
                    r=SCRK + ["sel"], w=[PK(bk)])
                add("dve", lambda e, b=b, half=half, bk=bk: e.tensor_tensor(
                    out=gp[b][:, half * 512:(half + 1) * 512], in0=ps[bk][:, :],
                    in1=gpost[:, half * 512:(half + 1) * 512], op=ALU.mult),
                    r=GPOSTK, w=[PK(bk), ("gp", b)])
        add("dve", lambda e: e.tensor_tensor(out=wsTb[:].rearrange("p (g t) -> p g t", g=4),
                                             in0=wsTf.rearrange("p (g t) -> p g t", g=4),
                                             in1=cmask[:].unsqueeze(1).to_broadcast([128, 4, 128]), op=ALU.mult),
            r=WSTFK + ["cmask"], w=["wsTb"])
        add("dve", lambda e: e.tensor_copy(out=hi_b, in_=bs_f), r=[("A", 0)], w=[("A", 1)])
        add("dve", lambda e: e.tensor_copy(out=hi_f, in_=hi_b), r=[("A", 1)], w=[("B", 0)])
        add("dve", lambda e: e.tensor_tensor(out=lo_b, in0=bs_f, in1=hi_f, op=ALU.subtract), r=[("A", 0), ("B", 0)], w=[("B", 1)])
        dma("sp", b2[0:1, :], hi_b, "c2", r=[("A", 1)], w=["b2"])
        dma("sp", b2[1:2, :], lo_b, "c2", r=[("B", 1)], w=["b2"])

        dma("sp", gi_f, lngr_d[:, :], "c2", w=[("S", 1)])
        add("dve", lambda e: e.reciprocal(out=gi_f, in_=gi_f), r=[("S", 1)], w=[("S", 1)])
        add("dve", lambda e: e.tensor_copy(out=ginv2[:], in_=gi_f), r=[("S", 1)], w=["ginv2"])
        def precise_mm():
            ug0v = tmpz[0]
            ug1v = tmpz[1]
            sr0v = xsl[3]
            sr1v = tmpz[1]
            add("dve", lambda e: e.memset(ones_f[:], 1.0), w=["ones_f"])
            add("dve", lambda e: e.tensor_tensor(out=sqt[:], in0=x0T[:], in1=x0T[:], op=ALU.mult), r=["x0T"], w=["sqt"])
            add("dve", lambda e: e.reduce_sum(out=sqs[:], in_=sqt[:].rearrange("p k b -> p b k"), axis=mybir.AxisListType.X),
                r=["sqt"], w=["sqs"])
            add("pe", lambda e: e.matmul(ps[7][0:1, 0:NB], lhsT=ones_f[:, 0:1], rhs=sqs[:, :], start=True, stop=True),
                r=["ones_f", "sqs"], w=[PK(7)])
            add("dve", lambda e: e.tensor_scalar(out=r0t[:], in0=ps[7][0:1, 0:NB], scalar1=1.0 / D, scalar2=EPS, op0=ALU.mult, op1=ALU.add),
                w=[PK(7), "r0t"])
            add("pool", lambda e: e.tensor_tensor(out=rstd0[:], in0=r0t[:], in1=neghalf[0:1, 0:NB], op=ALU.pow), r=["r0t", "neghalf"], w=["rstd0"])
            add("dve", lambda e: e.tensor_tensor(out=Lq[:, :, :, 0], in0=x0T[:], in1=gsT[:], op=ALU.mult), r=["x0T", "gsT"], w=["Lq0"])
            add("dve", lambda e: e.tensor_copy(out=Lq[:, :, :, 1], in_=shT[:]), r=["shT"], w=["Lq1"])
            Lq2 = Lq[:].rearrange("p k b t -> p k (b t)")
            for k in range(8):
                slot = 3 + k % 3
                dma("sp", xsl[slot], win_d[k * 128:(k + 1) * 128, O_RQ:O_RQ + 1024], f"wq{k % 3}", w=[("x", slot)])

                def f(e, k=k, slot=slot):
                    ins = None
                    for half in range(2):
                        ins = e.matmul(ps[5 + half][0:2 * NB, :], lhsT=Lq2[:, k, :], rhs=xsl[slot][:, half * 512:(half + 1) * 512],
                                       start=(k == 0), stop=(k == 7))
                    return ins
                add("pe", f, r=[("x", slot), "Lq0", "Lq1"], w=[PK(5), PK(6)])
            add("dve", lambda e: e.tensor_copy(out=ug0v[0:2 * NB, 0:512], in_=ps[5][0:2 * NB, :]), w=[PK(5), ("tmp", 0)])
            add("act", lambda e: e.activation(out=ug0v[0:2 * NB, 512:1024], in_=ps[6][0:2 * NB, :], func=AF.Copy), w=[PK(6), ("tmp", 0)])

        def precise_el():
            ug0v = tmpz[0]
            ug1v = tmpz[1]
            sr0v = xsl[3]
            sr1v = tmpz[1]
            dma("sp", ug1v[0:1, :], ug0v[1:2, :], "c3", r=[("tmp", 0)], w=[("tmp", 1)])
            dma("sp", sr0v[0:1, :], ug0v[2:3, :], "c3", r=[("tmp", 0)], w=[("x", 3)])
            add("dve", lambda e: e.tensor_copy(out=pos0f[:], in_=pos_i[0:1, ::NCH]), r=["pos_i"], w=["pos0f"])
            f0_3 = f0[:].rearrange("p (b d) -> p b d", b=NB)
            add("dve", lambda e: e.tensor_tensor(out=f0_3, in0=pos0f[:].unsqueeze(2).to_broadcast([1, NB, 64]),
                                                 in1=invf[0:1, :].unsqueeze(1).to_broadcast([1, NB, 64]), op=ALU.mult),
                r=["pos0f", "invf"], w=["f0"])
            add("dve", lambda e: e.tensor_copy(out=f0i[:], in_=f0[:]), r=["f0"], w=["f0i"])
            add("dve", lambda e: e.tensor_copy(out=f0b[:], in_=f0i[:]), r=["f0i"], w=["f0b"])
            add("dve", lambda e: e.tensor_tensor(out=f0[:], in0=f0[:], in1=f0b[:], op=ALU.subtract), r=["f0", "f0b"], w=["f0"])
            add("act", lambda e: e.activation(out=sin0[:], in_=f0[:], func=AF.Sin, scale=TWO_PI_SAFE), r=["f0"], w=["sin0"])
            add("dve", lambda e: e.tensor_scalar(out=f0b[:], in0=f0[:], scalar1=0.25, scalar2=None, op0=ALU.add), r=["f0"], w=["f0b"])
            add("dve", lambda e: e.tensor_copy(out=f0i[:], in_=f0b[:]), r=["f0b"], w=["f0i"])
            add("dve", lambda e: e.tensor_copy(out=f0[:], in_=f0i[:]), r=["f0i", "sin0"], w=["f0"])
            add("dve", lambda e: e.tensor_tensor(out=f0b[:], in0=f0b[:], in1=f0[:], op=ALU.subtract), r=["f0", "f0b"], w=["f0b"])
            add("act", lambda e: e.activation(out=cos0[:], in_=f0b[:], func=AF.Sin, scale=TWO_PI_SAFE), r=["f0b"], w=["cos0"])
            for b in range(NB):
                if b == 1:
                    dma("sp", sr1v[0:1, :], ug0v[3:4, :], "c3", r=[("tmp", 0)], w=[("tmp", 1)])
                U = (ug0v if b == 0 else sr0v)[0:1, :]
                Wb = (ug1v if b == 0 else sr1v)[0:1, :]
                ku = ("tmp", 0) if b == 0 else ("x", 3)
                kw = ("tmp", 1) if b == 0 else ("tmp", 1)
                Aq = xsl[4][0:1, :]
                Bq = xsl[5][0:1, :]
                U4 = U.rearrange("p (h two d) -> p h two d", h=8, two=2)
                A4 = Aq.rearrange("p (h two d) -> p h two d", h=8, two=2)
                B4 = Bq.rearrange("p (h two d) -> p h two d", h=8, two=2)
                cb = cos0[0:1, b * 64:(b + 1) * 64].unsqueeze(1).unsqueeze(1).to_broadcast([1, 8, 2, 64])
                sbc = sin0[0:1, b * 64:(b + 1) * 64].unsqueeze(1).unsqueeze(1).to_broadcast([1, 8, 2, 64])
                add("dve", lambda e, U=U, Wb=Wb, b=b: e.scalar_tensor_tensor(out=U, in0=U, scalar=rstd0[0:1, b:b + 1], in1=Wb,
                                                                             op0=ALU.mult, op1=ALU.add),
                    r=[ku, kw, "rstd0"], w=[ku])
                add("dve", lambda e, U4=U4, A4=A4, cb=cb: e.tensor_tensor(out=A4, in0=U4, in1=cb, op=ALU.mult), r=[ku, "cos0"], w=[("x", 4)])
                add("dve", lambda e, U4=U4, B4=B4, sbc=sbc: e.tensor_tensor(out=B4, in0=U4[:, :, ::-1, :], in1=sbc, op=ALU.mult),
                    r=[ku, "sin0"], w=[("x", 5)])
                add("pool", lambda e, U4=U4, A4=A4, B4=B4: e.tensor_tensor(out=U4[:, :, 0, :], in0=A4[:, :, 0, :], in1=B4[:, :, 0, :], op=ALU.subtract),
                    r=[("x", 4), ("x", 5)], w=[ku])
                add("pool", lambda e, U4=U4, A4=A4, B4=B4: e.tensor_tensor(out=U4[:, :, 1, :], in0=A4[:, :, 1, :], in1=B4[:, :, 1, :], op=ALU.add),
                    r=[("x", 4), ("x", 5)], w=[ku])
                add("dve", lambda e, U=U, Aq=Aq: e.tensor_tensor(out=Aq[:, 0:512], in0=U[:, 0:512], in1=U[:, 512:1024], op=ALU.mult),
                    r=[ku], w=[("x", 4)])
                add("dve", lambda e, Aq=Aq: e.reduce_sum(out=s4[:], in_=Aq[:, 0:512].rearrange("p (h d) -> p h d", h=4), axis=mybir.AxisListType.X),
                    r=[("x", 4)], w=["s4"])
                add("dve", lambda e, b=b: e.tensor_tensor(out=s0p[b][:], in0=s4[:], in1=gkinv[0:1, ::128], op=ALU.mult),
                    r=["s4", "gkinv"], w=[("s0p", b)])

        T1, T2, T3 = tmpz[0], tmpz[1], junkA
        T2i = T2.bitcast(I32)
        cos2 = cosT[:].rearrange("p n d -> p (n d)")
        sin2 = sinT[:].rearrange("p n d -> p (n d)")
        T1_3 = T1.rearrange("p (n d) -> p n d", d=64)

        def tables(b):
            add("dve", lambda e: e.tensor_copy(out=posf[:], in_=pos_i[:, b * NCH:(b + 1) * NCH]), r=["pos_i"], w=["posf"])
            add("dve", lambda e: e.tensor_tensor(out=T1_3, in0=posf[:].unsqueeze(2).to_broadcast([128, NCH, 64]),
                                                 in1=invf[:].unsqueeze(1).to_broadcast([128, NCH, 64]), op=ALU.mult),
                r=["posf", "invf"], w=[("tmp", 0)])
            add("dve", lambda e: e.tensor_copy(out=T2i, in_=T1), r=[("tmp", 0)], w=[("tmp", 1)])
            add("dve", lambda e: e.tensor_copy(out=T3, in_=T2i), r=[("tmp", 1)], w=["junkA"])
            add("dve", lambda e: e.tensor_tensor(out=T1, in0=T1, in1=T3, op=ALU.subtract), r=[("tmp", 0), "junkA"], w=[("tmp", 0)])
            add("act", lambda e: e.activation(out=sin2, in_=T1, func=AF.Sin, scale=TWO_PI_SAFE), r=[("tmp", 0)], w=["sinT"])
            add("dve", lambda e: e.tensor_scalar(out=T3, in0=T1, scalar1=0.25, scalar2=None, op0=ALU.add), r=[("tmp", 0)], w=["junkA"])
            add("dve", lambda e: e.tensor_copy(out=T2i, in_=T3), r=["junkA"], w=[("tmp", 1)])
            add("dve", lambda e: e.tensor_copy(out=T1, in_=T2i), r=[("tmp", 1)], w=[("tmp", 0)])
            add("dve", lambda e: e.tensor_tensor(out=T3, in0=T3, in1=T1, op=ALU.subtract), r=[("tmp", 0), "junkA"], w=["junkA"])
            add("act", lambda e: e.activation(out=cos2, in_=T3, func=AF.Sin, scale=TWO_PI_SAFE), r=["junkA"], w=["cosT"])

        tables(0)

        def front0(g):
            xl = g % 3
            dma("sp", xsl[xl], x_d[g * 128:(g + 1) * 128, :], f"x{xl}", w=[("x", xl)])

        def reload(g):
            xl = 3 + g % 3
            dma("sp", xsl[xl], x_d[g * 128:(g + 1) * 128, :], f"x{xl}", w=[("x", xl)])

        def front1(g):
            b, m, c, sl = g // NCH, g // MTC, g % MTC, g % NX
            xl = g % 3
            xs_ = xsl[xl]
            add("act", lambda e: e.activation(out=junkA, in_=xs_, func=AF.Square, accum_out=ssq[:, sl:sl + 1]),
                r=[("x", xl)], w=["junkA", ("ssq", sl)])
            add("dve", lambda e: e.tensor_scalar(out=rtmp[:, sl:sl + 1], in0=ssq[:, sl:sl + 1], scalar1=1.0 / D, scalar2=EPS,
                                                 op0=ALU.mult, op1=ALU.add), r=[("ssq", sl)], w=[("rtmp", sl)])
            add("pool", lambda e: e.tensor_tensor(out=rstd[:, sl:sl + 1], in0=rtmp[:, sl:sl + 1], in1=neghalf[:, 0:1], op=ALU.pow),
                r=[("rtmp", sl), "neghalf"], w=[("rstd", sl)])
            xb = xsb[g % 2]
            add("act", lambda e: e.activation(out=xb[:], in_=xs_, func=AF.Copy, scale=rstd[:, sl:sl + 1]),
                r=[("x", xl), ("rstd", sl)], w=[("xsb", g % 2)])

        def front2(g):
            b, m, c, sl = g // NCH, g // MTC, g % MTC, g % NX
            xb = xsb[g % 2]
            bk = trp.next()

            def f(e):
                ins = None
                for k in range(8):
                    ins = e.transpose(psb[bk][:, k * 128:(k + 1) * 128], xb[:, k * 128:(k + 1) * 128], identb[:])
                return ins
            add("pe", f, r=[("xsb", g % 2), "identb"], w=[PK(bk)])
            hdst = hT[m % 2]

            def fa(e):
                ins = None
                for k in range(0, 4):
                    ins = e.activation(out=hdst[:, k, c * 128:(c + 1) * 128], in_=psb[bk][:, k * 128:(k + 1) * 128],
                                       func=AF.Identity, scale=gsT[:, k, b:b + 1], bias=shT[:, k, b:b + 1])
                return ins

            def fd(e):
                ins = None
                for k in range(4, 8):
                    ins = e.tensor_scalar(out=hdst[:, k, c * 128:(c + 1) * 128], in0=psb[bk][:, k * 128:(k + 1) * 128],
                                          scalar1=gsT[:, k, b:b + 1], scalar2=shT[:, k, b:b + 1], op0=ALU.mult, op1=ALU.add)
                return ins
            add("act", fa, r=["gsT", "shT"], w=[PK(bk), ("hT", m % 2, c)])
            add("dve", fd, r=["gsT", "shT"], w=[PK(bk), ("hT", m % 2, c)])

        def projB(m):
            hsrc = hT[m % 2]
            hkeys = [("hT", m % 2, c) for c in range(MTC)]

            def proj(col0, wkey):
                bk = big.next()

                def f(e):
                    ins = None
                    for k in range(8):
                        ins = e.matmul(ps[bk][:, 0:TM], lhsT=w_in[:, k, col0:col0 + 128], rhs=hsrc[:, k, :],
                                       start=(k == 0), stop=(k == 7))
                    return ins
                add("pe", f, r=hkeys + [wkey], w=[PK(bk)])
                return bk

            for j in range(4):
                sg = sgg[j % 2]
                bk = proj(O_GG + j * 128, ("win", O_GG))
                add("act", lambda e, bk=bk, sg=sg: e.activation(out=sg[:], in_=ps[bk][:, 0:TM], func=AF.Silu),
                    w=[PK(bk), ("sgg", j % 2)])
                bk = proj(O_GU + j * 128, ("win", O_GU))
                add("dve", lambda e, bk=bk, sg=sg, j=j: e.scalar_tensor_tensor(
                    out=ug[m % 2][:, j, :], in0=ps[bk][:, 0:TM], scalar=lng[:, j:j + 1], in1=sg[:],
                    op0=ALU.mult, op1=ALU.mult), r=[("sgg", j % 2), "lng"], w=[PK(bk), ("ug", m % 2)])
            for j in range(4):
                st = srgt[j % 2]
                bk = proj(O_RG + j * 128, ("win", O_RG))
                add("act", lambda e, bk=bk, st=st: e.activation(out=st[:], in_=ps[bk][:, 0:TM], func=AF.Silu),
                    w=[PK(bk), ("srgt", j % 2)])
                add("act", lambda e, st=st, j=j: e.activation(out=srg[m % 2][:, j, :], in_=st[:], func=AF.Copy, scale=gng[:, j:j + 1]),
                    r=[("srgt", j % 2), "gng"], w=[("srg", m % 2)])

        def rotary(bk, n, dst, par, kname):
            A, Bm = At[par], Bt[par]
            p4 = ps[bk][:, :].rearrange("p (h two d) -> p h two d", h=4, two=2)
            A4 = A[:].rearrange("p (h two d) -> p h two d", h=4, two=2)
            B4 = Bm[:].rearrange("p (h two d) -> p h two d", h=4, two=2)
            d4 = dst[:].rearrange("p (h two d) -> p h two d", h=4, two=2)
            cb = cosT[:, n, :].unsqueeze(1).unsqueeze(1).to_broadcast([128, 4, 2, 64])
            sbc = sinT[:, n, :].unsqueeze(1).unsqueeze(1).to_broadcast([128, 4, 2, 64])
            add("dve", lambda e: e.tensor_tensor(out=A4, in0=p4, in1=cb, op=ALU.mult), r=["cosT"], w=[PK(bk), ("A", par)])
            add("dve", lambda e: e.tensor_tensor(out=B4, in0=p4[:, :, ::-1, :], in1=sbc, op=ALU.mult), r=["sinT"], w=[PK(bk), ("B", par)])
            add("pool", lambda e: e.tensor_tensor(out=d4[:, :, 0, :], in0=A4[:, :, 0, :], in1=B4[:, :, 0, :], op=ALU.subtract),
                r=[("A", par), ("B", par)], w=[kname])
            add("pool", lambda e: e.tensor_tensor(out=d4[:, :, 1, :], in0=A4[:, :, 1, :], in1=B4[:, :, 1, :], op=ALU.add),
                r=[("A", par), ("B", par)], w=[kname])

        def projA(g):
            b, n, m, c = g // NCH, g % NCH, g // MTC, g % MTC
            hsrc = hT[m % 2]
            hkey = ("hT", m % 2, c)

            def proj(col0):
                bk = big.next()

                def f(e):
                    ins = None
                    for k in range(8):
                        ins = e.matmul(ps[bk][:, :], lhsT=hsrc[:, k, c * 128:(c + 1) * 128], rhs=w_in[:, k, col0:col0 + 512],
                                       start=(k == 0), stop=(k == 7))
                    return ins
                add("pe", f, r=[hkey, ("win", col0)], w=[PK(bk)])
                return bk

            par = g % 2
            bk = proj(O_GV)
            st, mv, lr, lb = lnst[par], lnmv[par], lnr[par], lnb[par]

            def fs(e, bk=bk):
                ins = None
                for q in range(4):
                    ins = e.bn_stats(out=st[:, q, :], in_=ps[bk][:, q * 128:(q + 1) * 128])
                return ins
            add("dve", fs, w=[PK(bk), ("lnst", par)])

            def fg(e):
                ins = None
                for q in range(4):
                    ins = e.bn_aggr(out=mv[:, q, :], in_=st[:, q, :])
                return ins
            add("dve", fg, r=[("lnst", par)], w=[("lnmv", par)])
            add("dve", lambda e: e.tensor_scalar(out=lr[:], in0=mv[:, :, 1], scalar1=EPS, scalar2=None, op0=ALU.add),
                r=[("lnmv", par)], w=[("lnr", par)])
            add("pool", lambda e: e.tensor_tensor(out=lr[:], in0=lr[:], in1=neghalf[:, 0:4], op=ALU.pow), r=[("lnr", par)], w=[("lnr", par)])
            def ln_tail(bk=bk):
                add("dve", lambda e: e.scalar_tensor_tensor(out=lb[:], in0=mv[:, :, 0], scalar=-1.0, in1=lr[:], op0=ALU.mult, op1=ALU.mult),
                    r=[("lnmv", par), ("lnr", par)], w=[("lnb", par)])

                def fn_(e, bk=bk):
                    ins = None
                    for q in range(4):
                        ins = e.activation(out=vln[par][:, q * 128:(q + 1) * 128], in_=ps[bk][:, q * 128:(q + 1) * 128],
                                           func=AF.Identity, scale=lr[:, q:q + 1], bias=lb[:, q:q + 1])
                    return ins
                add("act", fn_, r=[("lnr", par), ("lnb", par)], w=[PK(bk), ("vln", par)])
            bk = proj(O_RQ)
            rotary(bk, n, qrot[par], 0, ("qrot", par))
            ln_tail()
            bk = proj(O_RK)
            rotary(bk, n, krot[par], 1, ("krot", par))
            add("pool", lambda e: e.tensor_tensor(out=ktok[par][:].rearrange("p (h d) -> p h d", h=4),
                                                  in0=krot[par][:].rearrange("p (h d) -> p h d", h=4),
                                                  in1=kdec[:].unsqueeze(2).to_broadcast([128, 4, 128]), op=ALU.mult),
                r=[("krot", par), "kdec"], w=[("ktok", par)])
            bk = proj(O_RV)
            add("act", lambda e, bk=bk: e.activation(out=vbf[g % 3][:], in_=ps[bk][:, :], func=AF.Copy), w=[PK(bk), ("vbf", g % 3)])

        def mixA1(g):
            b, n, m, c = g // NCH, g % NCH, g // MTC, g % MTC
            par = g % 2
            bk = trp.next()

            def f(e, bk=bk):
                ins = None
                for h in range(4):
                    ins = e.transpose(psb[bk][:, h * 128:(h + 1) * 128], qrot[par][:, h * 128:(h + 1) * 128], identb[:])
                for h in range(4):
                    ins = e.transpose(psb[bk][:, 512 + h * 128:512 + (h + 1) * 128], krot[par][:, h * 128:(h + 1) * 128], identb[:])
                return ins
            add("pe", f, r=[("qrot", par), ("krot", par), "identb"], w=[PK(bk)])
            add("act", lambda e, bk=bk: e.activation(out=qT[par][:], in_=psb[bk][:, 0:512], func=AF.Copy), w=[PK(bk), ("qT", par)])
            add("dve", lambda e, bk=bk: e.tensor_tensor(out=kT[par][:], in0=psb[bk][:, 512:1024], in1=gkinv[:], op=ALU.mult),
                r=["gkinv"], w=[PK(bk), ("kT", par)])

        def mixA2(g):
            b, n, m, c = g // NCH, g % NCH, g // MTC, g % MTC
            par = g % 2
            bk = mix.next()

            def f(e, bk=bk):
                ins = None
                for q in range(4):
                    ins = e.matmul(ps[bk][:, q * 128:(q + 1) * 128], lhsT=ginv2[:, q * 128:(q + 1) * 128],
                                   rhs=b2[:, q * 128:(q + 1) * 128], start=(q == 0), stop=False)
                for q in range(4):
                    ins = e.matmul(ps[bk][:, q * 128:(q + 1) * 128], lhsT=vln[par][:, q * 128:(q + 1) * 128],
                                   rhs=wsTb[:, q * 128:(q + 1) * 128], start=False, stop=(q == 3))
                return ins
            add("pe", f, r=[("vln", par), "wsTb", "b2", "ginv2"], w=[PK(bk)])
            add("dve", lambda e, bk=bk: e.tensor_tensor(out=yT[g % 3][:, 0:4, :], in0=ps[bk][:, :].rearrange("p (q t) -> p q t", q=4),
                                                 in1=ug[m % 2][:, :, c * 128:(c + 1) * 128], op=ALU.mult),
                r=[("ug", m % 2)], w=[PK(bk), ("yTg", g % 3)])
            bk = mix.next()

            def f(e, bk=bk):
                ins = None
                for h in range(4):
                    ins = e.matmul(ps[bk][:, h * 128:(h + 1) * 128], lhsT=kT[par][:, h * 128:(h + 1) * 128],
                                   rhs=qT[par][:, h * 128:(h + 1) * 128], start=True, stop=True)
                return ins
            add("pe", f, r=[("qT", par), ("kT", par)], w=[PK(bk)])
            add("dve", lambda e, bk=bk: e.tensor_tensor(out=msT[par][:].rearrange("p (h q) -> p h q", h=4),
                                                 in0=ps[bk][:, :].rearrange("p (h q) -> p h q", h=4),
                                                 in1=cmask[:].unsqueeze(1).to_broadcast([128, 4, 128]), op=ALU.mult),
                r=["cmask"], w=[PK(bk), ("msT", par)])
            if n == 0:
                add("dve", lambda e: e.tensor_copy(out=msT[par][0:1, ::128], in_=s0p[b][:]), r=[("s0p", b)], w=[("msT", par)])
            bk = mix.next()

            def f(e, bk=bk):
                ins = None
                for h in range(4):
                    ins = e.matmul(ps[bk][:, h * 128:(h + 1) * 128], lhsT=ktok[par][:, h * 128:(h + 1) * 128],
                                   rhs=vbf[g % 3][:, h * 128:(h + 1) * 128], start=True, stop=True)
                return ins
            add("pe", f, r=[("ktok", par), ("vbf", g % 3)], w=[PK(bk)])
            if n == 0:
                add("dve", lambda e, bk=bk: e.tensor_copy(out=Sst[b][:], in_=ps[bk][:, :]), w=[PK(bk), ("S", b)])
            else:
                def f(e, bk=bk):
                    ins = None
                    for h in range(4):
                        ins = e.scalar_tensor_tensor(out=Sst[b][:, h * 128:(h + 1) * 128], in0=Sst[b][:, h * 128:(h + 1) * 128],
                                                     scalar=gC[h], in1=ps[bk][:, h * 128:(h + 1) * 128], op0=ALU.mult, op1=ALU.add)
                    return ins
                add("dve", f, r=[("S", b)], w=[PK(bk), ("S", b)])
            add("act", lambda e: e.activation(out=Sbf[(g + 1) % 2][:], in_=Sst[b][:], func=AF.Copy),
                r=[("S", b)], w=[("Sbf", (g + 1) % 2)])

        def mixB1(g):
            b, n, m, c = g // NCH, g % NCH, g // MTC, g % MTC
            par = g % 2
            bk = mix.next()

            def f(e):
                ins = None
                for h in range(4):
                    hs = slice(h * 128, (h + 1) * 128)
                    ins = e.matmul(ps[bk][:, hs], lhsT=msT[par][:, hs], rhs=vbf[g % 3][:, hs], start=True, stop=(n == 0))
                    if n > 0:
                        ins = e.matmul(ps[bk][:, hs], lhsT=qT[par][:, hs], rhs=Sbf[g % 2][:, hs], start=False, stop=True)
                return ins
            rk = [("msT", par), ("vbf", g % 3), ("qT", par)] + ([("Sbf", g % 2)] if n > 0 else [])
            add("pe", f, r=rk, w=[PK(bk)])
            st, mv, gr, gb = gnst[par], gnmv[par], gnr[par], gnb[par]

            def fs(e):
                ins = None
                for q in range(4):
                    ins = e.bn_stats(out=st[:, q, :], in_=ps[bk][:, q * 128:(q + 1) * 128])
                return ins
            add("dve", fs, w=[PK(bk), ("gnst", par)])

            def fg(e):
                ins = None
                for q in range(4):
                    ins = e.bn_aggr(out=mv[:, q, :], in_=st[:, q, :])
                return ins
            add("dve", fg, r=[("gnst", par)], w=[("gnmv", par)])
            add("dve", lambda e: e.tensor_tensor(out=gr[:], in0=mv[:, :, 1], in1=epsp[:], op=ALU.add), r=[("gnmv", par), "epsp"], w=[("gnr", par)])
            add("pool", lambda e: e.tensor_tensor(out=gr[:], in0=gr[:], in1=neghalf[:, 0:4], op=ALU.pow), r=[("gnr", par)], w=[("gnr", par)])
            def gn_tail():
                add("dve", lambda e: e.scalar_tensor_tensor(out=gb[:], in0=mv[:, :, 0], scalar=-1.0, in1=gr[:], op0=ALU.mult, op1=ALU.mult),
                    r=[("gnmv", par), ("gnr", par)], w=[("gnb", par)])

                def fn_(e):
                    ins = None
                    for q in range(4):
                        ins = e.activation(out=onb[par][:, q * 128:(q + 1) * 128], in_=ps[bk][:, q * 128:(q + 1) * 128],
                                           func=AF.Identity, scale=gr[:, q:q + 1], bias=gb[:, q:q + 1])
                    return ins
                add("act", fn_, r=[("gnr", par), ("gnb", par)], w=[PK(bk), ("onb", par)])
            return gn_tail

        def mixB2(g):
            b, n, m, c = g // NCH, g % NCH, g // MTC, g % MTC
            par = g % 2
            bk2 = trp.next()

            def f(e):
                ins = None
                for h in range(4):
                    ins = e.transpose(psb[bk2][:, h * 128:(h + 1) * 128], onb[par][:, h * 128:(h + 1) * 128], identb[:])
                return ins
            add("pe", f, r=[("onb", par), "identb"], w=[PK(bk2)])
            add("dve", lambda e: e.tensor_tensor(out=yT[g % 3][:, 4:8, :], in0=psb[bk2][:, 0:512].rearrange("p (h t) -> p h t", h=4),
                                                 in1=srg[m % 2][:, :, c * 128:(c + 1) * 128], op=ALU.mult),
                r=[("srg", m % 2)], w=[PK(bk2), ("yTr", g % 3)])

        def outp(g):
            b, sl = g // NCH, g % NX
            par = g % 2
            bks = []
            for half in range(2):
                bk = big.next()
                bks.append(bk)

                def f(e, half=half, bk=bk):
                    ins = None
                    for k in range(8):
                        ins = e.matmul(ps[bk][:, :], lhsT=yT[g % 3][:, k, :], rhs=w_out[:, k, half * 512:(half + 1) * 512],
                                       start=(k == 0), stop=(k == 7))
                    return ins
                add("pe", f, r=[("yTg", g % 3), ("yTr", g % 3), ("wout", half)], w=[PK(bk)])
                add("act", lambda e, half=half, bk=bk: e.activation(out=junkA[:, 0:512], in_=ps[bk][:, :], func=AF.Square,
                                                                   accum_out=zss[par][:, half:half + 1]),
                    w=[PK(bk), "junkA", ("zss", par, half)])
            add("dve", lambda e: e.tensor_tensor(out=zr[par][:, 0:1], in0=zss[par][:, 0:1], in1=zss[par][:, 1:2], op=ALU.add),
                r=[("zss", par, 0), ("zss", par, 1)], w=[("zr0", par)])
            add("dve", lambda e: e.tensor_scalar(out=zr[par][:, 0:1], in0=zr[par][:, 0:1], scalar1=1.0 / D, scalar2=EPS,
                                                 op0=ALU.mult, op1=ALU.add), r=[("zr0", par)], w=[("zr0", par)])
            add("pool", lambda e: e.tensor_tensor(out=zr[par][:, 1:2], in0=zr[par][:, 0:1], in1=neghalf[:, 0:1], op=ALU.pow),
                r=[("zr0", par), "neghalf"], w=[("zr1", par)])
            def out_tail():
                for half in range(2):
                    bk = bks[half]
                    add("dve", lambda e, half=half, bk=bk: e.scalar_tensor_tensor(
                        out=tmpz[par][:, half * 512:(half + 1) * 512], in0=ps[bk][:, :], scalar=zr[par][:, 1:2],
                        in1=gp[b][:, half * 512:(half + 1) * 512], op0=ALU.mult, op1=ALU.mult),
                        r=[("zr1", par), ("gp", b)], w=[PK(bk), ("tmp", par)])
                xl = 3 + g % 3
                add("dve", lambda e: e.tensor_tensor(out=xsl[xl], in0=xsl[xl], in1=tmpz[par], op=ALU.add),
                    r=[("tmp", par)], w=[("x", xl)])
                dma("sp", out_d[g * 128:(g + 1) * 128, :], xsl[xl], f"o{xl}", r=[("x", xl)], w=[("out", g)])
            return out_tail

        assert MTC == 2
        front0(3)
        front2(0)
        early_front(2, head=False)
        front2(1)
        precise_mm()
        precise_el()
        for s in range(G + 4):
            if s == 2:
                wload(w_out, wout_d, 0, 512, "w_o0", ("wout", 0))
                wload(w_out, wout_d, 512, 512, "w_o1", ("wout", 1))
            if 0 <= s - 3 < G:
                reload(s - 3)
            if s + MTC + 2 < G:
                front0(s + MTC + 2)
            if s + MTC < G:
                front2(s + MTC)
            if 0 <= s - 1 < G:
                mixA1(s - 1)
            if s + MTC + 1 < G:
                front1(s + MTC + 1)
            if s == NCH:
                tables(1)
            if s < G:
                projA(s)
            if 0 <= s - 3 < G:
                mixB2(s - 3)
            if s < G and s % MTC == 0:
                projB(s // MTC)
            gn_tail = mixB1(s - 2) if 0 <= s - 2 < G else None
            out_tail = outp(s - 4) if 0 <= s - 4 < G else None
            if gn_tail is not None:
                gn_tail()
            if 0 <= s - 1 < G:
                mixA2(s - 1)
            if out_tail is not None:
                out_tail()
        add("sp", lambda e: e.nop(), r=[("out", g) for g in range(G)])

        print("sbuf bytes remaining:", nc.sbuf_bytes_remaining, "ops:", len(S.ops))
        S.analyze()
        eng_sems = {n: es.enter_context(nc.semaphore("s_" + n)) for n in ("pe", "act", "dve", "pool", "sp")}
        dma_sems = {n: es.enter_context(nc.semaphore("d_" + n)) for n in S.dma_total}
        block = es.enter_context(nc.Block())

        @block.sync
        def _(e):
            S.emit_engine("sp", e, eng_sems, dma_sems)

        @block.tensor
        def _(e):
            S.emit_engine("pe", e, eng_sems, dma_sems)

        @block.scalar
        def _(e):
            S.emit_engine("act", e, eng_sems, dma_sems)

        @block.vector
        def _(e):
            S.emit_engine("dve", e, eng_sems, dma_sems)

        @block.gpsimd
        def _(e):
            S.emit_engine("pool", e, eng_sems, dma_sems)
    return nc


def _consts():
    h = np.arange(4, dtype=np.float64)
    gam = 1.0 - 2.0 ** (-5.0 - h)
    t = np.arange(128, dtype=np.float64)
    sc = 128.0 ** -0.5
    gkinv = (sc * gam[:, None] ** (-(t[None, :] + 1.0))).reshape(1, 512)
    gkinv = np.broadcast_to(gkinv, (128, 512)).astype(np.float32)
    kdec = (sc * gam[None, :] ** (127.0 - t[:, None])).astype(np.float32)
    epsp = (EPS * gam[None, :] ** (-2.0 * (t[:, None] + 1.0))).astype(np.float32)
    half = 64
    inv_freq = (1.0 / (10000.0 ** (np.arange(half, dtype=np.float32) / half))).astype(np.float32)
    invf = np.broadcast_to((inv_freq.astype(np.float64) / (2.0 * np.pi)).astype(np.float32)[None, :], (128, 64))
    cmask = (t[None, :] >= t[:, None]).astype(np.float32)
    sel = np.zeros((2, 256), np.float32)
    sel[0, 0:128] = 1.0
    sel[1, 128:256] = 1.0
    return dict(gkinv=np.ascontiguousarray(gkinv), kdec=kdec, epsp=epsp, invf=np.ascontiguousarray(invf),
                cmask=cmask, sel=sel, ident=np.eye(128, dtype=np.float32))


def _pack(g_pre_t, lng_t, gng_t, kdec, epsp, invf, cT, x0T, pos):
    cpk = np.zeros((128, 160), np.float32)
    cpk[:, 0:8], cpk[:, 8:12], cpk[:, 12:16], cpk[:, 16:20], cpk[:, 20:24] = g_pre_t, lng_t, gng_t, kdec, epsp
    cpk[:, 24:88], cpk[:, 88:104], cpk[:, 104:120] = invf, cT, x0T
    cpk[:, 120:152] = np.ascontiguousarray(pos.astype(np.int32)).view(np.float32)
    return cpk


_NC_CACHE = {}


def kernel(x, c, positions, w_ada, b_ada, g_pre, w_in, gmlp_ln_g, gmlp_ws, gmlp_bs, ret_gn_g, w_out, g_post):
    f = lambda a: np.ascontiguousarray(np.asarray(a))
    x, c, positions = f(x), f(c), f(positions)
    w_ada, b_ada, g_pre, w_in = f(w_ada)[0], f(b_ada)[0], f(g_pre)[0], f(w_in)[0]
    lng, ws, bs, gng, w_out, g_post = f(gmlp_ln_g)[0], f(gmlp_ws)[0], f(gmlp_bs)[0], f(ret_gn_g)[0], f(w_out)[0], f(g_post)[0]
    if "nc" not in _NC_CACHE:
        _NC_CACHE["nc"] = build_program()
    nc = _NC_CACHE["nc"]
    cst = _consts()
    g_pre_t, lng_t, gng_t = f(g_pre.reshape(8, 128).T), f(lng.reshape(4, 128).T), f(gng.reshape(4, 128).T)
    kdec_c, epsp_c, invf_c = cst.pop("kdec"), cst.pop("epsp"), cst.pop("invf")
    shared = dict(
        w_ada=w_ada, b_ada=f(b_ada.reshape(1, -1)), w_in=w_in,
        lng_r2=f(np.broadcast_to(lng.reshape(1, 512), (2, 512))), wsT=f(ws.transpose(2, 0, 1).reshape(128, 512)), bs=f(bs.reshape(1, 512)),
        w_out=w_out, g_post_r=f(np.broadcast_to(g_post[None, :], (128, D))), **cst)
    in_maps = []
    for i in range(NCORES):
        bsl = slice(i * NB, (i + 1) * NB)
        xm = f(x[bsl].reshape(G * 128, D))
        cT = f(c[bsl].reshape(NB, 8, 128).transpose(2, 1, 0).reshape(128, 16))
        pos = f(positions[bsl].reshape(NB, NCH, 128).transpose(2, 0, 1).reshape(128, NB * NCH).astype(np.int32))
        x0T = f(x[bsl, 0, :].reshape(NB, 8, 128).transpose(2, 1, 0).reshape(128, 8 * NB))
        cpk = _pack(g_pre_t, lng_t, gng_t, kdec_c, epsp_c, invf_c, cT, x0T, pos)
        in_maps.append(dict(x=xm, cpk=cpk, **shared))
    res = run_bass_kernel_spmd(nc, in_maps, core_ids=list(range(NCORES)))
    out = np.concatenate([np.asarray(r["out"]).reshape(NB, SEQ, D) for r in res.results], axis=0)
    return out.astype(np.float32, copy=False)
```

```python
import numpy as np
from contextlib import ExitStack
from collections import defaultdict
import concourse.bass as bass
import concourse.mybir as mybir
from concourse.bass_utils import run_bass_kernel_spmd

F32 = mybir.dt.float32
BF16 = mybir.dt.bfloat16
I32 = mybir.dt.int32
AF = mybir.ActivationFunctionType
ALU = mybir.AluOpType

NCORES = 8
BATCH, SEQ, D = 16, 2048, 1024
NB = BATCH // NCORES
NCH = SEQ // 128
G = NB * NCH
PW = 3584
O_GU, O_GV, O_GG, O_RQ, O_RK, O_RV, O_RG = 0, 512, 1024, 1536, 2048, 2560, 3072
EPS = 1e-6
MTC = 2
TM = MTC * 128
NMT = G // MTC
NX = 6
TWO_PI_SAFE = 6.283185
NEAR = 3


class Op:
    __slots__ = ("eng", "fn", "reads", "writes", "dma", "idx", "eidx", "deps", "signal", "tick", "dma_ord")


class Sched:
    def __init__(self):
        self.ops = []
        self.dma_count = defaultdict(int)

    def add(self, eng, fn, r=(), w=(), dma=None):
        op = Op()
        op.eng, op.fn, op.reads, op.writes, op.dma = eng, fn, tuple(r), tuple(w), dma
        op.idx = len(self.ops)
        op.deps = []
        op.signal = False
        op.tick = 0
        op.dma_ord = 0
        self.ops.append(op)
        return op

    def analyze(self):
        last_w = {}
        readers = {}
        ecount = defaultdict(int)
        dma_seen = defaultdict(int)
        for op in self.ops:
            op.eidx = ecount[op.eng]
            ecount[op.eng] += 1
            deps = {}
            for k in op.reads:
                if k in last_w:
                    deps[last_w[k]] = "raw"
            for k in op.writes:
                if k in last_w:
                    deps.setdefault(last_w[k], "waw")
                for rr in readers.get(k, ()):
                    deps.setdefault(rr, "war")
            for k in op.reads:
                readers.setdefault(k, []).append(op.idx)
            for k in op.writes:
                last_w[k] = op.idx
                readers[k] = []
            need = {}
            for pi, kind in deps.items():
                if pi == op.idx:
                    continue
                p = self.ops[pi]
                if p.dma is None and op.dma is None and p.eng == op.eng:
                    if op.eng == "pe":
                        continue
                    if (op.eidx - p.eidx) > NEAR:
                        continue
                if p.dma is not None:
                    key = ("dma", p.dma)
                    need[key] = max(need.get(key, 0), 16 * dma_seen[p.dma])
                else:
                    p.signal = True
                    key = ("eng", p.eng)
                    prev = need.get(key)
                    if prev is None or self.ops[prev].idx < p.idx:
                        need[key] = p.idx
            op.deps = need
            if op.dma is not None:
                dma_seen[op.dma] += 1
                op.dma_ord = dma_seen[op.dma]
        tick = defaultdict(int)
        for op in self.ops:
            if op.dma is None and op.signal:
                tick[op.eng] += 1
                op.tick = tick[op.eng]
        self.dma_total = dict(dma_seen)

    def emit_engine(self, eng_name, eng, eng_sems, dma_sems):
        waited = {}
        for op in self.ops:
            if op.eng != eng_name:
                continue
            for key, val in op.deps.items():
                if key[0] == "dma":
                    sem, v = dma_sems[key[1]], val
                else:
                    sem, v = eng_sems[key[1]], self.ops[val].tick
                if waited.get(key, 0) >= v:
                    continue
                eng.wait_ge(sem, v)
                waited[key] = v
            ins = op.fn(eng)
            if op.dma is not None:
                ins.then_inc(dma_sems[op.dma], 16)
            elif op.signal:
                ins.then_inc(eng_sems[op.eng], 1)


class PsPool:
    def __init__(self, banks):
        self.banks = banks
        self.i = 0

    def next(self):
        b = self.banks[self.i % len(self.banks)]
        self.i += 1
        return b


def build_program():
    nc = bass.Bass("TRN2", target_bir_lowering=False)
    dt_in = lambda name, shape, dt=F32: nc.dram_tensor(name, list(shape), dt, kind="ExternalInput").ap()
    x_d = dt_in("x", [G * 128, D])
    wada_d = dt_in("w_ada", [D, 3 * D])
    bada_d = dt_in("b_ada", [1, 3 * D])
    win_d = dt_in("w_in", [D, PW])
    wsT_d = dt_in("wsT", [128, 512])
    bs_d = dt_in("bs", [1, 512])
    lngr_d = dt_in("lng_r2", [2, 512])
    wout_d = dt_in("w_out", [D, D])
    gpost_d = dt_in("g_post_r", [128, D])
    ident_d = dt_in("ident", [128, 128])
    cmask_d = dt_in("cmask", [128, 128])
    gkinv_d = dt_in("gkinv", [128, 512])
    sel_d = dt_in("sel", [2, 256])
    cpk_d = dt_in("cpk", [128, 160])
    out_d = nc.dram_tensor("out", [G * 128, D], F32, kind="ExternalOutput").ap()

    gC = [float((1.0 - 2.0 ** (-5.0 - h)) ** 128) for h in range(4)]

    es = ExitStack()
    with es:
        sb = lambda name, shape, dt=F32: es.enter_context(nc.sbuf_tensor(name, list(shape), dt))
        w_in = sb("w_in_sb", [128, 8, PW], BF16)
        w_out = sb("w_out_sb", [128, 8, D], BF16)
        xbuf = sb("xbuf", [128, NX * D], F32)
        xbuf_bf = xbuf[:].bitcast(BF16)
        xsl = [xbuf[:, i * D:(i + 1) * D] for i in range(NX)]
        wa_stage = [xbuf_bf[:, j * 3072:(j + 1) * 3072] for j in range(4)]
        wa_xkeys = [[("x", 0), ("x", 1)], [("x", 1), ("x", 2)], [("x", 3), ("x", 4)], [("x", 4), ("x", 5)]]
        xsb = [sb(f"xsb{i}", [128, D], BF16) for i in range(2)]
        hT = [sb(f"hT{i}", [128, 8, TM], BF16) for i in range(2)]
        ug = [sb(f"ug{i}", [128, 4, TM], F32) for i in range(2)]
        srg = [sb(f"srg{i}", [128, 4, TM], F32) for i in range(2)]
        sgg = [sb(f"sgg{i}", [128, TM], F32) for i in range(2)]
        srgt = [sb(f"srgt{i}", [128, TM], F32) for i in range(2)]
        vln = [sb(f"vln{i}", [128, 512], BF16) for i in range(2)]
        qrot = [sb(f"qrot{i}", [128, 512], BF16) for i in range(2)]
        krot = [sb(f"krot{i}", [128, 512], BF16) for i in range(2)]
        At = [sb(f"At{i}", [128, 512], F32) for i in range(2)]
        Bt = [sb(f"Bt{i}", [128, 512], F32) for i in range(2)]
        ktok = [sb(f"ktok{i}", [128, 512], BF16) for i in range(2)]
        vbf = [sb(f"vbf{i}", [128, 512], BF16) for i in range(3)]
        qT = [sb(f"qT{i}", [128, 512], BF16) for i in range(2)]
        kT = [sb(f"kT{i}", [128, 512], BF16) for i in range(2)]
        msT = [sb(f"msT{i}", [128, 512], BF16) for i in range(2)]
        Sst = [sb(f"Sst{i}", [128, 512], F32) for i in range(NB)]
        Sbf = [sb(f"Sbf{i}", [128, 512], BF16) for i in range(2)]
        onb = [sb(f"onb{i}", [128, 512], BF16) for i in range(2)]
        yT = [sb(f"yT{i}", [128, 8, 128], BF16) for i in range(3)]
        scr12 = sb("scr12", [128, 3072], F32)
        tmpz = [scr12[:, 0:1024], scr12[:, 1024:2048]]
        junkA = scr12[:, 2048:3072]
        mod_sb = scr12[0:2, :]
        SCRK = [("tmp", 0), ("tmp", 1), "junkA"]
        cosT = sb("cosT", [128, NCH, 64], F32)
        sinT = sb("sinT", [128, NCH, 64], F32)
        gp = [sb(f"gp{i}", [128, D], F32) for i in range(NB)]
        gkinv = sb("gkinv_sb", [128, 512], F32)
        wsTb = sb("wsTb", [128, 512], BF16)
        identf = sb("identf", [128, 128], F32)
        identb = sb("identb", [128, 128], BF16)
        cmask = sb("cmask_sb", [128, 128], F32)
        cpk = sb("cpk_sb", [128, 160], F32)
        gpre, lng, gng, kdec, epsp = cpk[:, 0:8], cpk[:, 8:12], cpk[:, 12:16], cpk[:, 16:20], cpk[:, 20:24]
        invf, cTf = cpk[:, 24:88], cpk[:, 88:104]
        x0T = cpk[:, 104:120].rearrange("p (k b) -> p k b", b=NB)
        pos_i = cpk[:, 120:152].bitcast(I32)
        CPK_KEYS = ["gpre", "lng", "gng", "kdec", "epsp", "invf", "cTf", "x0T", "pos_i"]
        sel = sb("sel_sb", [2, 256], F32)
        cTb = sb("cTb", [128, 16], BF16)
        posf = sb("posf", [128, NCH], F32)
        gsT = sb("gsT", [128, 8, NB], F32)
        shT = sb("shT", [128, 8, NB], F32)
        neghalf = sb("neghalf", [128, 8], F32)
        onesb = sb("onesb", [2, 128], BF16)
        b2 = sb("b2", [2, 512], BF16)
        ginv2 = sb("ginv2", [2, 512], BF16)
        ssq = sb("ssq", [128, NX], F32)
        rstd = sb("rstd", [128, NX], F32)
        rtmp = sb("rtmp", [128, NX], F32)
        lnst = [sb(f"lnst{i}", [128, 4, 6], F32) for i in range(2)]
        lnmv = [sb(f"lnmv{i}", [128, 4, 2], F32) for i in range(2)]
        lnr = [sb(f"lnr{i}", [128, 4], F32) for i in range(2)]
        lnb = [sb(f"lnb{i}", [128, 4], F32) for i in range(2)]
        gnst = [sb(f"gnst{i}", [128, 4, 6], F32) for i in range(2)]
        gnmv = [sb(f"gnmv{i}", [128, 4, 2], F32) for i in range(2)]
        gnr = [sb(f"gnr{i}", [128, 4], F32) for i in range(2)]
        gnb = [sb(f"gnb{i}", [128, 4], F32) for i in range(2)]
        zss = [sb(f"zss{i}", [128, 2], F32) for i in range(2)]
        zr = [sb(f"zr{i}", [128, 2], F32) for i in range(2)]
        sqt = sb("sqt", [128, 8, NB], F32)
        sqs = sb("sqs", [128, NB], F32)
        ones_f = sb("ones_f", [128, 1], F32)
        Lq = sb("Lq", [128, 8, NB, 2], F32)
        r0t = sb("r0t", [1, NB], F32)
        rstd0 = sb("rstd0", [1, NB], F32)
        pos0f = sb("pos0f", [1, NB], F32)
        f0 = sb("f0", [1, NB * 64], F32)
        f0i = sb("f0i", [1, NB * 64], I32)
        f0b = sb("f0b", [1, NB * 64], F32)
        cos0 = sb("cos0", [1, NB * 64], F32)
        sin0 = sb("sin0", [1, NB * 64], F32)
        s4 = sb("s4", [1, 4], F32)
        s0p = [sb(f"s0p{i}", [1, 4], F32) for i in range(NB)]
        gpost = hT[0][:].rearrange("p k t -> p (k t)").bitcast(F32)
        GPOSTK = [("hT", 0, c) for c in range(MTC)]
        wsTf = hT[1][:].rearrange("p k t -> p (k t)").bitcast(F32)[:, 0:512]
        WSTFK = [("hT", 1, c) for c in range(MTC)]
        bs_f = At[0][0:1, :]
        hi_f = Bt[0][0:1, :]
        hi_b = At[1][:].bitcast(BF16)[0:1, 0:512]
        lo_b = Bt[1][:].bitcast(BF16)[0:1, 0:512]
        gi_f = Sst[1][0:2, :]
        ps = [es.enter_context(nc.psum_tensor(f"ps{i}", [128, 512], F32)) for i in range(8)]
        psb = [p[:].bitcast(BF16) for p in ps]
        big = PsPool([0, 1, 2, 3, 4, 5, 6, 7])
        trp = big
        mix = big
        PK = lambda b: ("ps", b)

        S = Sched()
        add = S.add

        def dma(q, out, in_, sem, r=(), w=()):
            add(q, lambda e, out=out, in_=in_: e.dma_start(out=out, in_=in_), r=r, w=w, dma=sem)

        dma("sp", cpk[:], cpk_d[:, :], "c0", w=CPK_KEYS)
        dma("sp", identf[:], ident_d[:, :], "c0", w=["identf"])
        dma("sp", sel[:], sel_d[:, :], "c0", w=["sel"])
        dma("sp", gpost, gpost_d[:, :], "c1", w=GPOSTK)
        dma("sp", wsTf, wsT_d[:, :], "c1", w=WSTFK)
        dma("sp", cmask[:], cmask_d[:, :], "c1", w=["cmask"])
        dma("sp", bs_f, bs_d[:, :], "c1", w=[("A", 0)])
        dma("sp", gkinv[:], gkinv_d[:, :], "c1", w=["gkinv"])

        def ada_load(k):
            j = k % 4
            if k < 8:
                dma("pool", wa_stage[j], wada_d[k * 128:(k + 1) * 128, :], f"wa{j}", w=wa_xkeys[j])
            else:
                dma("pool", wa_stage[j][0:1, :], bada_d[:, :], f"wa{j}", w=wa_xkeys[j])

        for k in range(4):
            ada_load(k)

        add("dve", lambda e: e.memset(neghalf[:], -0.5), w=["neghalf"])
        add("dve", lambda e: e.memset(onesb[:], 1.0), w=["onesb"])
        add("dve", lambda e: e.tensor_copy(out=identb[:], in_=identf[:]), r=["identf"], w=["identb"])
        add("act", lambda e: e.activation(out=cTb[:], in_=cTf[:], func=AF.Silu), r=["cTf"], w=["cTb"])

        cTb3 = cTb[:].rearrange("p (k b) -> p k b", b=NB)
        for k in range(9):
            j = k % 4

            def f(e, k=k, j=j):
                ins = None
                for n in range(6):
                    if k < 8:
                        lhsT = cTb3[:, k, :]
                        rhs = wa_stage[j][:, n * 512:(n + 1) * 512]
                    else:
                        lhsT = onesb[0:1, 0:2]
                        rhs = wa_stage[j][0:1, n * 512:(n + 1) * 512]
                    ins = e.matmul(ps[n][0:2, :], lhsT=lhsT, rhs=rhs, start=(k == 0), stop=(k == 8))
                return ins

            add("pe", f, r=wa_xkeys[j] + ["cTb", "onesb"], w=[PK(n) for n in range(6)])
            if k + 4 < 9:
                ada_load(k + 4)

        def wload(dst, src, c0, width, sem, key):
            for kk in range(2):
                dma("pool", dst[:, kk * 4:(kk + 1) * 4, c0:c0 + width],
                    src[kk * 512:(kk + 1) * 512, c0:c0 + width].rearrange("(k p) c -> p k c", p=128), sem, w=[key])

        early_buf = {0: (ug[0][:].rearrange("p j t -> p (j t)"), ("ug", 0)),
                     1: (ug[1][:].rearrange("p j t -> p (j t)"), ("ug", 1)),
                     2: (srg[0][:].rearrange("p j t -> p (j t)"), ("srg", 0))}

        def early_front(g, head=True, tail=True):
            xs_, xkey = early_buf[g]
            sl = g % NX
            if head:
                dma("sp", xs_, x_d[g * 128:(g + 1) * 128, :], f"xe{g}", w=[xkey])
                add("act", lambda e: e.activation(out=junkA, in_=xs_, func=AF.Square, accum_out=ssq[:, sl:sl + 1]),
                    r=[xkey], w=["junkA", ("ssq", sl)])
                add("dve", lambda e: e.tensor_scalar(out=rtmp[:, sl:sl + 1], in0=ssq[:, sl:sl + 1], scalar1=1.0 / D, scalar2=EPS,
                                                     op0=ALU.mult, op1=ALU.add), r=[("ssq", sl)], w=[("rtmp", sl)])
                add("pool", lambda e: e.tensor_tensor(out=rstd[:, sl:sl + 1], in0=rtmp[:, sl:sl + 1], in1=neghalf[:, 0:1], op=ALU.pow),
                    r=[("rtmp", sl), "neghalf"], w=[("rstd", sl)])
            if tail:
                xb = xsb[g % 2]
                add("act", lambda e: e.activation(out=xb[:], in_=xs_, func=AF.Copy, scale=rstd[:, sl:sl + 1]),
                    r=[xkey, ("rstd", sl)], w=[("xsb", g % 2)])

        early_front(0)
        early_front(1)
        early_front(2, tail=False)

        for (c0, nm) in ((O_GV, "gv"), (O_RQ, "rq"), (O_RK, "rk"), (O_RV, "rv"), (O_GG, "gg"), (O_GU, "gu"), (O_RG, "rg")):
            wload(w_in, win_d, c0, 512, "w_" + nm, ("win", c0))

        for n in range(6):
            add("dve" if n % 2 == 0 else "act",
                (lambda e, n=n: e.tensor_copy(out=mod_sb[:, n * 512:(n + 1) * 512], in_=ps[n][0:2, :])) if n % 2 == 0 else
                (lambda e, n=n: e.activation(out=mod_sb[:, n * 512:(n + 1) * 512], in_=ps[n][0:2, :], func=AF.Copy)),
                w=[PK(n)] + SCRK)
        def f(e):
            ins = None
            for j in range(16):
                ins = e.transpose(ps[6][:, j * 2:(j + 1) * 2], mod_sb[:, j * 128:(j + 1) * 128], identf[0:2, 0:2])
            return ins
        add("pe", f, r=SCRK + ["identf"], w=[PK(6)])
        ps6v = ps[6][:, 0:32].rearrange("p (j b) -> p j b", b=NB)
        add("dve", lambda e: e.tensor_copy(out=shT[:], in_=ps6v[:, 0:8, :]), w=[PK(6), "shT"])
        add("dve", lambda e: e.scalar_tensor_tensor(out=gsT[:], in0=ps6v[:, 8:16, :], scalar=1.0,
                                                    in1=gpre[:].unsqueeze(2).to_broadcast([128, 8, NB]),
                                                    op0=ALU.add, op1=ALU.mult), r=["gpre"], w=[PK(6), "gsT"])
        for b in range(NB):
            for half in range(2):
                bk = 7 if (b * 2 + half) % 2 == 0 else 6
                add("pe", lambda e, b=b, half=half, bk=bk: e.matmul(
                    ps[bk][:, :], lhsT=sel[0:2, b * 128:(b + 1) * 128],
                    rhs=mod_sb[:, 2048 + half * 512:2048 + (half + 1) * 512], start=True, stop=True),
                    r=SCRK + ["sel"], w=[PK(bk)])
                add("dve", lambda e, b=b, half=half, bk=bk: e.tensor_tensor(
                    out=gp[b][:, half * 512:(half + 1) * 512], in0=ps[bk][:, :],
                    in1=gpost[:, half * 512:(half + 1) * 512], op=ALU.mult),
                    r=GPOSTK, w=[PK(bk), ("gp", b)])
        add("dve", lambda e: e.tensor_tensor(out=wsTb[:].rearrange("p (g t) -> p g t", g=4),
                                             in0=wsTf.rearrange("p (g t) -> p g t", g=4),
                                             in1=cmask[:].unsqueeze(1).to_broadcast([128, 4, 128]), op=ALU.mult),
            r=WSTFK + ["cmask"], w=["wsTb"])
        add("dve", lambda e: e.tensor_copy(out=hi_b, in_=bs_f), r=[("A", 0)], w=[("A", 1)])
        add("dve", lambda e: e.tensor_copy(out=hi_f, in_=hi_b), r=[("A", 1)], w=[("B", 0)])
        add("dve", lambda e: e.tensor_tensor(out=lo_b, in0=bs_f, in1=hi_f, op=ALU.subtract), r=[("A", 0), ("B", 0)], w=[("B", 1)])
        dma("sp", b2[0:1, :], hi_b, "c2", r=[("A", 1)], w=["b2"])
        dma("sp", b2[1:2, :], lo_b, "c2", r=[("B", 1)], w=["b2"])

        dma("sp", gi_f, lngr_d[:, :], "c2", w=[("S", 1)])
        add("dve", lambda e: e.reciprocal(out=gi_f, in_=gi_f), r=[("S", 1)], w=[("S", 1)])
        add("dve", lambda e: e.tensor_copy(out=ginv2[:], in_=gi_f), r=[("S", 1)], w=["ginv2"])
        def precise_mm():
            ug0v = tmpz[0]
            ug1v = tmpz[1]
            sr0v = xsl[3]
            sr1v = tmpz[1]
            add("dve", lambda e: e.memset(ones_f[:], 1.0), w=["ones_f"])
            add("dve", lambda e: e.tensor_tensor(out=sqt[:], in0=x0T[:], in1=x0T[:], op=ALU.mult), r=["x0T"], w=["sqt"])
            add("dve", lambda e: e.reduce_sum(out=sqs[:], in_=sqt[:].rearrange("p k b -> p b k"), axis=mybir.AxisListType.X),
                r=["sqt"], w=["sqs"])
            add("pe", lambda e: e.matmul(ps[7][0:1, 0:NB], lhsT=ones_f[:, 0:1], rhs=sqs[:, :], start=True, stop=True),
                r=["ones_f", "sqs"], w=[PK(7)])
            add("dve", lambda e: e.tensor_scalar(out=r0t[:], in0=ps[7][0:1, 0:NB], scalar1=1.0 / D, scalar2=EPS, op0=ALU.mult, op1=ALU.add),
                w=[PK(7), "r0t"])
            add("pool", lambda e: e.tensor_tensor(out=rstd0[:], in0=r0t[:], in1=neghalf[0:1, 0:NB], op=ALU.pow), r=["r0t", "neghalf"], w=["rstd0"])
            add("dve", lambda e: e.tensor_tensor(out=Lq[:, :, :, 0], in0=x0T[:], in1=gsT[:], op=ALU.mult), r=["x0T", "gsT"], w=["Lq0"])
            add("dve", lambda e: e.tensor_copy(out=Lq[:, :, :, 1], in_=shT[:]), r=["shT"], w=["Lq1"])
            Lq2 = Lq[:].rearrange("p k b t -> p k (b t)")
            for k in range(8):
                slot = 3 + k % 3
                dma("sp", xsl[slot], win_d[k * 128:(k + 1) * 128, O_RQ:O_RQ + 1024], f"wq{k % 3}", w=[("x", slot)])

                def f(e, k=k, slot=slot):
                    ins = None
                    for half in range(2):
                        ins = e.matmul(ps[5 + half][0:2 * NB, :], lhsT=Lq2[:, k, :], rhs=xsl[slot][:, half * 512:(half + 1) * 512],
                                       start=(k == 0), stop=(k == 7))
                    return ins
                add("pe", f, r=[("x", slot), "Lq0", "Lq1"], w=[PK(5), PK(6)])
            add("dve", lambda e: e.tensor_copy(out=ug0v[0:2 * NB, 0:512], in_=ps[5][0:2 * NB, :]), w=[PK(5), ("tmp", 0)])
            add("act", lambda e: e.activation(out=ug0v[0:2 * NB, 512:1024], in_=ps[6][0:2 * NB, :], func=AF.Copy), w=[PK(6), ("tmp", 0)])

        def precise_el():
            ug0v = tmpz[0]
            ug1v = tmpz[1]
            sr0v = xsl[3]
            sr1v = tmpz[1]
            dma("sp", ug1v[0:1, :], ug0v[1:2, :], "c3", r=[("tmp", 0)], w=[("tmp", 1)])
            dma("sp", sr0v[0:1, :], ug0v[2:3, :], "c3", r=[("tmp", 0)], w=[("x", 3)])
            add("dve", lambda e: e.tensor_copy(out=pos0f[:], in_=pos_i[0:1, ::NCH]), r=["pos_i"], w=["pos0f"])
            f0_3 = f0[:].rearrange("p (b d) -> p b d", b=NB)
            add("dve", lambda e: e.tensor_tensor(out=f0_3, in0=pos0f[:].unsqueeze(2).to_broadcast([1, NB, 64]),
                                                 in1=invf[0:1, :].unsqueeze(1).to_broadcast([1, NB, 64]), op=ALU.mult),
                r=["pos0f", "invf"], w=["f0"])
            add("dve", lambda e: e.tensor_copy(out=f0i[:], in_=f0[:]), r=["f0"], w=["f0i"])
            add("dve", lambda e: e.tensor_copy(out=f0b[:], in_=f0i[:]), r=["f0i"], w=["f0b"])
            add("dve", lambda e: e.tensor_tensor(out=f0[:], in0=f0[:], in1=f0b[:], op=ALU.subtract), r=["f0", "f0b"], w=["f0"])
            add("act", lambda e: e.activation(out=sin0[:], in_=f0[:], func=AF.Sin, scale=TWO_PI_SAFE), r=["f0"], w=["sin0"])
            add("dve", lambda e: e.tensor_scalar(out=f0b[:], in0=f0[:], scalar1=0.25, scalar2=None, op0=ALU.add), r=["f0"], w=["f0b"])
            add("dve", lambda e: e.tensor_copy(out=f0i[:], in_=f0b[:]), r=["f0b"], w=["f0i"])
            add("dve", lambda e: e.tensor_copy(out=f0[:], in_=f0i[:]), r=["f0i", "sin0"], w=["f0"])
            add("dve", lambda e: e.tensor_tensor(out=f0b[:], in0=f0b[:], in1=f0[:], op=ALU.subtract), r=["f0", "f0b"], w=["f0b"])
            add("act", lambda e: e.activation(out=cos0[:], in_=f0b[:], func=AF.Sin, scale=TWO_PI_SAFE), r=["f0b"], w=["cos0"])
            for b in range(NB):
                if b == 1:
                    dma("sp", sr1v[0:1, :], ug0v[3:4, :], "c3", r=[("tmp", 0)], w=[("tmp", 1)])
                U = (ug0v if b == 0 else sr0v)[0:1, :]
                Wb = (ug1v if b == 0 else sr1v)[0:1, :]
                ku = ("tmp", 0) if b == 0 else ("x", 3)
                kw = ("tmp", 1) if b == 0 else ("tmp", 1)
                Aq = xsl[4][0:1, :]
                Bq = xsl[5][0:1, :]
                U4 = U.rearrange("p (h two d) -> p h two d", h=8, two=2)
                A4 = Aq.rearrange("p (h two d) -> p h two d", h=8, two=2)
                B4 = Bq.rearrange("p (h two d) -> p h two d", h=8, two=2)
                cb = cos0[0:1, b * 64:(b + 1) * 64].unsqueeze(1).unsqueeze(1).to_broadcast([1, 8, 2, 64])
                sbc = sin0[0:1, b * 64:(b + 1) * 64].unsqueeze(1).unsqueeze(1).to_broadcast([1, 8, 2, 64])
                add("dve", lambda e, U=U, Wb=Wb, b=b: e.scalar_tensor_tensor(out=U, in0=U, scalar=rstd0[0:1, b:b + 1], in1=Wb,
                                                                             op0=ALU.mult, op1=ALU.add),
                    r=[ku, kw, "rstd0"], w=[ku])
                add("dve", lambda e, U4=U4, A4=A4, cb=cb: e.tensor_tensor(out=A4, in0=U4, in1=cb, op=ALU.mult), r=[ku, "cos0"], w=[("x", 4)])
                add("dve", lambda e, U4=U4, B4=B4, sbc=sbc: e.tensor_tensor(out=B4, in0=U4[:, :, ::-1, :], in1=sbc, op=ALU.mult),
                    r=[ku, "sin0"], w=[("x", 5)])
                add("pool", lambda e, U4=U4, A4=A4, B4=B4: e.tensor_tensor(out=U4[:, :, 0, :], in0=A4[:, :, 0, :], in1=B4[:, :, 0, :], op=ALU.subtract),
                    r=[("x", 4), ("x", 5)], w=[ku])
                add("pool", lambda e, U4=U4, A4=A4, B4=B4: e.tensor_tensor(out=U4[:, :, 1, :], in0=A4[:, :, 1, :], in1=B4[:, :, 1, :], op=ALU.add),
                    r=[("x", 4), ("x", 5)], w=[ku])
                add("dve", lambda e, U=U, Aq=Aq: e.tensor_tensor(out=Aq[:, 0:512], in0=U[:, 0:512], in1=U[:, 512:1024], op=ALU.mult),
                    r=[ku], w=[("x", 4)])
                add("dve", lambda e, Aq=Aq: e.reduce_sum(out=s4[:], in_=Aq[:, 0:512].rearrange("p (h d) -> p h d", h=4), axis=mybir.AxisListType.X),
                    r=[("x", 4)], w=["s4"])
                add("dve", lambda e, b=b: e.tensor_tensor(out=s0p[b][:], in0=s4[:], in1=gkinv[0:1, ::128], op=ALU.mult),
                    r=["s4", "gkinv"], w=[("s0p", b)])

        T1, T2, T3 = tmpz[0], tmpz[1], junkA
        T2i = T2.bitcast(I32)
        cos2 = cosT[:].rearrange("p n d -> p (n d)")
        sin2 = sinT[:].rearrange("p n d -> p (n d)")
        T1_3 = T1.rearrange("p (n d) -> p n d", d=64)

        def tables(b):
            add("dve", lambda e: e.tensor_copy(out=posf[:], in_=pos_i[:, b * NCH:(b + 1) * NCH]), r=["pos_i"], w=["posf"])
            add("dve", lambda e: e.tensor_tensor(out=T1_3, in0=posf[:].unsqueeze(2).to_broadcast([128, NCH, 64]),
                                                 in1=invf[:].unsqueeze(1).to_broadcast([128, NCH, 64]), op=ALU.mult),
                r=["posf", "invf"], w=[("tmp", 0)])
            add("dve", lambda e: e.tensor_copy(out=T2i, in_=T1), r=[("tmp", 0)], w=[("tmp", 1)])
            add("dve", lambda e: e.tensor_copy(out=T3, in_=T2i), r=[("tmp", 1)], w=["junkA"])
            add("dve", lambda e: e.tensor_tensor(out=T1, in0=T1, in1=T3, op=ALU.subtract), r=[("tmp", 0), "junkA"], w=[("tmp", 0)])
            add("act", lambda e: e.activation(out=sin2, in_=T1, func=AF.Sin, scale=TWO_PI_SAFE), r=[("tmp", 0)], w=["sinT"])
            add("dve", lambda e: e.tensor_scalar(out=T3, in0=T1, scalar1=0.25, scalar2=None, op0=ALU.add), r=[("tmp", 0)], w=["junkA"])
            add("dve", lambda e: e.tensor_copy(out=T2i, in_=T3), r=["junkA"], w=[("tmp", 1)])
            add("dve", lambda e: e.tensor_copy(out=T1, in_=T2i), r=[("tmp", 1)], w=[("tmp", 0)])
            add("dve", lambda e: e.tensor_tensor(out=T3, in0=T3, in1=T1, op=ALU.subtract), r=[("tmp", 0), "junkA"], w=["junkA"])
            add("act", lambda e: e.activation(out=cos2, in_=T3, func=AF.Sin, scale=TWO_PI_SAFE), r=["junkA"], w=["cosT"])

        tables(0)

        def front0(g):
            xl = g % 3
            dma("sp", xsl[xl], x_d[g * 128:(g + 1) * 128, :], f"x{xl}", w=[("x", xl)])

        def reload(g):
            xl = 3 + g % 3
            dma("sp", xsl[xl], x_d[g * 128:(g + 1) * 128, :], f"x{xl}", w=[("x", xl)])

        def front1(g):
            b, m, c, sl = g // NCH, g // MTC, g % MTC, g % NX
            xl = g % 3
            xs_ = xsl[xl]
            add("act", lambda e: e.activation(out=junkA, in_=xs_, func=AF.Square, accum_out=ssq[:, sl:sl + 1]),
                r=[("x", xl)], w=["junkA", ("ssq", sl)])
            add("dve", lambda e: e.tensor_scalar(out=rtmp[:, sl:sl + 1], in0=ssq[:, sl:sl + 1], scalar1=1.0 / D, scalar2=EPS,
                                                 op0=ALU.mult, op1=ALU.add), r=[("ssq", sl)], w=[("rtmp", sl)])
            add("pool", lambda e: e.tensor_tensor(out=rstd[:, sl:sl + 1], in0=rtmp[:, sl:sl + 1], in1=neghalf[:, 0:1], op=ALU.pow),
                r=[("rtmp", sl), "neghalf"], w=[("rstd", sl)])
            xb = xsb[g % 2]
            add("act", lambda e: e.activation(out=xb[:], in_=xs_, func=AF.Copy, scale=rstd[:, sl:sl + 1]),
                r=[("x", xl), ("rstd", sl)], w=[("xsb", g % 2)])

        def front2(g):
            b, m, c, sl = g // NCH, g // MTC, g % MTC, g % NX
            xb = xsb[g % 2]
            bk = trp.next()

            def f(e):
                ins = None
                for k in range(8):
                    ins = e.transpose(psb[bk][:, k * 128:(k + 1) * 128], xb[:, k * 128:(k + 1) * 128], identb[:])
                return ins
            add("pe", f, r=[("xsb", g % 2), "identb"], w=[PK(bk)])
            hdst = hT[m % 2]

            def fa(e):
                ins = None
                for k in range(0, 4):
                    ins = e.activation(out=hdst[:, k, c * 128:(c + 1) * 128], in_=psb[bk][:, k * 128:(k + 1) * 128],
                                       func=AF.Identity, scale=gsT[:, k, b:b + 1], bias=shT[:, k, b:b + 1])
                return ins

            def fd(e):
                ins = None
                for k in range(4, 8):
                    ins = e.tensor_scalar(out=hdst[:, k, c * 128:(c + 1) * 128], in0=psb[bk][:, k * 128:(k + 1) * 128],
                                          scalar1=gsT[:, k, b:b + 1], scalar2=shT[:, k, b:b + 1], op0=ALU.mult, op1=ALU.add)
                return ins
            add("act", fa, r=["gsT", "shT"], w=[PK(bk), ("hT", m % 2, c)])
            add("dve", fd, r=["gsT", "shT"], w=[PK(bk), ("hT", m % 2, c)])

        def projB(m):
            hsrc = hT[m % 2]
            hkeys = [("hT", m % 2, c) for c in range(MTC)]

            def proj(col0, wkey):
                bk = big.next()

                def f(e):
                    ins = None
                    for k in range(8):
                        ins = e.matmul(ps[bk][:, 0:TM], lhsT=w_in[:, k, col0:col0 + 128], rhs=hsrc[:, k, :],
                                       start=(k == 0), stop=(k == 7))
                    return ins
                add("pe", f, r=hkeys + [wkey], w=[PK(bk)])
                return bk

            for j in range(4):
                sg = sgg[j % 2]
                bk = proj(O_GG + j * 128, ("win", O_GG))
                add("act", lambda e, bk=bk, sg=sg: e.activation(out=sg[:], in_=ps[bk][:, 0:TM], func=AF.Silu),
                    w=[PK(bk), ("sgg", j % 2)])
                bk = proj(O_GU + j * 128, ("win", O_GU))
                add("dve", lambda e, bk=bk, sg=sg, j=j: e.scalar_tensor_tensor(
                    out=ug[m % 2][:, j, :], in0=ps[bk][:, 0:TM], scalar=lng[:, j:j + 1], in1=sg[:],
                    op0=ALU.mult, op1=ALU.mult), r=[("sgg", j % 2), "lng"], w=[PK(bk), ("ug", m % 2)])
            for j in range(4):
                st = srgt[j % 2]
                bk = proj(O_RG + j * 128, ("win", O_RG))
                add("act", lambda e, bk=bk, st=st: e.activation(out=st[:], in_=ps[bk][:, 0:TM], func=AF.Silu),
                    w=[PK(bk), ("srgt", j % 2)])
                add("act", lambda e, st=st, j=j: e.activation(out=srg[m % 2][:, j, :], in_=st[:], func=AF.Copy, scale=gng[:, j:j + 1]),
                    r=[("srgt", j % 2), "gng"], w=[("srg", m % 2)])

        def rotary(bk, n, dst, par, kname):
            A, Bm = At[par], Bt[par]
            p4 = ps[bk][:, :].rearrange("p (h two d) -> p h two d", h=4, two=2)
            A4 = A[:].rearrange("p (h two d) -> p h two d", h=4, two=2)
            B4 = Bm[:].rearrange("p (h two d) -> p h two d", h=4, two=2)
            d4 = dst[:].rearrange("p (h two d) -> p h two d", h=4, two=2)
            cb = cosT[:, n, :].unsqueeze(1).unsqueeze(1).to_broadcast([128, 4, 2, 64])
            sbc = sinT[:, n, :].unsqueeze(1).unsqueeze(1).to_broadcast([128, 4, 2, 64])
            add("dve", lambda e: e.tensor_tensor(out=A4, in0=p4, in1=cb, op=ALU.mult), r=["cosT"], w=[PK(bk), ("A", par)])
            add("dve", lambda e: e.tensor_tensor(out=B4, in0=p4[:, :, ::-1, :], in1=sbc, op=ALU.mult), r=["sinT"], w=[PK(bk), ("B", par)])
            add("pool", lambda e: e.tensor_tensor(out=d4[:, :, 0, :], in0=A4[:, :, 0, :], in1=B4[:, :, 0, :], op=ALU.subtract),
                r=[("A", par), ("B", par)], w=[kname])
            add("pool", lambda e: e.tensor_tensor(out=d4[:, :, 1, :], in0=A4[:, :, 1, :], in1=B4[:, :, 1, :], op=ALU.add),
                r=[("A", par), ("B", par)], w=[kname])

        def projA(g):
            b, n, m, c = g // NCH, g % NCH, g // MTC, g % MTC
            hsrc = hT[m % 2]
            hkey = ("hT", m % 2, c)

            def proj(col0):
                bk = big.next()

                def f(e):
                    ins = None
                    for k in range(8):
                        ins = e.matmul(ps[bk][:, :], lhsT=hsrc[:, k, c * 128:(c + 1) * 128], rhs=w_in[:, k, col0:col0 + 512],
                                       start=(k == 0), stop=(k == 7))
                    return ins
                add("pe", f, r=[hkey, ("win", col0)], w=[PK(bk)])
                return bk

            par = g % 2
            bk = proj(O_GV)
            st, mv, lr, lb = lnst[par], lnmv[par], lnr[par], lnb[par]

            def fs(e, bk=bk):
                ins = None
                for q in range(4):
                    ins = e.bn_stats(out=st[:, q, :], in_=ps[bk][:, q * 128:(q + 1) * 128])
                return ins
            add("dve", fs, w=[PK(bk), ("lnst", par)])

            def fg(e):
                ins = None
                for q in range(4):
                    ins = e.bn_aggr(out=mv[:, q, :], in_=st[:, q, :])
                return ins
            add("dve", fg, r=[("lnst", par)], w=[("lnmv", par)])
            add("dve", lambda e: e.tensor_scalar(out=lr[:], in0=mv[:, :, 1], scalar1=EPS, scalar2=None, op0=ALU.add),
                r=[("lnmv", par)], w=[("lnr", par)])
            add("pool", lambda e: e.tensor_tensor(out=lr[:], in0=lr[:], in1=neghalf[:, 0:4], op=ALU.pow), r=[("lnr", par)], w=[("lnr", par)])
            def ln_tail(bk=bk):
                add("dve", lambda e: e.scalar_tensor_tensor(out=lb[:], in0=mv[:, :, 0], scalar=-1.0, in1=lr[:], op0=ALU.mult, op1=ALU.mult),
                    r=[("lnmv", par), ("lnr", par)], w=[("lnb", par)])

                def fn_(e, bk=bk):
                    ins = None
                    for q in range(4):
                        ins = e.activation(out=vln[par][:, q * 128:(q + 1) * 128], in_=ps[bk][:, q * 128:(q + 1) * 128],
                                           func=AF.Identity, scale=lr[:, q:q + 1], bias=lb[:, q:q + 1])
                    return ins
                add("act", fn_, r=[("lnr", par), ("lnb", par)], w=[PK(bk), ("vln", par)])
            bk = proj(O_RQ)
            rotary(bk, n, qrot[par], 0, ("qrot", par))
            ln_tail()
            bk = proj(O_RK)
            rotary(bk, n, krot[par], 1, ("krot", par))
            add("pool", lambda e: e.tensor_tensor(out=ktok[par][:].rearrange("p (h d) -> p h d", h=4),
                                                  in0=krot[par][:].rearrange("p (h d) -> p h d", h=4),
                                                  in1=kdec[:].unsqueeze(2).to_broadcast([128, 4, 128]), op=ALU.mult),
                r=[("krot", par), "kdec"], w=[("ktok", par)])
            bk = proj(O_RV)
            add("act", lambda e, bk=bk: e.activation(out=vbf[g % 3][:], in_=ps[bk][:, :], func=AF.Copy), w=[PK(bk), ("vbf", g % 3)])

        def mixA1(g):
            b, n, m, c = g // NCH, g % NCH, g // MTC, g % MTC
            par = g % 2
            bk = trp.next()

            def f(e, bk=bk):
                ins = None
                for h in range(4):
                    ins = e.transpose(psb[bk][:, h * 128:(h + 1) * 128], qrot[par][:, h * 128:(h + 1) * 128], identb[:])
                for h in range(4):
                    ins = e.transpose(psb[bk][:, 512 + h * 128:512 + (h + 1) * 128], krot[par][:, h * 128:(h + 1) * 128], identb[:])
                return ins
            add("pe", f, r=[("qrot", par), ("krot", par), "identb"], w=[PK(bk)])
            add("act", lambda e, bk=bk: e.activation(out=qT[par][:], in_=psb[bk][:, 0:512], func=AF.Copy), w=[PK(bk), ("qT", par)])
            add("dve", lambda e, bk=bk: e.tensor_tensor(out=kT[par][:], in0=psb[bk][:, 512:1024], in1=gkinv[:], op=ALU.mult),
                r=["gkinv"], w=[PK(bk), ("kT", par)])

        def mixA2(g):
            b, n, m, c = g // NCH, g % NCH, g // MTC, g % MTC
            par = g % 2
            bk = mix.next()

            def f(e, bk=bk):
                ins = None
                for q in range(4):
                    ins = e.matmul(ps[bk][:, q * 128:(q + 1) * 128], lhsT=ginv2[:, q * 128:(q + 1) * 128],
                                   rhs=b2[:, q * 128:(q + 1) * 128], start=(q == 0), stop=False)
                for q in range(4):
                    ins = e.matmul(ps[bk][:, q * 128:(q + 1) * 128], lhsT=vln[par][:, q * 128:(q + 1) * 128],
                                   rhs=wsTb[:, q * 128:(q + 1) * 128], start=False, stop=(q == 3))
                return ins
            add("pe", f, r=[("vln", par), "wsTb", "b2", "ginv2"], w=[PK(bk)])
            add("dve", lambda e, bk=bk: e.tensor_tensor(out=yT[g % 3][:, 0:4, :], in0=ps[bk][:, :].rearrange("p (q t) -> p q t", q=4),
                                                 in1=ug[m % 2][:, :, c * 128:(c + 1) * 128], op=ALU.mult),
                r=[("ug", m % 2)], w=[PK(bk), ("yTg", g % 3)])
            bk = mix.next()

            def f(e, bk=bk):
                ins = None
                for h in range(4):
                    ins = e.matmul(ps[bk][:, h * 128:(h + 1) * 128], lhsT=kT[par][:, h * 128:(h + 1) * 128],
                                   rhs=qT[par][:, h * 128:(h + 1) * 128], start=True, stop=True)
                return ins
            add("pe", f, r=[("qT", par), ("kT", par)], w=[PK(bk)])
            add("dve", lambda e, bk=bk: e.tensor_tensor(out=msT[par][:].rearrange("p (h q) -> p h q", h=4),
                                                 in0=ps[bk][:, :].rearrange("p (h q) -> p h q", h=4),
                                                 in1=cmask[:].unsqueeze(1).to_broadcast([128, 4, 128]), op=ALU.mult),
                r=["cmask"], w=[PK(bk), ("msT", par)])
            if n == 0:
                add("dve", lambda e: e.tensor_copy(out=msT[par][0:1, ::128], in_=s0p[b][:]), r=[("s0p", b)], w=[("msT", par)])
            bk = mix.next()

            def f(e, bk=bk):
                ins = None
                for h in range(4):
                    ins = e.matmul(ps[bk][:, h * 128:(h + 1) * 128], lhsT=ktok[par][:, h * 128:(h + 1) * 128],
                                   rhs=vbf[g % 3][:, h * 128:(h + 1) * 128], start=True, stop=True)
                return ins
            add("pe", f, r=[("ktok", par), ("vbf", g % 3)], w=[PK(bk)])
            if n == 0:
                add("dve", lambda e, bk=bk: e.tensor_copy(out=Sst[b][:], in_=ps[bk][:, :]), w=[PK(bk), ("S", b)])
            else:
                def f(e, bk=bk):
                    ins = None
                    for h in range(4):
                        ins = e.scalar_tensor_tensor(out=Sst[b][:, h * 128:(h + 1) * 128], in0=Sst[b][:, h * 128:(h + 1) * 128],
                                                     scalar=gC[h], in1=ps[bk][:, h * 128:(h + 1) * 128], op0=ALU.mult, op1=ALU.add)
                    return ins
                add("dve", f, r=[("S", b)], w=[PK(bk), ("S", b)])
            add("act", lambda e: e.activation(out=Sbf[(g + 1) % 2][:], in_=Sst[b][:], func=AF.Copy),
                r=[("S", b)], w=[("Sbf", (g + 1) % 2)])

        def mixB1(g):
            b, n, m, c = g // NCH, g % NCH, g // MTC, g % MTC
            par = g % 2
            bk = mix.next()

            def f(e):
                ins = None
                for h in range(4):
                    hs = slice(h * 128, (h + 1) * 128)
                    ins = e.matmul(ps[bk][:, hs], lhsT=msT[par][:, hs], rhs=vbf[g % 3][:, hs], start=True, stop=(n == 0))
                    if n > 0:
                        ins = e.matmul(ps[bk][:, hs], lhsT=qT[par][:, hs], rhs=Sbf[g % 2][:, hs], start=False, stop=True)
                return ins
            rk = [("msT", par), ("vbf", g % 3), ("qT", par)] + ([("Sbf", g % 2)] if n > 0 else [])
            add("pe", f, r=rk, w=[PK(bk)])
            st, mv, gr, gb = gnst[par], gnmv[par], gnr[par], gnb[par]

            def fs(e):
                ins = None
                for q in range(4):
                    ins = e.bn_stats(out=st[:, q, :], in_=ps[bk][:, q * 128:(q + 1) * 128])
                return ins
            add("dve", fs, w=[PK(bk), ("gnst", par)])

            def fg(e):
                ins = None
                for q in range(4):
                    ins = e.bn_aggr(out=mv[:, q, :], in_=st[:, q, :])
                return ins
            add("dve", fg, r=[("gnst", par)], w=[("gnmv", par)])
            add("dve", lambda e: e.tensor_tensor(out=gr[:], in0=mv[:, :, 1], in1=epsp[:], op=ALU.add), r=[("gnmv", par), "epsp"], w=[("gnr", par)])
            add("pool", lambda e: e.tensor_tensor(out=gr[:], in0=gr[:], in1=neghalf[:, 0:4], op=ALU.pow), r=[("gnr", par)], w=[("gnr", par)])
            def gn_tail():
                add("dve", lambda e: e.scalar_tensor_tensor(out=gb[:], in0=mv[:, :, 0], scalar=-1.0, in1=gr[:], op0=ALU.mult, op1=ALU.mult),
                    r=[("gnmv", par), ("gnr", par)], w=[("gnb", par)])

                def fn_(e):
                    ins = None
                    for q in range(4):
                        ins = e.activation(out=onb[par][:, q * 128:(q + 1) * 128], in_=ps[bk][:, q * 128:(q + 1) * 128],
                                           func=AF.Identity, scale=gr[:, q:q + 1], bias=gb[:, q:q + 1])
                    return ins
                add("act", fn_, r=[("gnr", par), ("gnb", par)], w=[PK(bk), ("onb", par)])
            return gn_tail

        def mixB2(g):
            b, n, m, c = g // NCH, g % NCH, g // MTC, g % MTC
            par = g % 2
            bk2 = trp.next()

            def f(e):
                ins = None
                for h in range(4):
                    ins = e.transpose(psb[bk2][:, h * 128:(h + 1) * 128], onb[par][:, h * 128:(h + 1) * 128], identb[:])
                return ins
            add("pe", f, r=[("onb", par), "identb"], w=[PK(bk2)])
            add("dve", lambda e: e.tensor_tensor(out=yT[g % 3][:, 4:8, :], in0=psb[bk2][:, 0:512].rearrange("p (h t) -> p h t", h=4),
                                                 in1=srg[m % 2][:, :, c * 128:(c + 1) * 128], op=ALU.mult),
                r=[("srg", m % 2)], w=[PK(bk2), ("yTr", g % 3)])

        def outp(g):
            b, sl = g // NCH, g % NX
            par = g % 2
            bks = []
            for half in range(2):
                bk = big.next()
                bks.append(bk)

                def f(e, half=half, bk=bk):
                    ins = None
                    for k in range(8):
                        ins = e.matmul(ps[bk][:, :], lhsT=yT[g % 3][:, k, :], rhs=w_out[:, k, half * 512:(half + 1) * 512],
                                       start=(k == 0), stop=(k == 7))
                    return ins
                add("pe", f, r=[("yTg", g % 3), ("yTr", g % 3), ("wout", half)], w=[PK(bk)])
                add("act", lambda e, half=half, bk=bk: e.activation(out=junkA[:, 0:512], in_=ps[bk][:, :], func=AF.Square,
                                                                   accum_out=zss[par][:, half:half + 1]),
                    w=[PK(bk), "junkA", ("zss", par, half)])
            add("dve", lambda e: e.tensor_tensor(out=zr[par][:, 0:1], in0=zss[par][:, 0:1], in1=zss[par][:, 1:2], op=ALU.add),
                r=[("zss", par, 0), ("zss", par, 1)], w=[("zr0", par)])
            add("dve", lambda e: e.tensor_scalar(out=zr[par][:, 0:1], in0=zr[par][:, 0:1], scalar1=1.0 / D, scalar2=EPS,
                                                 op0=ALU.mult, op1=ALU.add), r=[("zr0", par)], w=[("zr0", par)])
            add("pool", lambda e: e.tensor_tensor(out=zr[par][:, 1:2], in0=zr[par][:, 0:1], in1=neghalf[:, 0:1], op=ALU.pow),
                r=[("zr0", par), "neghalf"], w=[("zr1", par)])
            def out_tail():
                for half in range(2):
                    bk = bks[half]
                    add("dve", lambda e, half=half, bk=bk: e.scalar_tensor_tensor(
                        out=tmpz[par][:, half * 512:(half + 1) * 512], in0=ps[bk][:, :], scalar=zr[par][:, 1:2],
                        in1=gp[b][:, half * 512:(half + 1) * 512], op0=ALU.mult, op1=ALU.mult),
                        r=[("zr1", par), ("gp", b)], w=[PK(bk), ("tmp", par)])
                xl = 3 + g % 3
                add("dve", lambda e: e.tensor_tensor(out=xsl[xl], in0=xsl[xl], in1=tmpz[par], op=ALU.add),
                    r=[("tmp", par)], w=[("x", xl)])
                dma("sp", out_d[g * 128:(g + 1) * 128, :], xsl[xl], f"o{xl}", r=[("x", xl)], w=[("out", g)])
            return out_tail

        assert MTC == 2
        front0(3)
        front2(0)
        early_front(2, head=False)
        front2(1)
        precise_mm()
        for s in range(G + 4):
            if s == 1:
                precise_el()
            if s == 2:
                wload(w_out, wout_d, 0, 512, "w_o0", ("wout", 0))
            if s == 3:
                wload(w_out, wout_d, 512, 512, "w_o1", ("wout", 1))
            if 0 <= s - 3 < G:
                reload(s - 3)
            if s + MTC + 2 < G:
                front0(s + MTC + 2)
            if s + MTC < G:
                front2(s + MTC)
            if 0 <= s - 1 < G:
                mixA1(s - 1)
            if s + MTC + 1 < G:
                front1(s + MTC + 1)
            if s == NCH:
                tables(1)
            if s < G:
                projA(s)
            if 0 <= s - 3 < G:
                mixB2(s - 3)
            if s < G and s % MTC == 0:
                projB(s // MTC)
            gn_tail = mixB1(s - 2) if 0 <= s - 2 < G else None
            out_tail = outp(s - 4) if 0 <= s - 4 < G else None
            if gn_tail is not None:
                gn_tail()
            if 0 <= s - 1 < G:
                mixA2(s - 1)
            if out_tail is not None:
                out_tail()
        add("sp", lambda e: e.nop(), r=[("out", g) for g in range(G)])

        print("sbuf bytes remaining:", nc.sbuf_bytes_remaining, "ops:", len(S.ops))
        S.analyze()
        eng_sems = {n: es.enter_context(nc.semaphore("s_" + n)) for n in ("pe", "act", "dve", "pool", "sp")}
        dma_sems = {n: es.enter_context(nc.semaphore("d_" + n)) for n in S.dma_total}
        block = es.enter_context(nc.Block())

        @block.sync
        def _(e):
            S.emit_engine("sp", e, eng_sems, dma_sems)

        @block.tensor
        def _(e):
            S.emit_engine("pe", e, eng_sems, dma_sems)

        @block.scalar
        def _(e):
            S.emit_engine("act", e, eng_sems, dma_sems)

        @block.vector
        def _(e):
            S.emit_engine("dve", e, eng_sems, dma_sems)

        @block.gpsimd
        def _(e):
            S.emit_engine("pool", e, eng_sems, dma_sems)
    return nc


def _consts():
    h = np.arange(4, dtype=np.float64)
    gam = 1.0 - 2.0 ** (-5.0 - h)
    t = np.arange(128, dtype=np.float64)
    sc = 128.0 ** -0.5
    gkinv = (sc * gam[:, None] ** (-(t[None, :] + 1.0))).reshape(1, 512)
    gkinv = np.broadcast_to(gkinv, (128, 512)).astype(np.float32)
    kdec = (sc * gam[None, :] ** (127.0 - t[:, None])).astype(np.float32)
    epsp = (EPS * gam[None, :] ** (-2.0 * (t[:, None] + 1.0))).astype(np.float32)
    half = 64
    inv_freq = (1.0 / (10000.0 ** (np.arange(half, dtype=np.float32) / half))).astype(np.float32)
    invf = np.broadcast_to((inv_freq.astype(np.float64) / (2.0 * np.pi)).astype(np.float32)[None, :], (128, 64))
    cmask = (t[None, :] >= t[:, None]).astype(np.float32)
    sel = np.zeros((2, 256), np.float32)
    sel[0, 0:128] = 1.0
    sel[1, 128:256] = 1.0
    return dict(gkinv=np.ascontiguousarray(gkinv), kdec=kdec, epsp=epsp, invf=np.ascontiguousarray(invf),
                cmask=cmask, sel=sel, ident=np.eye(128, dtype=np.float32))


def _pack(g_pre_t, lng_t, gng_t, kdec, epsp, invf, cT, x0T, pos):
    cpk = np.zeros((128, 160), np.float32)
    cpk[:, 0:8], cpk[:, 8:12], cpk[:, 12:16], cpk[:, 16:20], cpk[:, 20:24] = g_pre_t, lng_t, gng_t, kdec, epsp
    cpk[:, 24:88], cpk[:, 88:104], cpk[:, 104:120] = invf, cT, x0T
    cpk[:, 120:152] = np.ascontiguousarray(pos.astype(np.int32)).view(np.float32)
    return cpk


_NC_CACHE = {}


def kernel(x, c, positions, w_ada, b_ada, g_pre, w_in, gmlp_ln_g, gmlp_ws, gmlp_bs, ret_gn_g, w_out, g_post):
    f = lambda a: np.ascontiguousarray(np.asarray(a))
    x, c, positions = f(x), f(c), f(positions)
    w_ada, b_ada, g_pre, w_in = f(w_ada)[0], f(b_ada)[0], f(g_pre)[0], f(w_in)[0]
    lng, ws, bs, gng, w_out, g_post = f(gmlp_ln_g)[0], f(gmlp_ws)[0], f(gmlp_bs)[0], f(ret_gn_g)[0], f(w_out)[0], f(g_post)[0]
    if "nc" not in _NC_CACHE:
        _NC_CACHE["nc"] = build_program()
    nc = _NC_CACHE["nc"]
    cst = _consts()
    g_pre_t, lng_t, gng_t = f(g_pre.reshape(8, 128).T), f(lng.reshape(4, 128).T), f(gng.reshape(4, 128).T)
    kdec_c, epsp_c, invf_c = cst.pop("kdec"), cst.pop("epsp"), cst.pop("invf")
    shared = dict(
        w_ada=w_ada, b_ada=f(b_ada.reshape(1, -1)), w_in=w_in,
        lng_r2=f(np.broadcast_to(lng.reshape(1, 512), (2, 512))), wsT=f(ws.transpose(2, 0, 1).reshape(128, 512)), bs=f(bs.reshape(1, 512)),
        w_out=w_out, g_post_r=f(np.broadcast_to(g_post[None, :], (128, D))), **cst)
    in_maps = []
    for i in range(NCORES):
        bsl = slice(i * NB, (i + 1) * NB)
        xm = f(x[bsl].reshape(G * 128, D))
        cT = f(c[bsl].reshape(NB, 8, 128).transpose(2, 1, 0).reshape(128, 16))
        pos = f(positions[bsl].reshape(NB, NCH, 128).transpose(2, 0, 1).reshape(128, NB * NCH).astype(np.int32))
        x0T = f(x[bsl, 0, :].reshape(NB, 8, 128).transpose(2, 1, 0).reshape(128, 8 * NB))
        cpk = _pack(g_pre_t, lng_t, gng_t, kdec_c, epsp_c, invf_c, cT, x0T, pos)
        in_maps.append(dict(x=xm, cpk=cpk, **shared))
    res = run_bass_kernel_spmd(nc, in_maps, core_ids=list(range(NCORES)))
    out = np.concatenate([np.asarray(r["out"]).reshape(NB, SEQ, D) for r in res.results], axis=0)
    return out.astype(np.float32, copy=False)
```

```python
import numpy as np
from contextlib import ExitStack
from collections import defaultdict
import concourse.bass as bass
import concourse.mybir as mybir
from concourse.bass_utils import run_bass_kernel_spmd

F32 = mybir.dt.float32
BF16 = mybir.dt.bfloat16
I32 = mybir.dt.int32
AF = mybir.ActivationFunctionType
ALU = mybir.AluOpType

NCORES = 8
BATCH, SEQ, D = 16, 2048, 1024
NB = BATCH // NCORES
NCH = SEQ // 128
G = NB * NCH
PW = 3584
O_GU, O_GV, O_GG, O_RQ, O_RK, O_RV, O_RG = 0, 512, 1024, 1536, 2048, 2560, 3072
EPS = 1e-6
MTC = 2
TM = MTC * 128
NMT = G // MTC
NX = 6
TWO_PI_SAFE = 6.283185
NEAR = 3


class Op:
    __slots__ = ("eng", "fn", "reads", "writes", "dma", "idx", "eidx", "deps", "signal", "tick", "dma_ord")


class Sched:
    def __init__(self):
        self.ops = []
        self.dma_count = defaultdict(int)

    def add(self, eng, fn, r=(), w=(), dma=None):
        op = Op()
        op.eng, op.fn, op.reads, op.writes, op.dma = eng, fn, tuple(r), tuple(w), dma
        op.idx = len(self.ops)
        op.deps = []
        op.signal = False
        op.tick = 0
        op.dma_ord = 0
        self.ops.append(op)
        return op

    def analyze(self):
        last_w = {}
        readers = {}
        ecount = defaultdict(int)
        dma_seen = defaultdict(int)
        for op in self.ops:
            op.eidx = ecount[op.eng]
            ecount[op.eng] += 1
            deps = {}
            for k in op.reads:
                if k in last_w:
                    deps[last_w[k]] = "raw"
            for k in op.writes:
                if k in last_w:
                    deps.setdefault(last_w[k], "waw")
                for rr in readers.get(k, ()):
                    deps.setdefault(rr, "war")
            for k in op.reads:
                readers.setdefault(k, []).append(op.idx)
            for k in op.writes:
                last_w[k] = op.idx
                readers[k] = []
            need = {}
            for pi, kind in deps.items():
                if pi == op.idx:
                    continue
                p = self.ops[pi]
                if p.dma is None and op.dma is None and p.eng == op.eng:
                    if op.eng == "pe":
                        continue
                    if (op.eidx - p.eidx) > NEAR:
                        continue
                if p.dma is not None:
                    key = ("dma", p.dma)
                    need[key] = max(need.get(key, 0), 16 * dma_seen[p.dma])
                else:
                    p.signal = True
                    key = ("eng", p.eng)
                    prev = need.get(key)
                    if prev is None or self.ops[prev].idx < p.idx:
                        need[key] = p.idx
            op.deps = need
            if op.dma is not None:
                dma_seen[op.dma] += 1
                op.dma_ord = dma_seen[op.dma]
        tick = defaultdict(int)
        for op in self.ops:
            if op.dma is None and op.signal:
                tick[op.eng] += 1
                op.tick = tick[op.eng]
        self.dma_total = dict(dma_seen)

    def emit_engine(self, eng_name, eng, eng_sems, dma_sems):
        waited = {}
        for op in self.ops:
            if op.eng != eng_name:
                continue
            for key, val in op.deps.items():
                if key[0] == "dma":
                    sem, v = dma_sems[key[1]], val
                else:
                    sem, v = eng_sems[key[1]], self.ops[val].tick
                if waited.get(key, 0) >= v:
                    continue
                eng.wait_ge(sem, v)
                waited[key] = v
            ins = op.fn(eng)
            if op.dma is not None:
                ins.then_inc(dma_sems[op.dma], 16)
            elif op.signal:
                ins.then_inc(eng_sems[op.eng], 1)


class PsPool:
    def __init__(self, banks):
        self.banks = banks
        self.i = 0

    def next(self):
        b = self.banks[self.i % len(self.banks)]
        self.i += 1
        return b


def build_program():
    nc = bass.Bass("TRN2", target_bir_lowering=False)
    dt_in = lambda name, shape, dt=F32: nc.dram_tensor(name, list(shape), dt, kind="ExternalInput").ap()
    x_d = dt_in("x", [G * 128, D])
    wada_d = dt_in("w_ada", [D, 3 * D])
    bada_d = dt_in("b_ada", [1, 3 * D])
    win_d = dt_in("w_in", [D, PW])
    wsT_d = dt_in("wsT", [128, 512])
    bs_d = dt_in("bs", [1, 512])
    lngr_d = dt_in("lng_r2", [2, 512])
    wout_d = dt_in("w_out", [D, D])
    gpost_d = dt_in("g_post_r", [128, D])
    ident_d = dt_in("ident", [128, 128])
    cmask_d = dt_in("cmask", [128, 128])
    gkinv_d = dt_in("gkinv", [128, 512])
    sel_d = dt_in("sel", [2, 256])
    cpk_d = dt_in("cpk", [128, 160])
    out_d = nc.dram_tensor("out", [G * 128, D], F32, kind="ExternalOutput").ap()

    gC = [float((1.0 - 2.0 ** (-5.0 - h)) ** 128) for h in range(4)]

    es = ExitStack()
    with es:
        sb = lambda name, shape, dt=F32: es.enter_context(nc.sbuf_tensor(name, list(shape), dt))
        w_in = sb("w_in_sb", [128, 8, PW], BF16)
        w_out = sb("w_out_sb", [128, 8, D], BF16)
        xbuf = sb("xbuf", [128, NX * D], F32)
        xbuf_bf = xbuf[:].bitcast(BF16)
        xsl = [xbuf[:, i * D:(i + 1) * D] for i in range(NX)]
        wa_stage = [xbuf_bf[:, j * 3072:(j + 1) * 3072] for j in range(4)]
        wa_xkeys = [[("x", 0), ("x", 1)], [("x", 1), ("x", 2)], [("x", 3), ("x", 4)], [("x", 4), ("x", 5)]]
        xsb = [sb(f"xsb{i}", [128, D], BF16) for i in range(2)]
        hT = [sb(f"hT{i}", [128, 8, TM], BF16) for i in range(2)]
        ug = [sb(f"ug{i}", [128, 4, TM], F32) for i in range(2)]
        srg = [sb(f"srg{i}", [128, 4, TM], F32) for i in range(2)]
        sgg = [sb(f"sgg{i}", [128, TM], F32) for i in range(2)]
        srgt = [sb(f"srgt{i}", [128, TM], F32) for i in range(2)]
        vln = [sb(f"vln{i}", [128, 512], BF16) for i in range(2)]
        qrot = [sb(f"qrot{i}", [128, 512], BF16) for i in range(2)]
        krot = [sb(f"krot{i}", [128, 512], BF16) for i in range(2)]
        At = [sb(f"At{i}", [128, 512], F32) for i in range(2)]
        Bt = [sb(f"Bt{i}", [128, 512], F32) for i in range(2)]
        ktok = [sb(f"ktok{i}", [128, 512], BF16) for i in range(2)]
        vbf = [sb(f"vbf{i}", [128, 512], BF16) for i in range(3)]
        qT = [sb(f"qT{i}", [128, 512], BF16) for i in range(2)]
        kT = [sb(f"kT{i}", [128, 512], BF16) for i in range(2)]
        msT = [sb(f"msT{i}", [128, 512], BF16) for i in range(2)]
        Sst = [sb(f"Sst{i}", [128, 512], F32) for i in range(NB)]
        Sbf = [sb(f"Sbf{i}", [128, 512], BF16) for i in range(2)]
        onb = [sb(f"onb{i}", [128, 512], BF16) for i in range(2)]
        yT = [sb(f"yT{i}", [128, 8, 128], BF16) for i in range(3)]
        scr12 = sb("scr12", [128, 3072], F32)
        tmpz = [scr12[:, 0:1024], scr12[:, 1024:2048]]
        junkA = scr12[:, 2048:3072]
        mod_sb = scr12[0:2, :]
        SCRK = [("tmp", 0), ("tmp", 1), "junkA"]
        cosT = sb("cosT", [128, NCH, 64], F32)
        sinT = sb("sinT", [128, NCH, 64], F32)
        gp = [sb(f"gp{i}", [128, D], F32) for i in range(NB)]
        gkinv = sb("gkinv_sb", [128, 512], F32)
        wsTb = sb("wsTb", [128, 512], BF16)
        identf = sb("identf", [128, 128], F32)
        identb = sb("identb", [128, 128], BF16)
        cmask = sb("cmask_sb", [128, 128], F32)
        cpk = sb("cpk_sb", [128, 160], F32)
        gpre, lng, gng, kdec, epsp = cpk[:, 0:8], cpk[:, 8:12], cpk[:, 12:16], cpk[:, 16:20], cpk[:, 20:24]
        invf, cTf = cpk[:, 24:88], cpk[:, 88:104]
        x0T = cpk[:, 104:120].rearrange("p (k b) -> p k b", b=NB)
        pos_i = cpk[:, 120:152].bitcast(I32)
        CPK_KEYS = ["gpre", "lng", "gng", "kdec", "epsp", "invf", "cTf", "x0T", "pos_i"]
        sel = sb("sel_sb", [2, 256], F32)
        cTb = sb("cTb", [128, 16], BF16)
        posf = sb("posf", [128, NCH], F32)
        gsT = sb("gsT", [128, 8, NB], F32)
        shT = sb("shT", [128, 8, NB], F32)
        neghalf = sb("neghalf", [128, 8], F32)
        onesb = sb("onesb", [2, 128], BF16)
        b2 = sb("b2", [2, 512], BF16)
        ginv2 = sb("ginv2", [2, 512], BF16)
        ssq = sb("ssq", [128, NX], F32)
        rstd = sb("rstd", [128, NX], F32)
        rtmp = sb("rtmp", [128, NX], F32)
        lnst = [sb(f"lnst{i}", [128, 4, 6], F32) for i in range(2)]
        lnmv = [sb(f"lnmv{i}", [128, 4, 2], F32) for i in range(2)]
        lnr = [sb(f"lnr{i}", [128, 4], F32) for i in range(2)]
        lnb = [sb(f"lnb{i}", [128, 4], F32) for i in range(2)]
        gnst = [sb(f"gnst{i}", [128, 4, 6], F32) for i in range(2)]
        gnmv = [sb(f"gnmv{i}", [128, 4, 2], F32) for i in range(2)]
        gnr = [sb(f"gnr{i}", [128, 4], F32) for i in range(2)]
        gnb = [sb(f"gnb{i}", [128, 4], F32) for i in range(2)]
        zss = [sb(f"zss{i}", [128, 2], F32) for i in range(2)]
        zr = [sb(f"zr{i}", [128, 2], F32) for i in range(2)]
        sqt = sb("sqt", [128, 8, NB], F32)
        sqs = sb("sqs", [128, NB], F32)
        ones_f = sb("ones_f", [128, 1], F32)
        Lq = sb("Lq", [128, 8, NB, 2], F32)
        r0t = sb("r0t", [1, NB], F32)
        rstd0 = sb("rstd0", [1, NB], F32)
        pos0f = sb("pos0f", [1, NB], F32)
        f0 = sb("f0", [1, NB * 64], F32)
        f0i = sb("f0i", [1, NB * 64], I32)
        f0b = sb("f0b", [1, NB * 64], F32)
        cos0 = sb("cos0", [1, NB * 64], F32)
        sin0 = sb("sin0", [1, NB * 64], F32)
        s4 = sb("s4", [1, 4], F32)
        s0p = [sb(f"s0p{i}", [1, 4], F32) for i in range(NB)]
        gpost = hT[0][:].rearrange("p k t -> p (k t)").bitcast(F32)
        GPOSTK = [("hT", 0, c) for c in range(MTC)]
        wsTf = hT[1][:].rearrange("p k t -> p (k t)").bitcast(F32)[:, 0:512]
        WSTFK = [("hT", 1, c) for c in range(MTC)]
        bs_f = At[0][0:1, :]
        hi_f = Bt[0][0:1, :]
        hi_b = At[1][:].bitcast(BF16)[0:1, 0:512]
        lo_b = Bt[1][:].bitcast(BF16)[0:1, 0:512]
        gi_f = Sst[1][0:2, :]
        ps = [es.enter_context(nc.psum_tensor(f"ps{i}", [128, 512], F32)) for i in range(8)]
        psb = [p[:].bitcast(BF16) for p in ps]
        big = PsPool([0, 1, 2, 3, 4, 5, 6, 7])
        trp = big
        mix = big
        PK = lambda b: ("ps", b)

        S = Sched()
        add = S.add

        def dma(q, out, in_, sem, r=(), w=()):
            add(q, lambda e, out=out, in_=in_: e.dma_start(out=out, in_=in_), r=r, w=w, dma=sem)

        dma("sp", cpk[:], cpk_d[:, :], "c0", w=CPK_KEYS)
        dma("sp", identf[:], ident_d[:, :], "c0", w=["identf"])
        dma("sp", sel[:], sel_d[:, :], "c0", w=["sel"])
        dma("sp", gpost, gpost_d[:, :], "c1", w=GPOSTK)
        dma("sp", wsTf, wsT_d[:, :], "c1", w=WSTFK)
        dma("sp", cmask[:], cmask_d[:, :], "c1", w=["cmask"])
        dma("sp", bs_f, bs_d[:, :], "c1", w=[("A", 0)])
        dma("sp", gkinv[:], gkinv_d[:, :], "c1", w=["gkinv"])

        def ada_load(k):
            j = k % 4
            if k < 8:
                dma("pool", wa_stage[j], wada_d[k * 128:(k + 1) * 128, :], f"wa{j}", w=wa_xkeys[j])
            else:
                dma("pool", wa_stage[j][0:1, :], bada_d[:, :], f"wa{j}", w=wa_xkeys[j])

        for k in range(4):
            ada_load(k)

        add("dve", lambda e: e.memset(neghalf[:], -0.5), w=["neghalf"])
        add("dve", lambda e: e.memset(onesb[:], 1.0), w=["onesb"])
        add("dve", lambda e: e.tensor_copy(out=identb[:], in_=identf[:]), r=["identf"], w=["identb"])
        add("act", lambda e: e.activation(out=cTb[:], in_=cTf[:], func=AF.Silu), r=["cTf"], w=["cTb"])

        cTb3 = cTb[:].rearrange("p (k b) -> p k b", b=NB)
        for k in range(9):
            j = k % 4

            def f(e, k=k, j=j):
                ins = None
                for n in range(6):
                    if k < 8:
                        lhsT = cTb3[:, k, :]
                        rhs = wa_stage[j][:, n * 512:(n + 1) * 512]
                    else:
                        lhsT = onesb[0:1, 0:2]
                        rhs = wa_stage[j][0:1, n * 512:(n + 1) * 512]
                    ins = e.matmul(ps[n][0:2, :], lhsT=lhsT, rhs=rhs, start=(k == 0), stop=(k == 8))
                return ins

            add("pe", f, r=wa_xkeys[j] + ["cTb", "onesb"], w=[PK(n) for n in range(6)])
            if k + 4 < 9:
                ada_load(k + 4)

        def wload(dst, src, c0, width, sem, key):
            for kk in range(2):
                dma("pool", dst[:, kk * 4:(kk + 1) * 4, c0:c0 + width],
                    src[kk * 512:(kk + 1) * 512, c0:c0 + width].rearrange("(k p) c -> p k c", p=128), sem, w=[key])

        early_buf = {0: (ug[0][:].rearrange("p j t -> p (j t)"), ("ug", 0)),
                     1: (ug[1][:].rearrange("p j t -> p (j t)"), ("ug", 1)),
                     2: (srg[0][:].rearrange("p j t -> p (j t)"), ("srg", 0))}

        def early_front(g, head=True, tail=True):
            xs_, xkey = early_buf[g]
            sl = g % NX
            if head:
                dma("sp", xs_, x_d[g * 128:(g + 1) * 128, :], f"xe{g}", w=[xkey])
                add("act", lambda e: e.activation(out=junkA, in_=xs_, func=AF.Square, accum_out=ssq[:, sl:sl + 1]),
                    r=[xkey], w=["junkA", ("ssq", sl)])
                add("dve", lambda e: e.tensor_scalar(out=rtmp[:, sl:sl + 1], in0=ssq[:, sl:sl + 1], scalar1=1.0 / D, scalar2=EPS,
                                                     op0=ALU.mult, op1=ALU.add), r=[("ssq", sl)], w=[("rtmp", sl)])
                add("pool", lambda e: e.tensor_tensor(out=rstd[:, sl:sl + 1], in0=rtmp[:, sl:sl + 1], in1=neghalf[:, 0:1], op=ALU.pow),
                    r=[("rtmp", sl), "neghalf"], w=[("rstd", sl)])
            if tail:
                xb = xsb[g % 2]
                add("act", lambda e: e.activation(out=xb[:], in_=xs_, func=AF.Copy, scale=rstd[:, sl:sl + 1]),
                    r=[xkey, ("rstd", sl)], w=[("xsb", g % 2)])

        early_front(0)
        early_front(1)
        early_front(2, tail=False)

        for (c0, nm) in ((O_GV, "gv"), (O_RQ, "rq"), (O_RK, "rk"), (O_RV, "rv"), (O_GG, "gg"), (O_GU, "gu"), (O_RG, "rg")):
            wload(w_in, win_d, c0, 512, "w_" + nm, ("win", c0))

        for n in range(6):
            add("dve" if n % 2 == 0 else "act",
                (lambda e, n=n: e.tensor_copy(out=mod_sb[:, n * 512:(n + 1) * 512], in_=ps[n][0:2, :])) if n % 2 == 0 else
                (lambda e, n=n: e.activation(out=mod_sb[:, n * 512:(n + 1) * 512], in_=ps[n][0:2, :], func=AF.Copy)),
                w=[PK(n)] + SCRK)
        def f(e):
            ins = None
            for j in range(16):
                ins = e.transpose(ps[6][:, j * 2:(j + 1) * 2], mod_sb[:, j * 128:(j + 1) * 128], identf[0:2, 0:2])
            return ins
        add("pe", f, r=SCRK + ["identf"], w=[PK(6)])
        ps6v = ps[6][:, 0:32].rearrange("p (j b) -> p j b", b=NB)
        add("dve", lambda e: e.tensor_copy(out=shT[:], in_=ps6v[:, 0:8, :]), w=[PK(6), "shT"])
        add("dve", lambda e: e.scalar_tensor_tensor(out=gsT[:], in0=ps6v[:, 8:16, :], scalar=1.0,
                                                    in1=gpre[:].unsqueeze(2).to_broadcast([128, 8, NB]),
                                                    op0=ALU.add, op1=ALU.mult), r=["gpre"], w=[PK(6), "gsT"])
        for b in range(NB):
            for half in range(2):
                bk = 7 if (b * 2 + half) % 2 == 0 else 6
                add("pe", lambda e, b=b, half=half, bk=bk: e.matmul(
                    ps[bk][:, :], lhsT=sel[0:2, b * 128:(b + 1) * 128],
                    rhs=mod_sb[:, 2048 + half * 512:2048 + (half + 1) * 512], start=True, stop=True),
                    r=SCRK + ["sel"], w=[PK(bk)])
                add("dve", lambda e, b=b, half=half, bk=bk: e.tensor_tensor(
                    out=gp[b][:, half * 512:(half + 1) * 512], in0=ps[bk][:, :],
                    in1=gpost[:, half * 512:(half + 1) * 512], op=ALU.mult),
                    r=GPOSTK, w=[PK(bk), ("gp", b)])
        add("dve", lambda e: e.tensor_tensor(out=wsTb[:].rearrange("p (g t) -> p g t", g=4),
                                             in0=wsTf.rearrange("p (g t) -> p g t", g=4),
                                             in1=cmask[:].unsqueeze(1).to_broadcast([128, 4, 128]), op=ALU.mult),
            r=WSTFK + ["cmask"], w=["wsTb"])
        add("dve", lambda e: e.tensor_copy(out=hi_b, in_=bs_f), r=[("A", 0)], w=[("A", 1)])
        add("dve", lambda e: e.tensor_copy(out=hi_f, in_=hi_b), r=[("A", 1)], w=[("B", 0)])
        add("dve", lambda e: e.tensor_tensor(out=lo_b, in0=bs_f, in1=hi_f, op=ALU.subtract), r=[("A", 0), ("B", 0)], w=[("B", 1)])
        dma("sp", b2[0:1, :], hi_b, "c2", r=[("A", 1)], w=["b2"])
        dma("sp", b2[1:2, :], lo_b, "c2", r=[("B", 1)], w=["b2"])

        dma("sp", gi_f, lngr_d[:, :], "c2", w=[("S", 1)])
        add("dve", lambda e: e.reciprocal(out=gi_f, in_=gi_f), r=[("S", 1)], w=[("S", 1)])
        add("dve", lambda e: e.tensor_copy(out=ginv2[:], in_=gi_f), r=[("S", 1)], w=["ginv2"])
        def precise_mm():
            ug0v = tmpz[0]
            ug1v = tmpz[1]
            sr0v = xsl[3]
            sr1v = tmpz[1]
            add("dve", lambda e: e.memset(ones_f[:], 1.0), w=["ones_f"])
            add("dve", lambda e: e.tensor_tensor(out=sqt[:], in0=x0T[:], in1=x0T[:], op=ALU.mult), r=["x0T"], w=["sqt"])
            add("dve", lambda e: e.reduce_sum(out=sqs[:], in_=sqt[:].rearrange("p k b -> p b k"), axis=mybir.AxisListType.X),
                r=["sqt"], w=["sqs"])
            add("pe", lambda e: e.matmul(ps[7][0:1, 0:NB], lhsT=ones_f[:, 0:1], rhs=sqs[:, :], start=True, stop=True),
                r=["ones_f", "sqs"], w=[PK(7)])
            add("dve", lambda e: e.tensor_scalar(out=r0t[:], in0=ps[7][0:1, 0:NB], scalar1=1.0 / D, scalar2=EPS, op0=ALU.mult, op1=ALU.add),
                w=[PK(7), "r0t"])
            add("pool", lambda e: e.tensor_tensor(out=rstd0[:], in0=r0t[:], in1=neghalf[0:1, 0:NB], op=ALU.pow), r=["r0t", "neghalf"], w=["rstd0"])
            add("dve", lambda e: e.tensor_tensor(out=Lq[:, :, :, 0], in0=x0T[:], in1=gsT[:], op=ALU.mult), r=["x0T", "gsT"], w=["Lq0"])
            add("dve", lambda e: e.tensor_copy(out=Lq[:, :, :, 1], in_=shT[:]), r=["shT"], w=["Lq1"])
            Lq2 = Lq[:].rearrange("p k b t -> p k (b t)")
            for k in range(8):
                slot = 3 + k % 3
                dma("sp", xsl[slot], win_d[k * 128:(k + 1) * 128, O_RQ:O_RQ + 1024], f"wq{k % 3}", w=[("x", slot)])

                def f(e, k=k, slot=slot):
                    ins = None
                    for half in range(2):
                        ins = e.matmul(ps[5 + half][0:2 * NB, :], lhsT=Lq2[:, k, :], rhs=xsl[slot][:, half * 512:(half + 1) * 512],
                                       start=(k == 0), stop=(k == 7))
                    return ins
                add("pe", f, r=[("x", slot), "Lq0", "Lq1"], w=[PK(5), PK(6)])
            add("dve", lambda e: e.tensor_copy(out=ug0v[0:2 * NB, 0:512], in_=ps[5][0:2 * NB, :]), w=[PK(5), ("tmp", 0)])
            add("act", lambda e: e.activation(out=ug0v[0:2 * NB, 512:1024], in_=ps[6][0:2 * NB, :], func=AF.Copy), w=[PK(6), ("tmp", 0)])

        def precise_el():
            ug0v = tmpz[0]
            ug1v = tmpz[1]
            sr0v = xsl[3]
            sr1v = tmpz[1]
            dma("sp", ug1v[0:1, :], ug0v[1:2, :], "c3", r=[("tmp", 0)], w=[("tmp", 1)])
            dma("sp", sr0v[0:1, :], ug0v[2:3, :], "c3", r=[("tmp", 0)], w=[("x", 3)])
            add("dve", lambda e: e.tensor_copy(out=pos0f[:], in_=pos_i[0:1, ::NCH]), r=["pos_i"], w=["pos0f"])
            f0_3 = f0[:].rearrange("p (b d) -> p b d", b=NB)
            add("dve", lambda e: e.tensor_tensor(out=f0_3, in0=pos0f[:].unsqueeze(2).to_broadcast([1, NB, 64]),
                                                 in1=invf[0:1, :].unsqueeze(1).to_broadcast([1, NB, 64]), op=ALU.mult),
                r=["pos0f", "invf"], w=["f0"])
            add("dve", lambda e: e.tensor_copy(out=f0i[:], in_=f0[:]), r=["f0"], w=["f0i"])
            add("dve", lambda e: e.tensor_copy(out=f0b[:], in_=f0i[:]), r=["f0i"], w=["f0b"])
            add("dve", lambda e: e.tensor_tensor(out=f0[:], in0=f0[:], in1=f0b[:], op=ALU.subtract), r=["f0", "f0b"], w=["f0"])
            add("act", lambda e: e.activation(out=sin0[:], in_=f0[:], func=AF.Sin, scale=TWO_PI_SAFE), r=["f0"], w=["sin0"])
            add("dve", lambda e: e.tensor_scalar(out=f0b[:], in0=f0[:], scalar1=0.25, scalar2=None, op0=ALU.add), r=["f0"], w=["f0b"])
            add("dve", lambda e: e.tensor_copy(out=f0i[:], in_=f0b[:]), r=["f0b"], w=["f0i"])
            add("dve", lambda e: e.tensor_copy(out=f0[:], in_=f0i[:]), r=["f0i", "sin0"], w=["f0"])
            add("dve", lambda e: e.tensor_tensor(out=f0b[:], in0=f0b[:], in1=f0[:], op=ALU.subtract), r=["f0", "f0b"], w=["f0b"])
            add("act", lambda e: e.activation(out=cos0[:], in_=f0b[:], func=AF.Sin, scale=TWO_PI_SAFE), r=["f0b"], w=["cos0"])
            for b in range(NB):
                if b == 1:
                    dma("sp", sr1v[0:1, :], ug0v[3:4, :], "c3", r=[("tmp", 0)], w=[("tmp", 1)])
                U = (ug0v if b == 0 else sr0v)[0:1, :]
                Wb = (ug1v if b == 0 else sr1v)[0:1, :]
                ku = ("tmp", 0) if b == 0 else ("x", 3)
                kw = ("tmp", 1) if b == 0 else ("tmp", 1)
                Aq = xsl[4][0:1, :]
                Bq = xsl[5][0:1, :]
                U4 = U.rearrange("p (h two d) -> p h two d", h=8, two=2)
                A4 = Aq.rearrange("p (h two d) -> p h two d", h=8, two=2)
                B4 = Bq.rearrange("p (h two d) -> p h two d", h=8, two=2)
                cb = cos0[0:1, b * 64:(b + 1) * 64].unsqueeze(1).unsqueeze(1).to_broadcast([1, 8, 2, 64])
                sbc = sin0[0:1, b * 64:(b + 1) * 64].unsqueeze(1).unsqueeze(1).to_broadcast([1, 8, 2, 64])
                add("dve", lambda e, U=U, Wb=Wb, b=b: e.scalar_tensor_tensor(out=U, in0=U, scalar=rstd0[0:1, b:b + 1], in1=Wb,
                                                                             op0=ALU.mult, op1=ALU.add),
                    r=[ku, kw, "rstd0"], w=[ku])
                add("dve", lambda e, U4=U4, A4=A4, cb=cb: e.tensor_tensor(out=A4, in0=U4, in1=cb, op=ALU.mult), r=[ku, "cos0"], w=[("x", 4)])
                add("dve", lambda e, U4=U4, B4=B4, sbc=sbc: e.tensor_tensor(out=B4, in0=U4[:, :, ::-1, :], in1=sbc, op=ALU.mult),
                    r=[ku, "sin0"], w=[("x", 5)])
                add("pool", lambda e, U4=U4, A4=A4, B4=B4: e.tensor_tensor(out=U4[:, :, 0, :], in0=A4[:, :, 0, :], in1=B4[:, :, 0, :], op=ALU.subtract),
                    r=[("x", 4), ("x", 5)], w=[ku])
                add("pool", lambda e, U4=U4, A4=A4, B4=B4: e.tensor_tensor(out=U4[:, :, 1, :], in0=A4[:, :, 1, :], in1=B4[:, :, 1, :], op=ALU.add),
                    r=[("x", 4), ("x", 5)], w=[ku])
                add("dve", lambda e, U=U, Aq=Aq: e.tensor_tensor(out=Aq[:, 0:512], in0=U[:, 0:512], in1=U[:, 512:1024], op=ALU.mult),
                    r=[ku], w=[("x", 4)])
                add("dve", lambda e, Aq=Aq: e.reduce_sum(out=s4[:], in_=Aq[:, 0:512].rearrange("p (h d) -> p h d", h=4), axis=mybir.AxisListType.X),
                    r=[("x", 4)], w=["s4"])
                add("dve", lambda e, b=b: e.tensor_tensor(out=s0p[b][:], in0=s4[:], in1=gkinv[0:1, ::128], op=ALU.mult),
                    r=["s4", "gkinv"], w=[("s0p", b)])

        T1, T2, T3 = tmpz[0], tmpz[1], junkA
        T2i = T2.bitcast(I32)
        cos2 = cosT[:].rearrange("p n d -> p (n d)")
        sin2 = sinT[:].rearrange("p n d -> p (n d)")
        T1_3 = T1.rearrange("p (n d) -> p n d", d=64)

        def tables(b):
            add("dve", lambda e: e.tensor_copy(out=posf[:], in_=pos_i[:, b * NCH:(b + 1) * NCH]), r=["pos_i"], w=["posf"])
            add("dve", lambda e: e.tensor_tensor(out=T1_3, in0=posf[:].unsqueeze(2).to_broadcast([128, NCH, 64]),
                                                 in1=invf[:].unsqueeze(1).to_broadcast([128, NCH, 64]), op=ALU.mult),
                r=["posf", "invf"], w=[("tmp", 0)])
            add("dve", lambda e: e.tensor_copy(out=T2i, in_=T1), r=[("tmp", 0)], w=[("tmp", 1)])
            add("dve", lambda e: e.tensor_copy(out=T3, in_=T2i), r=[("tmp", 1)], w=["junkA"])
            add("dve", lambda e: e.tensor_tensor(out=T1, in0=T1, in1=T3, op=ALU.subtract), r=[("tmp", 0), "junkA"], w=[("tmp", 0)])
            add("act", lambda e: e.activation(out=sin2, in_=T1, func=AF.Sin, scale=TWO_PI_SAFE), r=[("tmp", 0)], w=["sinT"])
            add("dve", lambda e: e.tensor_scalar(out=T3, in0=T1, scalar1=0.25, scalar2=None, op0=ALU.add), r=[("tmp", 0)], w=["junkA"])
            add("dve", lambda e: e.tensor_copy(out=T2i, in_=T3), r=["junkA"], w=[("tmp", 1)])
            add("dve", lambda e: e.tensor_copy(out=T1, in_=T2i), r=[("tmp", 1)], w=[("tmp", 0)])
            add("dve", lambda e: e.tensor_tensor(out=T3, in0=T3, in1=T1, op=ALU.subtract), r=[("tmp", 0), "junkA"], w=["junkA"])
            add("act", lambda e: e.activation(out=cos2, in_=T3, func=AF.Sin, scale=TWO_PI_SAFE), r=["junkA"], w=["cosT"])

        tables(0)

        def front0(g):
            xl = g % 3
            dma("sp", xsl[xl], x_d[g * 128:(g + 1) * 128, :], f"x{xl}", w=[("x", xl)])

        def reload(g):
            xl = 3 + g % 3
            dma("sp", xsl[xl], x_d[g * 128:(g + 1) * 128, :], f"x{xl}", w=[("x", xl)])

        def front1(g):
            b, m, c, sl = g // NCH, g // MTC, g % MTC, g % NX
            xl = g % 3
            xs_ = xsl[xl]
            add("act", lambda e: e.activation(out=junkA, in_=xs_, func=AF.Square, accum_out=ssq[:, sl:sl + 1]),
                r=[("x", xl)], w=["junkA", ("ssq", sl)])
            add("dve", lambda e: e.tensor_scalar(out=rtmp[:, sl:sl + 1], in0=ssq[:, sl:sl + 1], scalar1=1.0 / D, scalar2=EPS,
                                                 op0=ALU.mult, op1=ALU.add), r=[("ssq", sl)], w=[("rtmp", sl)])
            add("pool", lambda e: e.tensor_tensor(out=rstd[:, sl:sl + 1], in0=rtmp[:, sl:sl + 1], in1=neghalf[:, 0:1], op=ALU.pow),
                r=[("rtmp", sl), "neghalf"], w=[("rstd", sl)])
            xb = xsb[g % 2]
            add("act", lambda e: e.activation(out=xb[:], in_=xs_, func=AF.Copy, scale=rstd[:, sl:sl + 1]),
                r=[("x", xl), ("rstd", sl)], w=[("xsb", g % 2)])

        def front2(g):
            b, m, c, sl = g // NCH, g // MTC, g % MTC, g % NX
            xb = xsb[g % 2]
            bk = trp.next()

            def f(e):
                ins = None
                for k in range(8):
                    ins = e.transpose(psb[bk][:, k * 128:(k + 1) * 128], xb[:, k * 128:(k + 1) * 128], identb[:])
                return ins
            add("pe", f, r=[("xsb", g % 2), "identb"], w=[PK(bk)])
            hdst = hT[m % 2]

            def fa(e):
                ins = None
                for k in range(0, 4):
                    ins = e.activation(out=hdst[:, k, c * 128:(c + 1) * 128], in_=psb[bk][:, k * 128:(k + 1) * 128],
                                       func=AF.Identity, scale=gsT[:, k, b:b + 1], bias=shT[:, k, b:b + 1])
                return ins

            def fd(e):
                ins = None
                for k in range(4, 8):
                    ins = e.tensor_scalar(out=hdst[:, k, c * 128:(c + 1) * 128], in0=psb[bk][:, k * 128:(k + 1) * 128],
                                          scalar1=gsT[:, k, b:b + 1], scalar2=shT[:, k, b:b + 1], op0=ALU.mult, op1=ALU.add)
                return ins
            add("act", fa, r=["gsT", "shT"], w=[PK(bk), ("hT", m % 2, c)])
            add("dve", fd, r=["gsT", "shT"], w=[PK(bk), ("hT", m % 2, c)])

        def projB(m):
            hsrc = hT[m % 2]
            hkeys = [("hT", m % 2, c) for c in range(MTC)]

            def proj(col0, wkey):
                bk = big.next()

                def f(e):
                    ins = None
                    for k in range(8):
                        ins = e.matmul(ps[bk][:, 0:TM], lhsT=w_in[:, k, col0:col0 + 128], rhs=hsrc[:, k, :],
                                       start=(k == 0), stop=(k == 7))
                    return ins
                add("pe", f, r=hkeys + [wkey], w=[PK(bk)])
                return bk

            for j in range(4):
                sg = sgg[j % 2]
                bk = proj(O_GG + j * 128, ("win", O_GG))
                add("act", lambda e, bk=bk, sg=sg: e.activation(out=sg[:], in_=ps[bk][:, 0:TM], func=AF.Silu),
                    w=[PK(bk), ("sgg", j % 2)])
                bk = proj(O_GU + j * 128, ("win", O_GU))
                add("dve", lambda e, bk=bk, sg=sg, j=j: e.scalar_tensor_tensor(
                    out=ug[m % 2][:, j, :], in0=ps[bk][:, 0:TM], scalar=lng[:, j:j + 1], in1=sg[:],
                    op0=ALU.mult, op1=ALU.mult), r=[("sgg", j % 2), "lng"], w=[PK(bk), ("ug", m % 2)])
            for j in range(4):
                st = srgt[j % 2]
                bk = proj(O_RG + j * 128, ("win", O_RG))
                add("act", lambda e, bk=bk, st=st: e.activation(out=st[:], in_=ps[bk][:, 0:TM], func=AF.Silu),
                    w=[PK(bk), ("srgt", j % 2)])
                add("act", lambda e, st=st, j=j: e.activation(out=srg[m % 2][:, j, :], in_=st[:], func=AF.Copy, scale=gng[:, j:j + 1]),
                    r=[("srgt", j % 2), "gng"], w=[("srg", m % 2)])

        def rotary(bk, n, dst, par, kname):
            A, Bm = At[par], Bt[par]
            p4 = ps[bk][:, :].rearrange("p (h two d) -> p h two d", h=4, two=2)
            A4 = A[:].rearrange("p (h two d) -> p h two d", h=4, two=2)
            B4 = Bm[:].rearrange("p (h two d) -> p h two d", h=4, two=2)
            d4 = dst[:].rearrange("p (h two d) -> p h two d", h=4, two=2)
            cb = cosT[:, n, :].unsqueeze(1).unsqueeze(1).to_broadcast([128, 4, 2, 64])
            sbc = sinT[:, n, :].unsqueeze(1).unsqueeze(1).to_broadcast([128, 4, 2, 64])
            add("dve", lambda e: e.tensor_tensor(out=A4, in0=p4, in1=cb, op=ALU.mult), r=["cosT"], w=[PK(bk), ("A", par)])
            add("dve", lambda e: e.tensor_tensor(out=B4, in0=p4[:, :, ::-1, :], in1=sbc, op=ALU.mult), r=["sinT"], w=[PK(bk), ("B", par)])
            add("pool", lambda e: e.tensor_tensor(out=d4[:, :, 0, :], in0=A4[:, :, 0, :], in1=B4[:, :, 0, :], op=ALU.subtract),
                r=[("A", par), ("B", par)], w=[kname])
            add("pool", lambda e: e.tensor_tensor(out=d4[:, :, 1, :], in0=A4[:, :, 1, :], in1=B4[:, :, 1, :], op=ALU.add),
                r=[("A", par), ("B", par)], w=[kname])

        def projA(g):
            b, n, m, c = g // NCH, g % NCH, g // MTC, g % MTC
            hsrc = hT[m % 2]
            hkey = ("hT", m % 2, c)

            def proj(col0):
                bk = big.next()

                def f(e):
                    ins = None
                    for k in range(8):
                        ins = e.matmul(ps[bk][:, :], lhsT=hsrc[:, k, c * 128:(c + 1) * 128], rhs=w_in[:, k, col0:col0 + 512],
                                       start=(k == 0), stop=(k == 7))
                    return ins
                add("pe", f, r=[hkey, ("win", col0)], w=[PK(bk)])
                return bk

            par = g % 2
            bk = proj(O_GV)
            st, mv, lr, lb = lnst[par], lnmv[par], lnr[par], lnb[par]

            def fs(e, bk=bk):
                ins = None
                for q in range(4):
                    ins = e.bn_stats(out=st[:, q, :], in_=ps[bk][:, q * 128:(q + 1) * 128])
                return ins
            add("dve", fs, w=[PK(bk), ("lnst", par)])

            def fg(e):
                ins = None
                for q in range(4):
                    ins = e.bn_aggr(out=mv[:, q, :], in_=st[:, q, :])
                return ins
            add("dve", fg, r=[("lnst", par)], w=[("lnmv", par)])
            add("dve", lambda e: e.tensor_scalar(out=lr[:], in0=mv[:, :, 1], scalar1=EPS, scalar2=None, op0=ALU.add),
                r=[("lnmv", par)], w=[("lnr", par)])
            add("pool", lambda e: e.tensor_tensor(out=lr[:], in0=lr[:], in1=neghalf[:, 0:4], op=ALU.pow), r=[("lnr", par)], w=[("lnr", par)])
            def ln_tail(bk=bk):
                add("dve", lambda e: e.scalar_tensor_tensor(out=lb[:], in0=mv[:, :, 0], scalar=-1.0, in1=lr[:], op0=ALU.mult, op1=ALU.mult),
                    r=[("lnmv", par), ("lnr", par)], w=[("lnb", par)])

                def fn_(e, bk=bk):
                    ins = None
                    for q in range(4):
                        ins = e.activation(out=vln[par][:, q * 128:(q + 1) * 128], in_=ps[bk][:, q * 128:(q + 1) * 128],
                                           func=AF.Identity, scale=lr[:, q:q + 1], bias=lb[:, q:q + 1])
                    return ins
                add("act", fn_, r=[("lnr", par), ("lnb", par)], w=[PK(bk), ("vln", par)])
            bk = proj(O_RQ)
            rotary(bk, n, qrot[par], 0, ("qrot", par))
            ln_tail()
            bk = proj(O_RK)
            rotary(bk, n, krot[par], 1, ("krot", par))
            add("pool", lambda e: e.tensor_tensor(out=ktok[par][:].rearrange("p (h d) -> p h d", h=4),
                                                  in0=krot[par][:].rearrange("p (h d) -> p h d", h=4),
                                                  in1=kdec[:].unsqueeze(2).to_broadcast([128, 4, 128]), op=ALU.mult),
                r=[("krot", par), "kdec"], w=[("ktok", par)])
            bk = proj(O_RV)
            add("act", lambda e, bk=bk: e.activation(out=vbf[g % 3][:], in_=ps[bk][:, :], func=AF.Copy), w=[PK(bk), ("vbf", g % 3)])

        def mixA1(g):
            b, n, m, c = g // NCH, g % NCH, g // MTC, g % MTC
            par = g % 2
            bk = trp.next()

            def f(e, bk=bk):
                ins = None
                for h in range(4):
                    ins = e.transpose(psb[bk][:, h * 128:(h + 1) * 128], qrot[par][:, h * 128:(h + 1) * 128], identb[:])
                for h in range(4):
                    ins = e.transpose(psb[bk][:, 512 + h * 128:512 + (h + 1) * 128], krot[par][:, h * 128:(h + 1) * 128], identb[:])
                return ins
            add("pe", f, r=[("qrot", par), ("krot", par), "identb"], w=[PK(bk)])
            add("act", lambda e, bk=bk: e.activation(out=qT[par][:], in_=psb[bk][:, 0:512], func=AF.Copy), w=[PK(bk), ("qT", par)])
            add("dve", lambda e, bk=bk: e.tensor_tensor(out=kT[par][:], in0=psb[bk][:, 512:1024], in1=gkinv[:], op=ALU.mult),
                r=["gkinv"], w=[PK(bk), ("kT", par)])

        def mixA2(g):
            b, n, m, c = g // NCH, g % NCH, g // MTC, g % MTC
            par = g % 2
            bk = mix.next()

            def f(e, bk=bk):
                ins = None
                for q in range(4):
                    ins = e.matmul(ps[bk][:, q * 128:(q + 1) * 128], lhsT=ginv2[:, q * 128:(q + 1) * 128],
                                   rhs=b2[:, q * 128:(q + 1) * 128], start=(q == 0), stop=False)
                for q in range(4):
                    ins = e.matmul(ps[bk][:, q * 128:(q + 1) * 128], lhsT=vln[par][:, q * 128:(q + 1) * 128],
                                   rhs=wsTb[:, q * 128:(q + 1) * 128], start=False, stop=(q == 3))
                return ins
            add("pe", f, r=[("vln", par), "wsTb", "b2", "ginv2"], w=[PK(bk)])
            add("dve", lambda e, bk=bk: e.tensor_tensor(out=yT[g % 3][:, 0:4, :], in0=ps[bk][:, :].rearrange("p (q t) -> p q t", q=4),
                                                 in1=ug[m % 2][:, :, c * 128:(c + 1) * 128], op=ALU.mult),
                r=[("ug", m % 2)], w=[PK(bk), ("yTg", g % 3)])
            bk = mix.next()

            def f(e, bk=bk):
                ins = None
                for h in range(4):
                    ins = e.matmul(ps[bk][:, h * 128:(h + 1) * 128], lhsT=kT[par][:, h * 128:(h + 1) * 128],
                                   rhs=qT[par][:, h * 128:(h + 1) * 128], start=True, stop=True)
                return ins
            add("pe", f, r=[("qT", par), ("kT", par)], w=[PK(bk)])
            add("dve", lambda e, bk=bk: e.tensor_tensor(out=msT[par][:].rearrange("p (h q) -> p h q", h=4),
                                                 in0=ps[bk][:, :].rearrange("p (h q) -> p h q", h=4),
                                                 in1=cmask[:].unsqueeze(1).to_broadcast([128, 4, 128]), op=ALU.mult),
                r=["cmask"], w=[PK(bk), ("msT", par)])
            if n == 0:
                add("dve", lambda e: e.tensor_copy(out=msT[par][0:1, ::128], in_=s0p[b][:]), r=[("s0p", b)], w=[("msT", par)])
            bk = mix.next()

            def f(e, bk=bk):
                ins = None
                for h in range(4):
                    ins = e.matmul(ps[bk][:, h * 128:(h + 1) * 128], lhsT=ktok[par][:, h * 128:(h + 1) * 128],
                                   rhs=vbf[g % 3][:, h * 128:(h + 1) * 128], start=True, stop=True)
                return ins
            add("pe", f, r=[("ktok", par), ("vbf", g % 3)], w=[PK(bk)])
            if n == 0:
                add("dve", lambda e, bk=bk: e.tensor_copy(out=Sst[b][:], in_=ps[bk][:, :]), w=[PK(bk), ("S", b)])
            else:
                def f(e, bk=bk):
                    ins = None
                    for h in range(4):
                        ins = e.scalar_tensor_tensor(out=Sst[b][:, h * 128:(h + 1) * 128], in0=Sst[b][:, h * 128:(h + 1) * 128],
                                                     scalar=gC[h], in1=ps[bk][:, h * 128:(h + 1) * 128], op0=ALU.mult, op1=ALU.add)
                    return ins
                add("dve", f, r=[("S", b)], w=[PK(bk), ("S", b)])
            add("act", lambda e: e.activation(out=Sbf[(g + 1) % 2][:], in_=Sst[b][:], func=AF.Copy),
                r=[("S", b)], w=[("Sbf", (g + 1) % 2)])

        def mixB1(g):
            b, n, m, c = g // NCH, g % NCH, g // MTC, g % MTC
            par = g % 2
            bk = mix.next()

            def f(e):
                ins = None
                for h in range(4):
                    hs = slice(h * 128, (h + 1) * 128)
                    ins = e.matmul(ps[bk][:, hs], lhsT=msT[par][:, hs], rhs=vbf[g % 3][:, hs], start=True, stop=(n == 0))
                    if n > 0:
                        ins = e.matmul(ps[bk][:, hs], lhsT=qT[par][:, hs], rhs=Sbf[g % 2][:, hs], start=False, stop=True)
                return ins
            rk = [("msT", par), ("vbf", g % 3), ("qT", par)] + ([("Sbf", g % 2)] if n > 0 else [])
            add("pe", f, r=rk, w=[PK(bk)])
            st, mv, gr, gb = gnst[par], gnmv[par], gnr[par], gnb[par]

            def fs(e):
                ins = None
                for q in range(4):
                    ins = e.bn_stats(out=st[:, q, :], in_=ps[bk][:, q * 128:(q + 1) * 128])
                return ins
            add("dve", fs, w=[PK(bk), ("gnst", par)])

            def fg(e):
                ins = None
                for q in range(4):
                    ins = e.bn_aggr(out=mv[:, q, :], in_=st[:, q, :])
                return ins
            add("dve", fg, r=[("gnst", par)], w=[("gnmv", par)])
            add("dve", lambda e: e.tensor_tensor(out=gr[:], in0=mv[:, :, 1], in1=epsp[:], op=ALU.add), r=[("gnmv", par), "epsp"], w=[("gnr", par)])
            add("pool", lambda e: e.tensor_tensor(out=gr[:], in0=gr[:], in1=neghalf[:, 0:4], op=ALU.pow), r=[("gnr", par)], w=[("gnr", par)])
            def gn_tail():
                add("dve", lambda e: e.scalar_tensor_tensor(out=gb[:], in0=mv[:, :, 0], scalar=-1.0, in1=gr[:], op0=ALU.mult, op1=ALU.mult),
                    r=[("gnmv", par), ("gnr", par)], w=[("gnb", par)])

                def fn_(e):
                    ins = None
                    for q in range(4):
                        ins = e.activation(out=onb[par][:, q * 128:(q + 1) * 128], in_=ps[bk][:, q * 128:(q + 1) * 128],
                                           func=AF.Identity, scale=gr[:, q:q + 1], bias=gb[:, q:q + 1])
                    return ins
                add("act", fn_, r=[("gnr", par), ("gnb", par)], w=[PK(bk), ("onb", par)])
            return gn_tail

        def mixB2(g):
            b, n, m, c = g // NCH, g % NCH, g // MTC, g % MTC
            par = g % 2
            bk2 = trp.next()

            def f(e):
                ins = None
                for h in range(4):
                    ins = e.transpose(psb[bk2][:, h * 128:(h + 1) * 128], onb[par][:, h * 128:(h + 1) * 128], identb[:])
                return ins
            add("pe", f, r=[("onb", par), "identb"], w=[PK(bk2)])
            add("dve", lambda e: e.tensor_tensor(out=yT[g % 3][:, 4:8, :], in0=psb[bk2][:, 0:512].rearrange("p (h t) -> p h t", h=4),
                                                 in1=srg[m % 2][:, :, c * 128:(c + 1) * 128], op=ALU.mult),
                r=[("srg", m % 2)], w=[PK(bk2), ("yTr", g % 3)])

        def outp(g):
            b, sl = g // NCH, g % NX
            par = g % 2
            bks = []
            for half in range(2):
                bk = big.next()
                bks.append(bk)

                def f(e, half=half, bk=bk):
                    ins = None
                    for k in range(8):
                        ins = e.matmul(ps[bk][:, :], lhsT=yT[g % 3][:, k, :], rhs=w_out[:, k, half * 512:(half + 1) * 512],
                                       start=(k == 0), stop=(k == 7))
                    return ins
                add("pe", f, r=[("yTg", g % 3), ("yTr", g % 3), ("wout", half)], w=[PK(bk)])
                add("act", lambda e, half=half, bk=bk: e.activation(out=junkA[:, 0:512], in_=ps[bk][:, :], func=AF.Square,
                                                                   accum_out=zss[par][:, half:half + 1]),
                    w=[PK(bk), "junkA", ("zss", par, half)])
            add("dve", lambda e: e.tensor_tensor(out=zr[par][:, 0:1], in0=zss[par][:, 0:1], in1=zss[par][:, 1:2], op=ALU.add),
                r=[("zss", par, 0), ("zss", par, 1)], w=[("zr0", par)])
            add("dve", lambda e: e.tensor_scalar(out=zr[par][:, 0:1], in0=zr[par][:, 0:1], scalar1=1.0 / D, scalar2=EPS,
                                                 op0=ALU.mult, op1=ALU.add), r=[("zr0", par)], w=[("zr0", par)])
            add("pool", lambda e: e.tensor_tensor(out=zr[par][:, 1:2], in0=zr[par][:, 0:1], in1=neghalf[:, 0:1], op=ALU.pow),
                r=[("zr0", par), "neghalf"], w=[("zr1", par)])
            def out_tail():
                for half in range(2):
                    bk = bks[half]
                    add("dve", lambda e, half=half, bk=bk: e.scalar_tensor_tensor(
                        out=tmpz[par][:, half * 512:(half + 1) * 512], in0=ps[bk][:, :], scalar=zr[par][:, 1:2],
                        in1=gp[b][:, half * 512:(half + 1) * 512], op0=ALU.mult, op1=ALU.mult),
                        r=[("zr1", par), ("gp", b)], w=[PK(bk), ("tmp", par)])
                xl = 3 + g % 3
                add("dve", lambda e: e.tensor_tensor(out=xsl[xl], in0=xsl[xl], in1=tmpz[par], op=ALU.add),
                    r=[("tmp", par)], w=[("x", xl)])
                dma("sp", out_d[g * 128:(g + 1) * 128, :], xsl[xl], f"o{xl}", r=[("x", xl)], w=[("out", g)])
            return out_tail

        assert MTC == 2
        front0(3)
        front2(0)
        early_front(2, head=False)
        front2(1)
        precise_mm()
        for s in range(G + 4):
            if s == 1:
                precise_el()
            if s < 4:
                half, kk = s // 2, s % 2
                c0 = half * 512
                dma("pool", w_out[:, kk * 4:(kk + 1) * 4, c0:c0 + 512],
                    wout_d[kk * 512:(kk + 1) * 512, c0:c0 + 512].rearrange("(k p) c -> p k c", p=128), f"w_o{half}", w=[("wout", half)])
            if 0 <= s - 3 < G:
                reload(s - 3)
            if s + MTC + 2 < G:
                front0(s + MTC + 2)
            if s + MTC < G:
                front2(s + MTC)
            if 0 <= s - 1 < G:
                mixA1(s - 1)
            if s + MTC + 1 < G:
                front1(s + MTC + 1)
            if s == NCH:
                tables(1)
            if s < G:
                projA(s)
            if 0 <= s - 3 < G:
                mixB2(s - 3)
            if s < G and s % MTC == 0:
                projB(s // MTC)
            gn_tail = mixB1(s - 2) if 0 <= s - 2 < G else None
            out_tail = outp(s - 4) if 0 <= s - 4 < G else None
            if gn_tail is not None:
                gn_tail()
            if 0 <= s - 1 < G:
                mixA2(s - 1)
            if out_tail is not None:
                out_tail()
        add("sp", lambda e: e.nop(), r=[("out", g) for g in range(G)])

        print("sbuf bytes remaining:", nc.sbuf_bytes_remaining, "ops:", len(S.ops))
        S.analyze()
        eng_sems = {n: es.enter_context(nc.semaphore("s_" + n)) for n in ("pe", "act", "dve", "pool", "sp")}
        dma_sems = {n: es.enter_context(nc.semaphore("d_" + n)) for n in S.dma_total}
        block = es.enter_context(nc.Block())

        @block.sync
        def _(e):
            S.emit_engine("sp", e, eng_sems, dma_sems)

        @block.tensor
        def _(e):
            S.emit_engine("pe", e, eng_sems, dma_sems)

        @block.scalar
        def _(e):
            S.emit_engine("act", e, eng_sems, dma_sems)

        @block.vector
        def _(e):
            S.emit_engine("dve", e, eng_sems, dma_sems)

        @block.gpsimd
        def _(e):
            S.emit_engine("pool", e, eng_sems, dma_sems)
    return nc


def _consts():
    h = np.arange(4, dtype=np.float64)
    gam = 1.0 - 2.0 ** (-5.0 - h)
    t = np.arange(128, dtype=np.float64)
    sc = 128.0 ** -0.5
    gkinv = (sc * gam[:, None] ** (-(t[None, :] + 1.0))).reshape(1, 512)
    gkinv = np.broadcast_to(gkinv, (128, 512)).astype(np.float32)
    kdec = (sc * gam[None, :] ** (127.0 - t[:, None])).astype(np.float32)
    epsp = (EPS * gam[None, :] ** (-2.0 * (t[:, None] + 1.0))).astype(np.float32)
    half = 64
    inv_freq = (1.0 / (10000.0 ** (np.arange(half, dtype=np.float32) / half))).astype(np.float32)
    invf = np.broadcast_to((inv_freq.astype(np.float64) / (2.0 * np.pi)).astype(np.float32)[None, :], (128, 64))
    cmask = (t[None, :] >= t[:, None]).astype(np.float32)
    sel = np.zeros((2, 256), np.float32)
    sel[0, 0:128] = 1.0
    sel[1, 128:256] = 1.0
    return dict(gkinv=np.ascontiguousarray(gkinv), kdec=kdec, epsp=epsp, invf=np.ascontiguousarray(invf),
                cmask=cmask, sel=sel, ident=np.eye(128, dtype=np.float32))


def _pack(g_pre_t, lng_t, gng_t, kdec, epsp, invf, cT, x0T, pos):
    cpk = np.zeros((128, 160), np.float32)
    cpk[:, 0:8], cpk[:, 8:12], cpk[:, 12:16], cpk[:, 16:20], cpk[:, 20:24] = g_pre_t, lng_t, gng_t, kdec, epsp
    cpk[:, 24:88], cpk[:, 88:104], cpk[:, 104:120] = invf, cT, x0T
    cpk[:, 120:152] = np.ascontiguousarray(pos.astype(np.int32)).view(np.float32)
    return cpk


_NC_CACHE = {}


def kernel(x, c, positions, w_ada, b_ada, g_pre, w_in, gmlp_ln_g, gmlp_ws, gmlp_bs, ret_gn_g, w_out, g_post):
    f = lambda a: np.ascontiguousarray(np.asarray(a))
    x, c, positions = f(x), f(c), f(positions)
    w_ada, b_ada, g_pre, w_in = f(w_ada)[0], f(b_ada)[0], f(g_pre)[0], f(w_in)[0]
    lng, ws, bs, gng, w_out, g_post = f(gmlp_ln_g)[0], f(gmlp_ws)[0], f(gmlp_bs)[0], f(ret_gn_g)[0], f(w_out)[0], f(g_post)[0]
    if "nc" not in _NC_CACHE:
        _NC_CACHE["nc"] = build_program()
    nc = _NC_CACHE["nc"]
    cst = _consts()
    g_pre_t, lng_t, gng_t = f(g_pre.reshape(8, 128).T), f(lng.reshape(4, 128).T), f(gng.reshape(4, 128).T)
    kdec_c, epsp_c, invf_c = cst.pop("kdec"), cst.pop("epsp"), cst.pop("invf")
    shared = dict(
        w_ada=w_ada, b_ada=f(b_ada.reshape(1, -1)), w_in=w_in,
        lng_r2=f(np.broadcast_to(lng.reshape(1, 512), (2, 512))), wsT=f(ws.transpose(2, 0, 1).reshape(128, 512)), bs=f(bs.reshape(1, 512)),
        w_out=w_out, g_post_r=f(np.broadcast_to(g_post[None, :], (128, D))), **cst)
    in_maps = []
    for i in range(NCORES):
        bsl = slice(i * NB, (i + 1) * NB)
        xm = f(x[bsl].reshape(G * 128, D))
        cT = f(c[bsl].reshape(NB, 8, 128).transpose(2, 1, 0).reshape(128, 16))
        pos = f(positions[bsl].reshape(NB, NCH, 128).transpose(2, 0, 1).reshape(128, NB * NCH).astype(np.int32))
        x0T = f(x[bsl, 0, :].reshape(NB, 8, 128).transpose(2, 1, 0).reshape(128, 8 * NB))
        cpk = _pack(g_pre_t, lng_t, gng_t, kdec_c, epsp_c, invf_c, cT, x0T, pos)
        in_maps.append(dict(x=xm, cpk=cpk, **shared))
    res = run_bass_kernel_spmd(nc, in_maps, core_ids=list(range(NCORES)))
    out = np.concatenate([np.asarray(r["out"]).reshape(NB, SEQ, D) for r in res.results], axis=0)
    return out.astype(np.float32, copy=False)
```
